# Optimizing a Trainium2 kernel written in Bass

```python
import jax, jax.numpy as jnp
from jax import lax
import numpy as np

D_MODEL = 1024
BATCH = 4
SEQ = 4096
DEPTH = 1
DEC_BATCH = 128
DEC_SEQ = 1
PAST_LEN = 8192
PAGE_SIZE = 128

MIX_WIDTH = D_MODEL
MLSTM_HEADS = 4
MLSTM_HEAD_DIM = MIX_WIDTH // 2 // MLSTM_HEADS
MLSTM_WIDTH = MLSTM_HEADS * MLSTM_HEAD_DIM
MLSTM_CHUNK = 128
SWA_HEADS = 8
SWA_HEAD_DIM = (MIX_WIDTH - MLSTM_WIDTH) // SWA_HEADS
SWA_WIDTH = SWA_HEADS * SWA_HEAD_DIM
SWA_KV_HEADS = 2
SWA_GROUP = SWA_HEADS // SWA_KV_HEADS
SWA_KV_WIDTH = SWA_KV_HEADS * SWA_HEAD_DIM
WINDOW = 128
SWA_BLOCK = WINDOW
D_FF = 2816
FFN_RES_WEIGHT = 0.5
RMS_EPS = 1e-6
GATE_PAD = -1e30
PROJ_WIDTH = 4 * MLSTM_WIDTH + 2 * MLSTM_HEADS + SWA_WIDTH + 2 * SWA_KV_WIDTH

kernel_name = 'hymba_mlstm_swa_macaron_step'


def rms_norm(x, gain):
    xf = x.astype(jnp.float32)
    y = xf * lax.rsqrt(jnp.mean(xf * xf, axis=-1, keepdims=True) + RMS_EPS)
    return (y * gain.astype(jnp.float32)).astype(x.dtype)


def swiglu(x, w_gate, w_up, w_down):
    return (jax.nn.silu(x @ w_gate) * (x @ w_up)) @ w_down


def proj_split_points():
    sizes = [MLSTM_WIDTH] * 4 + [MLSTM_HEADS] * 2 + [SWA_WIDTH, SWA_KV_WIDTH, SWA_KV_WIDTH]
    return [int(s) for s in np.cumsum(sizes)[:-1]]


def mlstm_chunkwise(q, k, v, i_pre, log_f, c0, n0, m0):
    b_, t_, h_, dh = q.shape
    L = min(MLSTM_CHUNK, t_)
    n_chunks = -(-t_ // L)
    pad = n_chunks * L - t_
    if pad:
        pw = ((0, 0), (0, pad), (0, 0), (0, 0))
        q, k, v = jnp.pad(q, pw), jnp.pad(k, pw), jnp.pad(v, pw)
        i_pre = jnp.pad(i_pre, pw[:3], constant_values=GATE_PAD)
        log_f = jnp.pad(log_f, pw[:3])

    def to_chunks(a):
        return jnp.moveaxis(a.reshape((b_, n_chunks, L) + a.shape[2:]), 1, 0)

    causal = jnp.tril(jnp.ones((L, L), dtype=bool))[None, :, :, None]

    def step(carry, xs):
        c, n, m = carry
        qj, kj, vj, ij, fj = xs
        bcum = jnp.cumsum(fj, axis=1)
        log_inter = bcum + m[:, None, :]
        log_intra = bcum[:, :, None, :] - bcum[:, None, :, :] + ij[:, None, :, :]
        log_intra = jnp.where(causal, log_intra, -jnp.inf)
        m_t = jnp.maximum(log_inter, jnp.max(log_intra, axis=2))
        w_inter = jnp.exp(log_inter - m_t)
        scores = jnp.einsum('bthd,bshd->btsh', qj, kj) * jnp.exp(log_intra - m_t[:, :, None, :])
        num = w_inter[..., None] * jnp.einsum('bhvk,bthk->bthv', c, qj) + jnp.einsum('btsh,bshv->bthv', scores, vj)
        den = w_inter * jnp.einsum('bhk,bthk->bth', n, qj) + jnp.sum(scores, axis=2)
        h = num / jnp.maximum(jnp.abs(den), jnp.exp(-m_t))[..., None]
        b_last = bcum[:, -1, :]
        log_end = b_last[:, None, :] - bcum + ij
        m_new = jnp.maximum(b_last + m, jnp.max(log_end, axis=1))
        decay = jnp.exp(b_last + m - m_new)
        w_end = jnp.exp(log_end - m_new[:, None, :])
        c_new = decay[..., None, None] * c + jnp.einsum('bsh,bshv,bshk->bhvk', w_end, vj, kj)
        n_new = decay[..., None] * n + jnp.einsum('bsh,bshk->bhk', w_end, kj)
        return (c_new, n_new, m_new), h

    xs = (to_chunks(q), to_chunks(k), to_chunks(v), to_chunks(i_pre), to_chunks(log_f))
    (c1, n1, m1), hs = lax.scan(step, (c0, n0, m0), xs)
    hs = jnp.moveaxis(hs, 0, 1).reshape(b_, n_chunks * L, h_, dh)[:, :t_]
    return hs, (c1, n1, m1)


def mlstm_mixer(q_m, k_m, v_m, o_m, i_m, f_m, b_i, b_f, out_gain, c0, n0, m0):
    b_, t_, _ = q_m.shape
    shp = (b_, t_, MLSTM_HEADS, MLSTM_HEAD_DIM)
    q = q_m.reshape(shp).astype(jnp.float32)
    k = k_m.reshape(shp).astype(jnp.float32) * (MLSTM_HEAD_DIM ** -0.5)
    v = v_m.reshape(shp).astype(jnp.float32)
    i_pre = i_m.astype(jnp.float32) + b_i.astype(jnp.float32)
    log_f = jax.nn.log_sigmoid(f_m.astype(jnp.float32) + b_f.astype(jnp.float32))
    h, state = mlstm_chunkwise(q, k, v, i_pre, log_f, c0.astype(jnp.float32),
                               n0.astype(jnp.float32), m0.astype(jnp.float32))
    h = h * lax.rsqrt(jnp.mean(h * h, axis=-1, keepdims=True) + RMS_EPS)
    h = h * out_gain.astype(jnp.float32).reshape(MLSTM_HEADS, MLSTM_HEAD_DIM)
    h = h.reshape(b_, t_, MLSTM_WIDTH) * jax.nn.sigmoid(o_m.astype(jnp.float32))
    return h.astype(q_m.dtype), state


def sink_softmax(s, sink):
    mx = jnp.maximum(jnp.max(s, axis=-1, keepdims=True), sink)
    p = jnp.exp(s - mx)
    return p / (jnp.sum(p, axis=-1, keepdims=True) + jnp.exp(sink - mx))


def swa_banded(q, k, v, sinks):
    b_, t_, _, dh = q.shape
    nb = t_ // SWA_BLOCK
    qb = q.reshape(b_, nb, SWA_BLOCK, SWA_KV_HEADS, SWA_GROUP, dh)
    kb = k.reshape(b_, nb, SWA_BLOCK, SWA_KV_HEADS, dh)
    vb = v.reshape(b_, nb, SWA_BLOCK, SWA_KV_HEADS, dh)
    shift = ((0, 0), (1, 0), (0, 0), (0, 0), (0, 0))
    k_band = jnp.concatenate([jnp.pad(kb, shift)[:, :-1], kb], axis=2)
    v_band = jnp.concatenate([jnp.pad(vb, shift)[:, :-1], vb], axis=2)
    s = jnp.einsum('bnqkgd,bnskd->bnkgqs', qb, k_band).astype(jnp.float32) * (dh ** -0.5)
    q_pos = jnp.arange(SWA_BLOCK)[:, None] + SWA_BLOCK
    k_pos = jnp.arange(2 * SWA_BLOCK)[None, :]
    blk_start = (jnp.arange(nb) - 1) * SWA_BLOCK
    mask = ((k_pos <= q_pos) & (k_pos >= q_pos - WINDOW))[None] & ((blk_start[:, None, None] + k_pos[None]) >= 0)
    s = jnp.where(mask[None, :, None, None], s, -jnp.inf)
    sink = sinks.astype(jnp.float32).reshape(SWA_KV_HEADS, SWA_GROUP)[None, None, :, :, None, None]
    p = sink_softmax(s, sink)
    o = jnp.einsum('bnkgqs,bnskd->bnqkgd', p.astype(v.dtype), v_band)
    return o.reshape(b_, t_, SWA_WIDTH)


def swa_with_buffer(q, k, v, buf_k, buf_v, sinks):
    b_, t_, _, dh = q.shape
    w = buf_k.shape[1]
    k_all = jnp.concatenate([buf_k.astype(k.dtype), k], axis=1)
    v_all = jnp.concatenate([buf_v.astype(v.dtype), v], axis=1)
    qg = q.reshape(b_, t_, SWA_KV_HEADS, SWA_GROUP, dh)
    s = jnp.einsum('bqkgd,bskd->bkgqs', qg, k_all).astype(jnp.float32) * (dh ** -0.5)
    q_rel = jnp.arange(t_)[:, None]
    k_rel = jnp.arange(w + t_)[None, :] - w
    mask = (k_rel <= q_rel) & (k_rel >= q_rel - WINDOW)
    s = jnp.where(mask, s, -jnp.inf)
    sink = sinks.astype(jnp.float32).reshape(SWA_KV_HEADS, SWA_GROUP)[None, :, :, None, None]
    p = sink_softmax(s, sink)
    o = jnp.einsum('bkgqs,bskd->bqkgd', p.astype(v_all.dtype), v_all)
    return o.reshape(b_, t_, SWA_WIDTH), k_all[:, -w:], v_all[:, -w:]


def trunk_layer(x, p, c0, n0, m0, buf_k, buf_v):
    x = x + FFN_RES_WEIGHT * swiglu(rms_norm(x, p['ffn1_norm']), p['ffn1_w_gate'], p['ffn1_w_up'], p['ffn1_w_down'])
    h = rms_norm(x, p['mix_norm'])
    proj = h @ p['w_in']
    q_m, k_m, v_m, o_m, i_m, f_m, q_a, k_a, v_a = jnp.split(proj, proj_split_points(), axis=-1)
    b_, t_, _ = x.shape
    y_m, (c1, n1, m1) = mlstm_mixer(q_m, k_m, v_m, o_m, i_m, f_m, p['mlstm_b_i'], p['mlstm_b_f'],
                                    p['mlstm_out_norm'], c0, n0, m0)
    q_a = rms_norm(q_a.reshape(b_, t_, SWA_HEADS, SWA_HEAD_DIM), p['swa_q_norm'])
    k_a = rms_norm(k_a.reshape(b_, t_, SWA_KV_HEADS, SWA_HEAD_DIM), p['swa_k_norm'])
    v_a = v_a.reshape(b_, t_, SWA_KV_HEADS, SWA_HEAD_DIM)
    if buf_k is None:
        y_a = swa_banded(q_a, k_a, v_a, p['swa_sinks'])
        w = min(WINDOW, t_)
        new_k, new_v = k_a[:, t_ - w:], v_a[:, t_ - w:]
    else:
        y_a, new_k, new_v = swa_with_buffer(q_a, k_a, v_a, buf_k, buf_v, p['swa_sinks'])
    x = x + jnp.concatenate([y_m, y_a], axis=-1) @ p['w_out']
    x = x + FFN_RES_WEIGHT * swiglu(rms_norm(x, p['ffn2_norm']), p['ffn2_w_gate'], p['ffn2_w_up'], p['ffn2_w_down'])
    return x, (new_k, new_v, c1, n1, m1)


def setup_inputs(seed: int = 0) -> dict:
    key = jax.random.key(seed)
    ks = jax.random.split(key, 32)
    f32 = jnp.float32

    def nrm(k, shape, scale):
        return jax.random.normal(k, shape, f32) * scale

    def gain(k, shape):
        return 1.0 + 0.05 * jax.random.normal(k, shape, f32)

    L, d, hd = DEPTH, D_MODEL, MLSTM_HEAD_DIM
    w_buf = min(WINDOW, PAST_LEN)
    return {
        'x_prompt': nrm(ks[0], (BATCH, SEQ, d), 1.0),
        'x_sample': nrm(ks[1], (DEC_BATCH, DEC_SEQ, d), 1.0),
        'cache_swa_k': nrm(ks[2], (L, DEC_BATCH, w_buf, SWA_KV_HEADS, SWA_HEAD_DIM), 1.0),
        'cache_swa_v': nrm(ks[3], (L, DEC_BATCH, w_buf, SWA_KV_HEADS, SWA_HEAD_DIM), 1.0),
        'state_mlstm_C': nrm(ks[4], (L, DEC_BATCH, MLSTM_HEADS, hd, hd), 0.1),
        'state_mlstm_n': nrm(ks[5], (L, DEC_BATCH, MLSTM_HEADS, hd), 0.1),
        'state_mlstm_m': nrm(ks[6], (L, DEC_BATCH, MLSTM_HEADS), 1.0),
        'ffn1_norm': gain(ks[7], (L, d)),
        'ffn1_w_gate': nrm(ks[8], (L, d, D_FF), d ** -0.5),
        'ffn1_w_up': nrm(ks[9], (L, d, D_FF), d ** -0.5),
        'ffn1_w_down': nrm(ks[10], (L, D_FF, d), D_FF ** -0.5),
        'mix_norm': gain(ks[11], (L, d)),
        'w_in': nrm(ks[12], (L, d, PROJ_WIDTH), d ** -0.5),
        'mlstm_b_i': nrm(ks[13], (L, MLSTM_HEADS), 0.1),
        'mlstm_b_f': jnp.broadcast_to(jnp.linspace(3.0, 6.0, MLSTM_HEADS, dtype=f32), (L, MLSTM_HEADS)) + nrm(ks[14], (L, MLSTM_HEADS), 0.01),
        'mlstm_out_norm': gain(ks[15], (L, MLSTM_WIDTH)),
        'swa_q_norm': gain(ks[16], (L, SWA_HEAD_DIM)),
        'swa_k_norm': gain(ks[17], (L, SWA_HEAD_DIM)),
        'swa_sinks': nrm(ks[18], (L, SWA_HEADS), 0.5),
        'w_out': nrm(ks[19], (L, MIX_WIDTH, d), MIX_WIDTH ** -0.5),
        'ffn2_norm': gain(ks[20], (L, d)),
        'ffn2_w_gate': nrm(ks[21], (L, d, D_FF), d ** -0.5),
        'ffn2_w_up': nrm(ks[22], (L, d, D_FF), d ** -0.5),
        'ffn2_w_down': nrm(ks[23], (L, D_FF, d), D_FF ** -0.5),
    }


def reference(x_prompt, x_sample, cache_swa_k, cache_swa_v, state_mlstm_C, state_mlstm_n, state_mlstm_m,
              ffn1_norm, ffn1_w_gate, ffn1_w_up, ffn1_w_down, mix_norm, w_in, mlstm_b_i, mlstm_b_f,
              mlstm_out_norm, swa_q_norm, swa_k_norm, swa_sinks, w_out, ffn2_norm, ffn2_w_gate,
              ffn2_w_up, ffn2_w_down):
    yp, ys = x_prompt, x_sample
    bp = x_prompt.shape[0]
    pk, pv, pc, pn, pm = [], [], [], [], []
    sk, sv, sc, sn, sm = [], [], [], [], []
    for l in range(DEPTH):
        p = {
            'ffn1_norm': ffn1_norm[l], 'ffn1_w_gate': ffn1_w_gate[l], 'ffn1_w_up': ffn1_w_up[l],
            'ffn1_w_down': ffn1_w_down[l], 'mix_norm': mix_norm[l], 'w_in': w_in[l],
            'mlstm_b_i': mlstm_b_i[l], 'mlstm_b_f': mlstm_b_f[l], 'mlstm_out_norm': mlstm_out_norm[l],
            'swa_q_norm': swa_q_norm[l], 'swa_k_norm': swa_k_norm[l], 'swa_sinks': swa_sinks[l],
            'w_out': w_out[l], 'ffn2_norm': ffn2_norm[l], 'ffn2_w_gate': ffn2_w_gate[l],
            'ffn2_w_up': ffn2_w_up[l], 'ffn2_w_down': ffn2_w_down[l],
        }
        c0 = jnp.zeros((bp, MLSTM_HEADS, MLSTM_HEAD_DIM, MLSTM_HEAD_DIM), jnp.float32)
        n0 = jnp.zeros((bp, MLSTM_HEADS, MLSTM_HEAD_DIM), jnp.float32)
        m0 = jnp.zeros((bp, MLSTM_HEADS), jnp.float32)
        yp, (k1, v1, c1, n1, m1) = trunk_layer(yp, p, c0, n0, m0, None, None)
        ys, (k2, v2, c2, n2, m2) = trunk_layer(ys, p, state_mlstm_C[l], state_mlstm_n[l], state_mlstm_m[l],
                                               cache_swa_k[l], cache_swa_v[l])
        pk.append(k1); pv.append(v1); pc.append(c1); pn.append(n1); pm.append(m1)
        sk.append(k2); sv.append(v2); sc.append(c2); sn.append(n2); sm.append(m2)
    return (yp, ys, jnp.stack(pk), jnp.stack(pv), jnp.stack(pc), jnp.stack(pn), jnp.stack(pm),
            jnp.stack(sk), jnp.stack(sv), jnp.stack(sc), jnp.stack(sn), jnp.stack(sm))
```

```python
from contextlib import ExitStack
import numpy as np
import concourse.bass as bass
import concourse.mybir as mybir
from concourse.bass_utils import run_bass_kernel_spmd

F32 = mybir.dt.float32
BF16 = mybir.dt.bfloat16
AF = mybir.ActivationFunctionType
ALU = mybir.AluOpType
AX = mybir.AxisListType

D = 1024
DFF = 2816
NF = 22
PW = 2824
TOK = 2048
NCH = 16
NS = 16
REST_STEPS = 3
PHASE_A_256 = True
EPS = 1e-6
BIG = 1e30
C_Q, C_K, C_V, C_O, C_I, C_F, C_QA, C_KA, C_VA = 0, 512, 1024, 1536, 2048, 2052, 2056, 2568, 2696


class Prog:
    def __init__(self):
        self.ops = []

    def add(self, eng, fn, r=(), w=(), dma=None):
        r = list(r); w = list(w)
        ps = [k for k in r if len(k) == 2 and k[0] == 'b' and k[1].isdigit()]
        r = [k for k in r if k not in ps]
        w = w + [k for k in ps if k not in w]
        self.ops.append(dict(eng=eng, fn=fn, r=r, w=w, dma=dma))

    def analyze(self):
        last_w = {}
        readers = {}
        for i, op in enumerate(self.ops):
            deps = set()
            for k in op['r']:
                if k in last_w:
                    deps.add(last_w[k])
            for k in op['w']:
                if k in last_w:
                    deps.add(last_w[k])
                rd = readers.get(k)
                if rd:
                    deps.update(rd['c'].values())
                    deps.update(rd['d'])
            deps.discard(i)
            if op['dma'] and op['dma'].startswith('G_'):
                deps = {j for j in deps if self.ops[j]['dma'] != op['dma']}
            op['deps'] = deps
            for k in op['r']:
                rd = readers.setdefault(k, {'c': {}, 'd': []})
                if op['dma']:
                    rd['d'].append(i)
                else:
                    rd['c'][op['eng']] = i
            for k in op['w']:
                last_w[k] = i
                readers[k] = {'c': {}, 'd': []}
        for op in self.ops:
            op['need'] = False
        for i, op in enumerate(self.ops):
            nd = set()
            for j in op['deps']:
                pj = self.ops[j]
                if (not pj['dma']) and (not op['dma']) and pj['eng'] == 'pe' and op['eng'] == 'pe':
                    continue
                nd.add(j)
                if not pj['dma']:
                    pj['need'] = True
            op['deps'] = nd
        seq = {}
        dcnt = {}
        tot = {}
        for op in self.ops:
            if op['dma']:
                tot[op['dma']] = tot.get(op['dma'], 0) + 16
        for op in self.ops:
            if op['dma']:
                dcnt[op['dma']] = dcnt.get(op['dma'], 0) + 16
                op['val'] = tot[op['dma']] if op['dma'].startswith('G_') else dcnt[op['dma']]
            elif op['need']:
                seq[op['eng']] = seq.get(op['eng'], 0) + 1
                op['val'] = seq[op['eng']]
        waited = {}
        for op in self.ops:
            e = op['eng']
            wl = {}
            for j in op['deps']:
                pj = self.ops[j]
                key = ('d', pj['dma']) if pj['dma'] else ('c', pj['eng'])
                wl[key] = max(wl.get(key, 0), pj['val'])
            out = []
            we = waited.setdefault(e, {})
            for key, v in wl.items():
                if we.get(key, 0) >= v:
                    continue
                we[key] = v
                out.append((key, v))
            op['waits'] = out
        self.dma_keys = sorted(dcnt.keys())

    def emit(self, nc, es):
        self.analyze()
        sems = {}
        for e in ['pe', 'act', 'dve', 'pool']:
            sems[('c', e)] = es.enter_context(nc.semaphore("s_" + e))
        for k in self.dma_keys:
            sems[('d', k)] = es.enter_context(nc.semaphore("d_" + k))
        block = es.enter_context(nc.Block())
        ops = self.ops

        def run(engname):
            def body(eng):
                for op in ops:
                    if op['eng'] != engname:
                        continue
                    for key, v in op['waits']:
                        eng.wait_ge(sems[key], v)
                    ins = op['fn'](eng)
                    if op['dma']:
                        ins.then_inc(sems[('d', op['dma'])], 16)
                    elif op['need']:
                        ins.then_inc(sems[('c', engname)], 1)
            return body

        block.sync(run('sp'))
        block.tensor(run('pe'))
        block.scalar(run('act'))
        block.vector(run('dve'))
        block.gpsimd(run('pool'))


class Ctx:
    pass


def build_nc(debug=None):
    nc = bass.Bass("TRN2", target_bir_lowering=False)
    es = ExitStack()
    P = Prog()
    K = Ctx()

    def din(name, shape, dt=F32):
        return nc.dram_tensor(name, list(shape), dt, kind="ExternalInput").ap()

    def dout(name, shape, dt=F32):
        return nc.dram_tensor(name, list(shape), dt, kind="ExternalOutput").ap()

    def sb(name, shape, dt=F32):
        return es.enter_context(nc.sbuf_tensor("sb_" + name, list(shape), dt))

    xa_d = din("xa", [TOK, D]); xb_d = din("xb", [TOK, D]); xs_d = din("xs", [NS, D])
    cK_d = din("cK", [NS, 128, 128]); cV_d = din("cV", [NS, 128, 128])
    sC0_d = din("sC0", [NS, 4, 128, 128]); sn0_d = din("sn0", [NS, 512]); sm0_d = din("sm0", [NS, 4])
    n1_d = din("n1", [1, D]); wg1_d = din("wg1", [D, DFF]); wu1_d = din("wu1", [D, DFF]); wd1_d = din("wd1", [DFF, D])
    nm_d = din("nm", [1, D]); win_d = din("win", [D, PW]); bi_d = din("bi", [1, 4]); bf_d = din("bf", [1, 4])
    gon_d = din("gon", [1, 512]); gq_d = din("gq", [1, 64]); gk_d = din("gk", [1, 64]); snk_d = din("snk", [1, 8])
    wout_d = din("wout", [D, D])
    n2_d = din("n2", [1, D]); wg2_d = din("wg2", [D, DFF]); wu2_d = din("wu2", [D, DFF]); wd2_d = din("wd2", [DFF, D])
    ident_d = din("ident", [128, 128]); maskbig_d = din("maskbig", [128, 128])
    swam_d = din("swam", [128, 256]); swam0_d = din("swam0", [128, 256]); amask_d = din("amask", [128, 2])
    sel_d = din("sel", [4, 512]); esel_d = din("esel", [128, 16])

    yb_d = dout("yb", [TOK, D]); ys_d = dout("ys", [NS, D])
    pk_d = dout("pk", [128, 128]); pv_d = dout("pv", [128, 128])
    pC_d = dout("pC", [4, 128, 128]); pn_d = dout("pn", [4, 128]); pm_d = dout("pm", [1, 4])
    sk_d = dout("sk", [NS, 128, 128]); sv_d = dout("sv", [NS, 128, 128])
    sC_d = dout("sC", [NS, 4, 128, 128]); sn_d = dout("sn", [NS, 512]); sm_d = dout("sm", [NS, 4])
    scr_d = nc.dram_tensor("scr", [8, NS, 512], F32, kind="Internal").ap()
    dbg_d = dout("dbg", [TOK, D]) if debug else None
    dbg2_d = dout("dbg2", [128, 1024]) if debug else None

    x = sb("x", [128, NCH, D])
    xs = sb("xs_t", [NS, D])
    hT = sb("hT", [128, 8 * 2064], BF16)
    hTv = hT[:, :].rearrange("p (c n) -> p c n", n=2064)
    big = sb("big", [128, 24704], BF16)
    actv = big[:, 0:8 * 2064].rearrange("p (c n) -> p c n", n=2064)
    wdv = big[:, 8 * 2064:8 * 2064 + 8 * 1024].rearrange("p (c n) -> p c n", n=1024)
    winv = big[:, 0:8 * PW].rearrange("p (c n) -> p c n", n=PW)
    wgu = sb("wgu", [128, 2 * 2 * 8 * 256], BF16)
    wguv = wgu[:, :].rearrange("p (s g c n) -> p s g c n", s=2, g=2, c=8)
    woutv = wgu[:, :].rearrange("p (c n) -> p c n", n=1024)
    gain = sb("gain", [128, D])
    identf = sb("identf", [128, 128]); identb = sb("identb", [128, 128], BF16)
    maskbig = sb("maskbig", [128, 128]); swam = sb("swam_t", [128, 256]); swam0 = sb("swam0_t", [128, 256])
    amask = sb("amask_t", [128, 2]); selT = sb("selT", [4, 512]); esel = sb("esel_t", [128, 16])
    epsb = sb("epsb", [128, 1]); onesb = sb("onesb", [128, 128], BF16); onesf = sb("onesf", [128, 128])
    hn = sb("hn", [128, 2, D], BF16)
    stat = sb("stat", [128, 2, 4])
    statF = sb("statF", [128, 36])
    sg = sb("sg", [128, 2, 512])

    psb = [es.enter_context(nc.psum_tensor("ps%d" % i, [128, 512], F32)) for i in range(8)]
    psb_bf = [es.enter_context(nc.psum_tensor("psb%d" % i, [128, 1024], BF16)) if False else None for i in range(8)]

    K.nc, K.P = nc, P

    ucnt = [0]

    def dma(q, out, in_, r, w, key):
        if key in ('c0', 'pout', 'yout', 'dbg', 'dbg2'):
            key = 'G_' + key
        elif key in ('sio', 'sout'):
            ucnt[0] += 1
            key = 's%d' % ucnt[0]
        P.add(q, lambda e: e.dma_start(out=out, in_=in_), r=r, w=w, dma=key)

    def mm(out, lhsT, rhs, start, stop, r, w):
        P.add('pe', lambda e: e.matmul(out, lhsT, rhs, start=start, stop=stop), r=r, w=w)

    def tr(out, in_, ident, r, w):
        P.add('pe', lambda e: e.transpose(out, in_, ident), r=r, w=w)

    def act(out, in_, func, r, w, bias=None, scale=None, accum=None):
        kw = {}
        if bias is not None:
            kw['bias'] = bias
        if scale is not None:
            kw['scale'] = scale
        if accum is not None:
            kw['accum_out'] = accum
        P.add('act', lambda e: e.activation(out, in_, func, **kw), r=r, w=w)

    def dve(fn, r, w):
        P.add('dve', fn, r=r, w=w)

    def pool(fn, r, w):
        P.add('pool', fn, r=r, w=w)

    K.dma, K.mm, K.tr, K.act, K.dve = dma, mm, tr, act, dve

    dma('sp', identf[:, :], ident_d, [], ['identf'], 'c0')
    dma('sp', maskbig[:, :], maskbig_d, [], ['maskbig'], 'c0')
    dma('sp', swam[:, :], swam_d, [], ['swam'], 'c0')
    dma('sp', swam0[:, :], swam0_d, [], ['swam0'], 'c0')
    dma('sp', amask[:, :], amask_d, [], ['amask'], 'c0')
    dma('sp', esel[:, :], esel_d, [], ['esel'], 'c0')
    dve(lambda e: e.memset(selT[:, :], 0.0), [], ['selT'])
    dma('sp', selT[0:4, :], sel_d, ['selT'], ['selT'], 'c0')
    dve(lambda e: e.memset(epsb[:, :], EPS), [], ['epsb'])
    dve(lambda e: e.memset(onesb[:, :], 1.0), [], ['onesb'])
    dve(lambda e: e.memset(onesf[:, :], 1.0), [], ['onesf'])
    dve(lambda e: e.tensor_copy(identb[:, :], identf[:, :]), ['identf'], ['identb'])

    def norm_T(xap, nt, col0, xkey, hkey, slot):
        hs = hn[0:nt, slot, :]
        ss = stat[0:nt, slot, 0:1]; rt = stat[0:nt, slot, 1:2]; rs = stat[0:nt, slot, 2:3]
        sk_ = 'st%d' % slot
        act(hs, xap, AF.Square, [xkey], ['hn%d' % slot, sk_], accum=ss)
        act(rt, ss, AF.Ln, [sk_, 'epsb'], [sk_], bias=epsb[0:nt, :], scale=1.0 / D)
        act(rs, rt, AF.Exp, [sk_], [sk_], scale=-0.5)
        dve(lambda e: e.scalar_tensor_tensor(hs, xap, rs, gain[0:nt, :], ALU.mult, ALU.mult),
            [xkey, sk_, 'gain'], ['hn%d' % slot])
        bank = psb[slot]
        tp = bank[:, :].bitcast(BF16).rearrange("p (c n) -> p c n", n=128)
        for dc in range(8):
            tr(tp[:, dc, 0:nt], hn[0:nt, slot, dc * 128:(dc + 1) * 128], identb[0:nt, 0:nt],
               ['hn%d' % slot, 'identb'], ['b%d' % slot])
        act(hTv[:, :, col0:col0 + nt], tp[:, :, 0:nt], AF.Copy, ['b%d' % slot], [hkey])

    def ffn(tag, subt, nrm_d, wg_d, wu_d, wd_d):
        dma('sp', gain[:, :], nrm_d.to_broadcast([128, D]), [], ['gain'], 'gain')
        ns_ = len(subt)
        dve(lambda e: e.memset(statF[:, 0:18], 1.0), [], ['statF'])
        for i, (xap, nt, col0, xkey, hkey) in enumerate(subt):
            act(hn[0:nt, i % 2, :], xap, AF.Square, [xkey], ['hn%d' % (i % 2), 'statF'], accum=statF[0:nt, i:i + 1])
        act(statF[:, 18:18 + ns_], statF[:, 0:ns_], AF.Ln, ['statF', 'epsb'], ['statF'], bias=epsb[:, :], scale=1.0 / D)
        act(statF[:, 18:18 + ns_], statF[:, 18:18 + ns_], AF.Exp, ['statF'], ['statF'], scale=-0.5)
        for i, (xap, nt, col0, xkey, hkey) in enumerate(subt):
            slot = i % 2
            hs = hn[0:nt, slot, :]
            rs = statF[0:nt, 18 + i:19 + i]
            dve(lambda e, hs=hs, xap=xap, rs=rs, nt=nt: e.scalar_tensor_tensor(hs, xap, rs, gain[0:nt, :], ALU.mult, ALU.mult),
                [xkey, 'statF', 'gain'], ['hn%d' % slot])
            tp = psb[slot][:, :].bitcast(BF16).rearrange("p (c n) -> p c n", n=128)
            for dc in range(8):
                tr(tp[:, dc, 0:nt], hn[0:nt, slot, dc * 128:(dc + 1) * 128], identb[0:nt, 0:nt], ['hn%d' % slot, 'identb'], ['b%d' % slot])
            act(hTv[:, :, col0:col0 + nt], tp[:, :, 0:nt], AF.Copy, ['b%d' % slot], [hkey])
        tts = []
        for (xap, nt, col0, xkey, hkey) in subt:
            t0 = (col0 // 512) * 512
            if not tts or tts[-1][0] != t0:
                tts.append([t0, 0, []])
            tts[-1][1] = col0 + nt - t0
            tts[-1][2].append(hkey)
        wgv = wg_d.rearrange("(c p) f -> p c f", p=128)
        wuv = wu_d.rearrange("(c p) f -> p c f", p=128)
        groups = [[0, 1, 2, 3], [4, 5, 6, 7], [8, 9, 10]]
        cnt = 0
        pcnt = 0
        dcnt = 0
        for gi, pairs in enumerate(groups):
            for jp in pairs:
                slot = pcnt % 2
                pcnt += 1
                dma('pool', wguv[:, slot, 0, :, :], wgv[:, :, jp * 256:(jp + 1) * 256], [], ['wg%d' % slot], 'wg%d' % slot)
                dma('pool', wguv[:, slot, 1, :, :], wuv[:, :, jp * 256:(jp + 1) * 256], [], ['wu%d' % slot], 'wu%d' % slot)
                for jj in range(2):
                    j = jp * 2 + jj
                    jl = (jp - pairs[0]) * 2 + jj
                    dma('pool', wdv[:, jl, :], wd_d[j * 128:(j + 1) * 128, :], [], ['wd%d' % jl], 'wd%d' % jl)
                    for ti, (t0, n, hkeys) in enumerate(tts):
                        s2 = cnt % 2
                        cnt += 1
                        gps = psb[s2][:, 0:n]; ups = psb[2 + s2][:, 0:n]
                        for dc in range(8):
                            mm(gps, wguv[:, slot, 0, dc, jj * 128:(jj + 1) * 128], hTv[:, dc, t0:t0 + n],
                               dc == 0, dc == 7, ['wg%d' % slot] + hkeys, ['b%d' % s2])
                        for dc in range(8):
                            mm(ups, wguv[:, slot, 1, dc, jj * 128:(jj + 1) * 128], hTv[:, dc, t0:t0 + n],
                               dc == 0, dc == 7, ['wu%d' % slot] + hkeys, ['b%d' % (2 + s2)])
                        act(sg[:, s2, 0:n], gps, AF.Silu, ['b%d' % s2], ['sg%d' % s2])
                        ao = actv[:, jl, t0:t0 + n]
                        sgs = sg[:, s2, 0:n]
                        dve(lambda e, ao=ao, sgs=sgs, ups=ups: e.tensor_tensor(ao, sgs, ups, ALU.mult),
                            ['sg%d' % s2, 'b%d' % (2 + s2)], ['act%d.%d' % (jl, ti)])
            nj = len(pairs) * 2
            for (xap, nt, col0, xkey, hkey) in subt:
                ti = [k for k, t in enumerate(tts) if t[0] == (col0 // 512) * 512][0]
                s2 = dcnt % 2
                dcnt += 1
                for half in range(2):
                    bk = 4 + 2 * s2 + half
                    for jl in range(nj):
                        mm(psb[bk][0:nt, :], actv[:, jl, col0:col0 + nt], wdv[:, jl, half * 512:(half + 1) * 512],
                           jl == 0, jl == nj - 1, ['act%d.%d' % (jl, ti), 'wd%d' % jl], ['b%d' % bk])
                for half in range(2):
                    bk = 4 + 2 * s2 + half
                    xh = xap[:, half * 512:(half + 1) * 512]
                    pb = psb[bk][0:nt, :]
                    dve(lambda e, xh=xh, pb=pb: e.scalar_tensor_tensor(xh, pb, 0.5, xh, ALU.mult, ALU.add),
                        ['b%d' % bk, xkey], [xkey])

    K.ffn = ffn
    tmpA = sb("tmpA", [128, 1280])
    gts = tmpA[0:4, 0:768].rearrange("p (a n) -> p a n", n=128)
    smv = tmpA[:, 0:512].rearrange("p (h n) -> p h n", n=256)
    Pbv = tmpA[:, 512:768].bitcast(BF16).rearrange("p (h n) -> p h n", n=256)
    PTv = tmpA[:, 768:1024].bitcast(BF16).rearrange("p (h s n) -> p h s n", h=2, s=2)
    rowst = tmpA[:, 1024:1280]
    tmpB = sb("tmpB", [128, 1024])
    hnum = tmpB[:, 0:512].rearrange("p (h n) -> p h n", n=128)
    sigo = tmpB[:, 512:1024]
    mixb = sb("mixb", [128, 7 * 512], BF16)
    kM = mixb[:, 0:512]; vaug = mixb[:, 512:1024].rearrange("p (h n) -> p h n", n=128)
    vw = mixb[:, 1024:1536].rearrange("p (h n) -> p h n", n=128)
    qT = mixb[:, 1536:2048].rearrange("p (h n) -> p h n", n=128)
    kT = mixb[:, 2048:2560].rearrange("p (h n) -> p h n", n=128)
    SwT = mixb[:, 2560:3072].rearrange("p (h n) -> p h n", n=128)
    qn = mixb[:, 3072:3584]
    CT = sb("CT", [128, 512]); CTv = CT[:, :].rearrange("p (h n) -> p h n", n=128)
    smallb = sb("smallb", [128, 2048], BF16)
    CTb = smallb[:, 0:512].rearrange("p (h n) -> p h n", n=128)
    qaT = smallb[:, 512:1024].rearrange("p (h n) -> p h n", n=128)
    kaT = smallb[:, 1024:1280].rearrange("p (s n) -> p s n", n=128)
    vA = smallb[:, 1280:1536].rearrange("p (s n) -> p s n", n=128)
    knb = smallb[:, 1536:1664]; nTb = smallb[:, 1664:1668]; wendb = smallb[:, 1668:1672]
    scal = sb("scal", [128, 256])
    gtok = scal[:, 0:12]; mu_c = scal[:, 12:16]; mu_p = scal[:, 16:20]; wend = scal[:, 20:24]
    winter = scal[:, 24:28]; enm = scal[:, 28:32]; decay = scal[:, 32:36]; nT = scal[:, 36:40]
    den = scal[:, 40:44]; t4a = scal[:, 44:48]; t4b = scal[:, 48:52]; ssh = scal[:, 52:56]; comb = scal[:, 56:60]
    car = scal[0:4, 60:64]
    car2 = scal[0:4, 64:68]
    ssq = scal[:, 68:76]; rq = scal[:, 76:84]; ssk = scal[:, 84:86]; rk = scal[:, 86:88]
    mx = scal[:, 88:96]; negm2 = scal[:, 96:104]; rsum = scal[:, 104:112]; esnk = scal[:, 112:120]
    snkb = scal[:, 120:128]; gqb = scal[:, 128:192]; gkb = scal[:, 192:256]
    qsq = sg[:, 0, :]; knf = sg[:, 1, 0:128]; kvout = sg[:, 1, 128:384]
    ybf = hn[:, 0, :]; yTt = hn[:, 1, :].rearrange("p (c n) -> p c n", n=128)
    gonb = gain[:, 0:512]; DT = gain[:, 512:1024].rearrange("p (h n) -> p h n", n=128)
    ACTKEYS = ['act%d.%d' % (a, b) for a in range(8) for b in range(5)] + ['wd%d' % a for a in range(8)]
    WINBLK = [(C_K, C_V, 'win_k'), (C_I, C_QA, 'win_g'), (C_V, C_O, 'win_v'), (C_O, C_I, 'win_o'), (C_Q, C_K, 'win_q'),
              (C_KA, PW, 'win_ka'), (C_QA, C_KA, 'win_qa')]
    WINKEYS = [b_[2] for b_ in WINBLK]

    def wkey(c0):
        for (a0, a1, kk) in WINBLK:
            if a0 <= c0 < a1:
                return kk
        raise KeyError(c0)
    WGKEYS = ['wg0', 'wg1', 'wu0', 'wu1']
    selv = selT[0:4, :].rearrange("p (h n) -> p h n", n=128)

    def bank(i, h=None):
        return psb[i]

    def mixer_setup(first):
        winD = win_d.rearrange("(c p) f -> p c f", p=128)
        first_blk = True
        for (c0, c1, kk) in WINBLK:
            dma('pool', winv[:, :, c0:c1], winD[:, :, c0:c1], [], (ACTKEYS if first_blk else []) + [kk], kk)
            first_blk = False
        if not first:
            dma('pool', woutv[:, :, :], wout_d.rearrange("(c p) f -> p c f", p=128), [], WGKEYS + ['wout'], 'wout')
        dve(lambda e: e.memset(rowst[:, 0:1], 0.0), [], ['gain', 'gainA', 'gainB', 'tmpA'])
        dma('sp', gain[:, :], nm_d.to_broadcast([128, D]), [], ['gain', 'gainA', 'gainB'], 'gain')

    def mixer_setup2():
        dma('sp', gonb, gon_d.to_broadcast([128, 512]), [], ['gain', 'gainA', 'gainB'], 'gain')

    def consts_mixer():
        dma('sp', gqb, gq_d.to_broadcast([128, 64]), [], ['gqb'], 'c0')
        dma('sp', gkb, gk_d.to_broadcast([128, 64]), [], ['gkb'], 'c0')
        dma('sp', snkb, snk_d.to_broadcast([128, 8]), [], ['snkb'], 'c0')
        dve(lambda e: e.tensor_scalar(nsnkb, snkb, -1.0, None, ALU.mult), ['snkb'], ['nsnkb'])
        dma('sp', car[:, 2:3], bf_d.rearrange("o h -> h o"), [], ['car2'], 'c0')
        dma('sp', car[:, 3:4], bi_d.rearrange("o h -> h o"), [], ['car3'], 'c0')
        dve(lambda e: e.tensor_scalar(car[:, 2:3], car[:, 2:3], -1.0, None, ALU.mult), ['car2'], ['car2'])
        dve(lambda e: e.tensor_tensor(car[:, 3:4], car[:, 3:4], amask[0:4, 0:1], ALU.add), ['car3', 'amask'], ['car3'])
        dve(lambda e: e.tensor_scalar(car2[:, 0:1], amask[0:4, 1:2], -1.0, None, ALU.mult), ['amask'], ['car4'])
        dve(lambda e: e.memset(car[:, 0:2], 0.0), [], ['Bcar', 'Gcar'])
        dve(lambda e: e.memset(CT[:, :], 0.0), [], ['CT'])
        dve(lambda e: e.memset(nT, 0.0), [], ['nT'])
        dve(lambda e: e.memset(mu_p, 0.0), [], ['mu_p'])
        dve(lambda e: e.memset(smallb[:, :], 0.0), [], ['CTb', 'qaT', 'kaT0', 'kaT1', 'vA0', 'vA1', 'knb', 'nTb', 'wendb'])

    def chunk(c, full, xap, xkey, hkey, col0, kv_need, first_block):
        hcols = lambda dc: hTv[:, dc, col0:col0 + 128]
        b0 = psb[0]
        for k_, cc in ((0, C_I), (1, C_F)):
            for dc in range(8):
                mm(b0[0:4, k_ * 128:(k_ + 1) * 128], winv[:, dc, cc:cc + 4], hcols(dc), dc == 0, dc == 7,
                   ['win', hkey], ['b0'])
        e_, B_, a_, G_, nm_, on_ = [gts[:, i, :] for i in range(6)]
        dve(lambda e: e.memset(on_, 1.0), [], ['tmpA'])
        act(e_, b0[0:4, 128:256], AF.Exp, ['b0', 'car2'], ['tmpA'], bias=car[:, 2:3], scale=-1.0)
        act(e_, e_, AF.Ln, ['tmpA'], ['tmpA'], bias=1.0)
        dve(lambda e: e.tensor_scalar(e_, e_, car2[:, 0:1], None, ALU.mult), ['tmpA', 'car4'], ['tmpA'])
        dve(lambda e: e.tensor_tensor_scan(B_, on_, e_, car[:, 0:1], ALU.mult, ALU.add), ['tmpA', 'Bcar'], ['tmpA'])
        dve(lambda e: e.scalar_tensor_tensor(a_, b0[0:4, 0:128], car[:, 3:4], B_, ALU.add, ALU.subtract),
            ['b0', 'car3', 'tmpA'], ['tmpA'])
        dve(lambda e: e.tensor_tensor_scan(G_, a_, a_, car[:, 1:2], ALU.max, ALU.max), ['tmpA', 'Gcar'], ['tmpA'])
        dve(lambda e: e.scalar_tensor_tensor(nm_, B_, -1.0, G_, ALU.mult, ALU.subtract), ['tmpA'], ['tmpA'])
        dve(lambda e: e.tensor_copy(car[:, 0:1], B_[:, 127:128]), ['tmpA'], ['Bcar'])
        dve(lambda e: e.tensor_copy(car[:, 1:2], G_[:, 127:128]), ['tmpA'], ['Gcar'])
        for i_, src in enumerate((a_, G_, nm_)):
            tr(b0[:, 256 + 4 * i_:260 + 4 * i_], src, identf[0:4, 0:4], ['tmpA', 'identf'], ['b0'])
        for h in range(4):
            mm(b0[:, 272 + h:273 + h], selv[:, h, :], G_[:, 127:128], True, True, ['selT', 'tmpA'], ['b0'])
        dve(lambda e: e.tensor_copy(scal[:, 0:12], b0[:, 256:268]), ['b0'], ['gtok'])
        dve(lambda e: e.tensor_copy(mu_c, b0[:, 272:276]), ['b0'], ['gtok'])
        if full:
            for h in range(4):
                mm(psb[2][:, h * 128:(h + 1) * 128], selv[:, h, :], G_, True, False, ['selT', 'tmpA'], ['b2'])
                mm(psb[2][:, h * 128:(h + 1) * 128], identf[:, :], maskbig[:, :], False, True, ['identf', 'maskbig'], ['b2'])
        def proj(bk, c0, n):
            for dc in range(8):
                mm(psb[bk][:, 0:n], hcols(dc), winv[:, dc, c0:c0 + n], dc == 0, dc == 7, ['win', hkey], ['b%d' % bk])
        proj(1, C_K, 512)
        act(kM, psb[1][:, :], AF.Copy, ['b1'], ['kM'], scale=128.0 ** -0.5)
        proj(3, C_V, 512)
        dve(lambda e: e.tensor_copy(vaug, psb[3][:, :].rearrange("p (h n) -> p h n", n=128)), ['b3'], ['vaug'])
        dve(lambda e: e.tensor_tensor(wend, gtok[:, 0:4], mu_c, ALU.subtract), ['gtok'], ['wend'])
        act(wend, wend, AF.Exp, ['wend'], ['wend'])
        dve(lambda e: e.tensor_copy(wendb, wend), ['wend'], ['wendb'])
        dve(lambda e: e.tensor_tensor(decay, mu_p, mu_c, ALU.subtract), ['mu_p', 'gtok'], ['decay'])
        act(decay, decay, AF.Exp, ['decay'], ['decay'])
        if full:
            dve(lambda e: e.tensor_tensor(winter, mu_p, gtok[:, 4:8], ALU.subtract), ['mu_p', 'gtok'], ['winter'])
            act(winter, winter, AF.Exp, ['winter'], ['winter'])
            act(enm, gtok[:, 8:12], AF.Exp, ['gtok'], ['enm'])
        dve(lambda e: e.tensor_copy(mu_p, mu_c), ['gtok', 'decay', 'winter'], ['mu_p'])
        if full:
            proj(4, C_O, 512)
            act(sigo, psb[4][:, :], AF.Sigmoid, ['b4'], ['sigo'])
            dve(lambda e: e.tensor_tensor(sigo, sigo, gonb, ALU.mult), ['sigo', 'gainA'], ['sigo'])
            for h in range(4):
                for dc in range(8):
                    mm(psb[5][:, h * 128:(h + 1) * 128], winv[:, dc, C_Q + h * 128:C_Q + (h + 1) * 128], hcols(dc),
                       dc == 0, dc == 7, ['win', hkey], ['b5'])
            for h in range(4):
                for dc in range(8):
                    mm(psb[6][:, h * 128:(h + 1) * 128], winv[:, dc, C_K + h * 128:C_K + (h + 1) * 128], hcols(dc),
                       dc == 0, dc == 7, ['win', hkey], ['b6'])
            act(qT, psb[5][:, :].rearrange("p (h n) -> p h n", n=128), AF.Copy, ['b5'], ['qT'])
            act(kT, psb[6][:, :].rearrange("p (h n) -> p h n", n=128), AF.Copy, ['b6'], ['kT'], scale=128.0 ** -0.5)
            for h in range(4):
                mm(psb[1][:, h * 128:(h + 1) * 128], kT[:, h, :], qT[:, h, :], True, True, ['kT', 'qT'], ['b1'])
            for h in range(4):
                act(DT[:, h, :], psb[2][:, h * 128:(h + 1) * 128], AF.Exp, ['b2', 'gtok'], ['gainB'],
                    bias=gtok[:, h:h + 1], scale=-1.0)
            dve(lambda e: e.tensor_tensor(SwT, psb[1][:, :].rearrange("p (h n) -> p h n", n=128), DT, ALU.mult),
                ['b1', 'gainB'], ['SwT'])
            for h in range(4):
                mm(psb[3][:, h * 128:(h + 1) * 128], SwT[:, h, :], vaug[:, h, :], True, True, ['SwT', 'vaug'], ['b3'])
            for h in range(4):
                mm(psb[5][:, h * 128:(h + 1) * 128], qT[:, h, :], CTb[:, h, :], True, True, ['qT', 'CTb'], ['b5'])
            for h in range(4):
                mm(b0[:, 280 + h:281 + h], SwT[:, h, :], onesb[:, 0:1], True, True, ['SwT', 'onesb'], ['b0'])
            for h in range(4):
                mm(b0[:, 284 + h:285 + h], qT[:, h, :], nTb[:, h:h + 1], True, True, ['qT', 'nTb'], ['b0'])
            act(hnum, psb[3][:, :].rearrange("p (h n) -> p h n", n=128), AF.Copy, ['b3'], ['hnum'])
            for h in range(4):
                dve(lambda e, h=h: e.scalar_tensor_tensor(hnum[:, h, :], psb[5][:, h * 128:(h + 1) * 128], winter[:, h:h + 1],
                                                         hnum[:, h, :], ALU.mult, ALU.add), ['b5', 'winter', 'hnum'], ['hnum'])
            dve(lambda e: e.tensor_copy(den, b0[:, 280:284]), ['b0'], ['den'])
            dve(lambda e: e.tensor_tensor(t4a, b0[:, 284:288], winter, ALU.mult), ['b0', 'winter'], ['t4a'])
            dve(lambda e: e.tensor_tensor(den, den, t4a, ALU.add), ['den', 't4a'], ['den'])
            dve(lambda e: e.tensor_scalar(t4a, den, -1.0, None, ALU.mult), ['den'], ['t4a'])
            dve(lambda e: e.tensor_tensor(den, den, t4a, ALU.max), ['den', 't4a'], ['den'])
            dve(lambda e: e.tensor_tensor(den, den, enm, ALU.max), ['den', 'enm'], ['den'])
            dve(lambda e: e.reciprocal(den, den), ['den'], ['den'])
            for h in range(4):
                act(qsq[:, 0:128], hnum[:, h, :], AF.Square, ['hnum', 'den'], ['sg0', 'ssh%d' % h], scale=den[:, h:h + 1],
                    accum=ssh[:, h:h + 1])
            act(t4b, ssh, AF.Sqrt, ['ssh0', 'ssh1', 'ssh2', 'ssh3', 'epsb'], ['t4b'], bias=epsb[:, :], scale=1.0 / 128)
            dve(lambda e: e.reciprocal(t4b, t4b), ['t4b'], ['t4b'])
            dve(lambda e: e.tensor_tensor(comb, t4b, den, ALU.mult), ['t4b', 'den'], ['comb'])
            for h in range(4):
                dve(lambda e, h=h: e.scalar_tensor_tensor(ybf[:, h * 128:(h + 1) * 128], hnum[:, h, :], comb[:, h:h + 1],
                                                         sigo[:, h * 128:(h + 1) * 128], ALU.mult, ALU.mult),
                    ['hnum', 'comb', 'sigo'], ['hn0'])
        for h in range(4):
            dve(lambda e, h=h: e.tensor_scalar(vw[:, h, :], vaug[:, h, :], wend[:, h:h + 1], None, ALU.mult), ['vaug', 'wend'], ['vw'])
        for h in range(4):
            mm(psb[7][:, h * 128:(h + 1) * 128], kM[:, h * 128:(h + 1) * 128], vw[:, h, :], True, True, ['kM', 'vw'], ['b7'])
        for h in range(4):
            mm(b0[:, 288 + h:289 + h], kM[:, h * 128:(h + 1) * 128], wendb[:, h:h + 1], True, True, ['kM', 'wendb'], ['b0'])
        for h in range(4):
            dve(lambda e, h=h: e.scalar_tensor_tensor(CTv[:, h, :], CTv[:, h, :], decay[:, h:h + 1], psb[7][:, h * 128:(h + 1) * 128],
                                                     ALU.mult, ALU.add), ['CT', 'decay', 'b7'], ['CT'])
        act(CTb, CTv, AF.Copy, ['CT'], ['CTb'])
        dve(lambda e: e.tensor_tensor(nT, nT, decay, ALU.mult), ['nT', 'decay', 'nTb'], ['nT'])
        dve(lambda e: e.tensor_tensor(nT, nT, b0[:, 288:292], ALU.add), ['nT', 'b0'], ['nT'])
        dve(lambda e: e.tensor_copy(nTb, nT), ['nT'], ['nTb'])
        if dbg2_d is not None and full:
            dma('sp', dbg2_d[:, c * 64:(c + 1) * 64], scal[:, 0:64], ['gtok', 'wend', 'decay', 'winter', 'enm', 'nT', 'den', 'comb'], [], 'dbg2')
        cur = c % 2
        prv = 1 - cur
        if full or kv_need:
            proj(4, C_KA, 256)
            kaps = psb[4][:, 0:128]
            act(qsq[:, 0:128], kaps, AF.Square, ['b4'], ['sg0'])
            dve(lambda e: e.tensor_reduce(ssk, qsq[:, 0:128].rearrange("p (h n) -> p h n", n=64), AX.X, ALU.add), ['sg0'], ['ssk'])
            act(rk, ssk, AF.Sqrt, ['ssk', 'epsb'], ['rk'], bias=epsb[:, :], scale=1.0 / 64)
            dve(lambda e: e.reciprocal(rk, rk), ['rk'], ['rk'])
            for h in range(2):
                dve(lambda e, h=h: e.scalar_tensor_tensor(knf[:, h * 64:(h + 1) * 64], kaps[:, h * 64:(h + 1) * 64], rk[:, h:h + 1], gkb,
                                                         ALU.mult, ALU.mult), ['b4', 'rk', 'gkb'], ['sg1'])
            dve(lambda e: e.tensor_copy(knb, knf), ['sg1'], ['knb'])
            dve(lambda e: e.tensor_copy(vA[:, cur, :], psb[4][:, 128:256]), ['b4'], ['vA%d' % cur])
            if kv_need:
                dve(lambda e: e.tensor_copy(kvout[:, 0:128], knf), ['sg1'], ['sg1'])
                dve(lambda e: e.tensor_copy(kvout[:, 128:256], psb[4][:, 128:256]), ['b4'], ['sg1'])
            tpk = psb[6][:, :].bitcast(BF16)
            tr(tpk[:, 0:128], knb, identb[:, :], ['knb', 'identb'], ['b6'])
            act(kaT[:, cur, :], tpk[:, 0:128], AF.Copy, ['b6'], ['kaT%d' % cur])
        if full:
            proj(4, C_QA, 512)
            qaps = psb[4][:, :]
            act(qsq, qaps, AF.Square, ['b4'], ['sg0'])
            dve(lambda e: e.tensor_reduce(ssq, qsq.rearrange("p (h n) -> p h n", n=64), AX.X, ALU.add), ['sg0'], ['ssq'])
            act(rq, ssq, AF.Sqrt, ['ssq', 'epsb'], ['rq'], bias=epsb[:, :], scale=1.0 / 64)
            dve(lambda e: e.reciprocal(rq, rq), ['rq'], ['rq'])
            for h in range(8):
                qo = (h % 4) * 128 + (h // 4) * 64
                dve(lambda e, h=h, qo=qo: e.scalar_tensor_tensor(qn[:, qo:qo + 64], qaps[:, h * 64:(h + 1) * 64], rq[:, h:h + 1], gqb,
                                                                ALU.mult, ALU.mult), ['b4', 'rq', 'gqb'], ['qn'])
            tpq = psb[5][:, :].bitcast(BF16).rearrange("p (c n) -> p c n", n=128)
            for p_ in range(4):
                tr(tpq[:, p_, :], qn[:, p_ * 128:(p_ + 1) * 128], identb[:, :], ['qn', 'identb'], ['b5'])
            act(qaT, tpq[:, 0:4, :], AF.Copy, ['b5'], ['qaT'])
            msk = swam0 if first_block else swam
            mkey = 'swam0' if first_block else 'swam'
            ops_ = psb[3]
            for grp in range(4):
                heads = [2 * grp, 2 * grp + 1]
                scb = psb[1 + (grp % 2)]
                bkey = 'b%d' % (1 + (grp % 2))
                for i_, hd in enumerate(heads):
                    kvh, g_ = hd // 4, hd % 4
                    pr = slice(kvh * 64, (kvh + 1) * 64)
                    mm(scb[:, i_ * 256:i_ * 256 + 128], qaT[pr, g_, :], kaT[pr, prv, :], True, True,
                       ['qaT', 'kaT%d' % prv], [bkey])
                    mm(scb[:, i_ * 256 + 128:i_ * 256 + 256], qaT[pr, g_, :], kaT[pr, cur, :], True, True,
                       ['qaT', 'kaT%d' % cur], [bkey])
                for i_ in range(2):
                    dve(lambda e, scb=scb, msk=msk, i_=i_: e.scalar_tensor_tensor(
                        smv[:, i_, :], scb[:, i_ * 256:(i_ + 1) * 256], 0.125, msk[:, :], ALU.mult, ALU.add), [bkey, mkey], ['tmpA'])
                dve(lambda e, grp=grp: e.tensor_reduce(mx[:, 2 * grp:2 * grp + 2], smv, AX.X, ALU.max), ['tmpA'], ['mx'])
                dve(lambda e, grp=grp: e.tensor_tensor(mx[:, 2 * grp:2 * grp + 2], mx[:, 2 * grp:2 * grp + 2],
                                                      snkb[:, 2 * grp:2 * grp + 2], ALU.max), ['mx', 'snkb'], ['mx'])
                dve(lambda e, grp=grp: e.tensor_scalar(negm2[:, 2 * grp:2 * grp + 2], mx[:, 2 * grp:2 * grp + 2], -1.0, None, ALU.mult),
                    ['mx'], ['negm2'])
                for i_, hd in enumerate(heads):
                    act(Pbv[:, i_, :], smv[:, i_, :], AF.Exp, ['tmpA', 'negm2'], ['Pb', 'rsum%d' % hd], bias=negm2[:, hd:hd + 1],
                        accum=rsum[:, hd:hd + 1])
                tpp = psb[5 + (grp % 2)][:, :].bitcast(BF16).rearrange("p (h s n) -> p h s n", h=4, s=2)
                pkey = 'b%d' % (5 + (grp % 2))
                for i_, hd in enumerate(heads):
                    for s_ in range(2):
                        tr(tpp[:, i_, s_, :], Pbv[:, i_, s_ * 128:(s_ + 1) * 128], identb[:, :], ['Pb', 'identb'], [pkey])
                act(PTv, tpp[:, 0:2, :, :], AF.Copy, [pkey], ['PT'])
                for i_, hd in enumerate(heads):
                    kvh = hd // 4
                    mm(ops_[:, hd * 64:(hd + 1) * 64], PTv[:, i_, 0, :], vA[:, prv, kvh * 64:(kvh + 1) * 64], True, False,
                       ['PT', 'vA%d' % prv], ['b3'])
                    mm(ops_[:, hd * 64:(hd + 1) * 64], PTv[:, i_, 1, :], vA[:, cur, kvh * 64:(kvh + 1) * 64], False, True,
                       ['PT', 'vA%d' % cur], ['b3'])
            dve(lambda e: e.tensor_tensor(esnk, snkb, negm2, ALU.add), ['snkb', 'negm2'], ['esnk'])
            act(esnk, esnk, AF.Exp, ['esnk'], ['esnk'])
            dve(lambda e: e.tensor_tensor(esnk, esnk, rsum, ALU.add), ['esnk'] + ['rsum%d' % h for h in range(8)], ['esnk'])
            dve(lambda e: e.reciprocal(esnk, esnk), ['esnk'], ['esnk'])
            for h in range(8):
                dve(lambda e, h=h: e.tensor_scalar(ybf[:, 512 + h * 64:512 + (h + 1) * 64], ops_[:, h * 64:(h + 1) * 64],
                                                  esnk[:, h:h + 1], None, ALU.mult), ['b3', 'esnk'], ['hn0'])
            if dbg_d is not None:
                dma('pool', dbg_d[col0:col0 + 128, :], ybf, ['hn0'], [], 'dbg')
            tpy = psb[1][:, :].bitcast(BF16).rearrange("p (c n) -> p c n", n=128)
            for fc in range(8):
                tr(tpy[:, fc, :], ybf[:, fc * 128:(fc + 1) * 128], identb[:, :], ['hn0', 'identb'], ['b1'])
            act(yTt, tpy, AF.Copy, ['b1'], ['hn1'])
            for half in range(2):
                for fc in range(8):
                    mm(psb[6 + half][:, :], yTt[:, fc, :], woutv[:, fc, half * 512:(half + 1) * 512], fc == 0, fc == 7,
                       ['hn1', 'wout'], ['b%d' % (6 + half)])
            for half in range(2):
                xh = xap[:, half * 512:(half + 1) * 512]
                pb = psb[6 + half][:, :]
                dve(lambda e, xh=xh, pb=pb: e.tensor_tensor(xh, xh, pb, ALU.add), ['b%d' % (6 + half), xkey], [xkey])

    hTr = [hT[:, s * 1024:(s + 1) * 1024].rearrange("p (c n) -> p c n", n=128) for s in range(2)]
    hTs = hT[:, 2048:2176].rearrange("p (c n) -> p c n", n=NS)
    hoff = [2176]

    def hb(n):
        a = hoff[0]
        hoff[0] += n
        assert hoff[0] <= 16512
        return hT[:, a:a + n]

    def hbf(n):
        return hb(2 * n).bitcast(F32)

    v4 = lambda ap: ap.rearrange("p (h n) -> p h n", n=128)
    kM_s = [kM, hb(512)]; vaug_s = [vaug, v4(hb(512))]; qT_s = [qT, v4(hb(512))]; kT_s = [kT, v4(hb(512))]
    qaT_s = [qaT, v4(hb(512))]; sigo_s = [sigo, hbf(512)]; DT_s = [DT, v4(hbf(512))]
    gts_off = hoff[0]
    gts2_s = [hbf(768)[0:4, :].rearrange("p (a n) -> p a n", n=128) for _ in range(2)]
    gtsA = hT[:, gts_off:gts_off + 3072].bitcast(F32)[0:4, :].rearrange("p (a n) -> p a n", n=256)
    SC = [hbf(64), hbf(64)]
    kaT3 = [kaT[:, 0, :], kaT[:, 1, :], hb(128)]; vA3 = [vA[:, 0, :], vA[:, 1, :], hb(128)]
    qsqS = hbf(512); knfS = hbf(128); knbS = hb(128); qnS = hb(512); hnS = hb(1024); statS = hbf(4)
    gonbS = hbf(512); DT_s[0] = v4(hbf(512))
    smv4 = tmpA[:, 0:1024].rearrange("p (h n) -> p h n", n=256)
    nsnkb = tmpA[:, 1032:1040]
    tmpC = sb("tmpC", [128, 512])
    Pbv4 = tmpC[:, :].bitcast(BF16).rearrange("p (h n) -> p h n", n=256)
    PTv4 = hb(1024).rearrange("p (h s n) -> p h s n", h=4, s=2)
    PIPEKEYS = (['kM1', 'vaug1', 'qT1', 'kT1', 'qaT1', 'sigo1', 'gainB1', 'gts2_0', 'gts2_1', 'SC0', 'SC1', 'kaT2', 'vA2', 'qsqS', 'knS', 'qnS',
                 'hnS', 'hTr0', 'hTr1', 'hTs', 'gonbS', 'DT0', 'PT'])
    HKALL = ['hT%d' % c for c in range(17)]

    def pipe_begin():
        dve(lambda e: e.memset(statS, 0.0), [], HKALL + PIPEKEYS + ['hn0', 'ybm', 'yba'])
        dma('sp', gonbS, gon_d.to_broadcast([128, 512]), [], ['gonbS'], 'gain2')

    def pipe_end():
        dve(lambda e: e.memset(statS, 0.0), [], PIPEKEYS + HKALL)

    def stage0(c, xap, xkey, nt=128, dst=None, dkey=None):
        S = c % 2
        dst = hTr[S] if dst is None else dst
        dkey = ('hTr%d' % S) if dkey is None else dkey
        hs = hnS[0:nt, :]
        ss = statS[0:nt, 0:1]; rt = statS[0:nt, 1:2]; rs = statS[0:nt, 2:3]
        act(hs, xap, AF.Square, [xkey], ['hnS', 'stS'], accum=ss)
        act(rt, ss, AF.Ln, ['stS', 'epsb'], ['stS'], bias=epsb[0:nt, :], scale=1.0 / D)
        act(rs, rt, AF.Exp, ['stS'], ['stS'], scale=-0.5)
        dve(lambda e: e.scalar_tensor_tensor(hs, xap, rs, gain[0:nt, :], ALU.mult, ALU.mult), [xkey, 'stS', 'gain'], ['hnS'])
        yield
        tp = psb[1 + S][:, :].bitcast(BF16).rearrange("p (c n) -> p c n", n=128)
        for dc in range(8):
            tr(tp[:, dc, 0:nt], hnS[0:nt, dc * 128:(dc + 1) * 128], identb[0:nt, 0:nt], ['hnS', 'identb'], ['b%d' % (1 + S)])
        act(dst, tp[:, :, 0:nt], AF.Copy, ['b%d' % (1 + S)], [dkey])
        yield

    def stage1(c, full, kv_need, gc):
        S = c % 2
        hk = 'hTr%d' % S
        hcols = lambda dc: hTr[S][:, dc, :]
        b0 = psb[0]
        sc = SC[S]; sk = 'SC%d' % S
        gtok_ = sc[:, 0:12]; mu_c_ = sc[:, 12:16]; wend_ = sc[:, 16:20]; winter_ = sc[:, 20:24]; enm_ = sc[:, 24:28]
        decay_ = sc[:, 28:32]; wendb_ = sc[:, 32:34].bitcast(BF16)
        K3 = gc % 3
        for k_, cc in ((0, C_I), (1, C_F)):
            for dc in range(8):
                mm(b0[0:4, k_ * 128:(k_ + 1) * 128], winv[:, dc, cc:cc + 4], hcols(dc), dc == 0, dc == 7, ['win_g', hk], ['b0'])
        e_, B_, a_, G_, nm_, on_ = [gts2_s[S][:, i, :] for i in range(6)]
        GK = ['gts2_%d' % S]
        dve(lambda e: e.memset(on_, 1.0), [], GK)
        act(e_, b0[0:4, 128:256], AF.Exp, ['b0', 'car2'], GK, bias=car[:, 2:3], scale=-1.0)
        act(e_, e_, AF.Ln, GK, GK, bias=1.0)
        dve(lambda e: e.tensor_scalar(e_, e_, car2[:, 0:1], None, ALU.mult), GK + ['car4'], GK)
        dve(lambda e: e.tensor_tensor_scan(B_, on_, e_, car[:, 0:1], ALU.mult, ALU.add), GK + ['Bcar'], GK)
        dve(lambda e: e.scalar_tensor_tensor(a_, b0[0:4, 0:128], car[:, 3:4], B_, ALU.add, ALU.subtract), ['b0', 'car3'] + GK, GK)
        dve(lambda e: e.tensor_tensor_scan(G_, a_, a_, car[:, 1:2], ALU.max, ALU.max), GK + ['Gcar'], GK)
        dve(lambda e: e.scalar_tensor_tensor(nm_, B_, -1.0, G_, ALU.mult, ALU.subtract), GK, GK)
        dve(lambda e: e.tensor_copy(car[:, 0:1], B_[:, 127:128]), GK, ['Bcar'])
        dve(lambda e: e.tensor_copy(car[:, 1:2], G_[:, 127:128]), GK, ['Gcar'])
        yield

        def proj(bk, c0, n):
            for dc in range(8):
                mm(psb[bk][:, 0:n], hcols(dc), winv[:, dc, c0:c0 + n], dc == 0, dc == 7, [wkey(c0), hk], ['b%d' % bk])
        proj(1, C_K, 512)
        act(kM_s[S], psb[1][:, :], AF.Copy, ['b1'], ['kM%d' % S], scale=128.0 ** -0.5)
        proj(2, C_V, 512)
        dve(lambda e: e.tensor_copy(vaug_s[S], v4(psb[2][:, :])), ['b2'], ['vaug%d' % S])
        yield
        for i_, src in enumerate((a_, G_, nm_)):
            tr(b0[:, 256 + 4 * i_:260 + 4 * i_], src, identf[0:4, 0:4], GK + ['identf'], ['b0'])
        for h in range(4):
            mm(b0[:, 272 + h:273 + h], selv[:, h, :], G_[:, 127:128], True, True, ['selT'] + GK, ['b0'])
        dve(lambda e: e.tensor_copy(sc[:, 0:16], b0[:, 256:272]), ['b0'], [sk])
        dve(lambda e: e.tensor_copy(mu_c_, b0[:, 272:276]), ['b0'], [sk])
        dve(lambda e: e.tensor_tensor(wend_, gtok_[:, 0:4], mu_c_, ALU.subtract), [sk], [sk])
        act(wend_, wend_, AF.Exp, [sk], [sk])
        dve(lambda e: e.tensor_copy(wendb_, wend_), [sk], [sk])
        dve(lambda e: e.tensor_tensor(decay_, mu_p, mu_c_, ALU.subtract), ['mu_p', sk], [sk])
        act(decay_, decay_, AF.Exp, [sk], [sk])
        if full:
            dve(lambda e: e.tensor_tensor(winter_, mu_p, gtok_[:, 4:8], ALU.subtract), ['mu_p', sk], [sk])
            act(winter_, winter_, AF.Exp, [sk], [sk])
            act(enm_, gtok_[:, 8:12], AF.Exp, [sk], [sk])
        dve(lambda e: e.tensor_copy(mu_p, mu_c_), [sk], ['mu_p'])
        yield
        if full:
            proj(1, C_O, 512)
            act(sigo_s[S], psb[1][:, :], AF.Exp, ['b1'], ['sigo%d' % S], scale=-1.0)
            act(sigo_s[S], sigo_s[S], AF.Ln, ['sigo%d' % S], ['sigo%d' % S], bias=1.0)
            act(sigo_s[S], sigo_s[S], AF.Exp, ['sigo%d' % S], ['sigo%d' % S], scale=-1.0)
            dve(lambda e: e.tensor_tensor(sigo_s[S], sigo_s[S], gonbS, ALU.mult), ['sigo%d' % S, 'gonbS'], ['sigo%d' % S])
            for h in range(4):
                for dc in range(8):
                    mm(psb[2][:, h * 128:(h + 1) * 128], winv[:, dc, C_Q + h * 128:C_Q + (h + 1) * 128], hcols(dc),
                       dc == 0, dc == 7, ['win_q', hk], ['b2'])
            act(qT_s[S], v4(psb[2][:, :]), AF.Copy, ['b2'], ['qT%d' % S])
            yield
            for h in range(4):
                for dc in range(8):
                    mm(psb[1][:, h * 128:(h + 1) * 128], winv[:, dc, C_K + h * 128:C_K + (h + 1) * 128], hcols(dc),
                       dc == 0, dc == 7, ['win_k', hk], ['b1'])
            act(kT_s[S], v4(psb[1][:, :]), AF.Copy, ['b1'], ['kT%d' % S], scale=128.0 ** -0.5)
            for h in range(4):
                mm(psb[2][:, h * 128:(h + 1) * 128], selv[:, h, :], G_, True, False, ['selT'] + GK, ['b2'])
                mm(psb[2][:, h * 128:(h + 1) * 128], identf[:, :], maskbig[:, :], False, True, ['identf', 'maskbig'], ['b2'])
            dk = 'DT0' if S == 0 else 'gainB1'
            for h in range(4):
                act(DT_s[S][:, h, :], psb[2][:, h * 128:(h + 1) * 128], AF.Exp, ['b2', sk], [dk], bias=gtok_[:, h:h + 1], scale=-1.0)
            yield
        if full or kv_need:
            proj(1, C_KA, 256)
            kaps = psb[1][:, 0:128]
            act(qsqS[:, 0:128], kaps, AF.Square, ['b1'], ['qsqS'])
            dve(lambda e: e.tensor_reduce(ssk, qsqS[:, 0:128].rearrange("p (h n) -> p h n", n=64), AX.X, ALU.add), ['qsqS'], ['ssk'])
            act(rk, ssk, AF.Ln, ['ssk', 'epsb'], ['rk'], bias=epsb[:, :], scale=1.0 / 64)
            act(rk, rk, AF.Exp, ['rk'], ['rk'], scale=-0.5)
            for h in range(2):
                dve(lambda e, h=h: e.scalar_tensor_tensor(knfS[:, h * 64:(h + 1) * 64], kaps[:, h * 64:(h + 1) * 64], rk[:, h:h + 1], gkb,
                                                         ALU.mult, ALU.mult), ['b1', 'rk', 'gkb'], ['knS'])
            dve(lambda e: e.tensor_copy(knbS, knfS), ['knS'], ['knS'])
            dve(lambda e: e.tensor_copy(vA3[K3], psb[1][:, 128:256]), ['b1'], ['vA%d' % K3])
            if kv_need:
                dve(lambda e: e.tensor_copy(kvout[:, 0:128], knfS), ['knS'], ['sg1'])
                dve(lambda e: e.tensor_copy(kvout[:, 128:256], psb[1][:, 128:256]), ['b1'], ['sg1'])
            tpk = psb[2][:, :].bitcast(BF16)
            tr(tpk[:, 0:128], knbS, identb[:, :], ['knS', 'identb'], ['b2'])
            act(kaT3[K3], tpk[:, 0:128], AF.Copy, ['b2'], ['kaT%d' % K3])
            yield
        if full:
            proj(1, C_QA, 512)
            qaps = psb[1][:, :]
            act(qsqS, qaps, AF.Square, ['b1'], ['qsqS'])
            dve(lambda e: e.tensor_reduce(ssq, qsqS.rearrange("p (h n) -> p h n", n=64), AX.X, ALU.add), ['qsqS'], ['ssq'])
            act(rq, ssq, AF.Ln, ['ssq', 'epsb'], ['rq'], bias=epsb[:, :], scale=1.0 / 64)
            act(rq, rq, AF.Exp, ['rq'], ['rq'], scale=-0.5)
            for h in range(8):
                qo = (h % 4) * 128 + (h // 4) * 64
                dve(lambda e, h=h, qo=qo: e.scalar_tensor_tensor(qnS[:, qo:qo + 64], qaps[:, h * 64:(h + 1) * 64], rq[:, h:h + 1], gqb,
                                                                ALU.mult, ALU.mult), ['b1', 'rq', 'gqb'], ['qnS'])
            yield
            tpq = psb[2][:, :].bitcast(BF16).rearrange("p (c n) -> p c n", n=128)
            for p_ in range(4):
                tr(tpq[:, p_, :], qnS[:, p_ * 128:(p_ + 1) * 128], identb[:, :], ['qnS', 'identb'], ['b2'])
            act(qaT_s[S], tpq[:, 0:4, :], AF.Copy, ['b2'], ['qaT%d' % S])
            yield

    def stage2m(c, full, gc):
        S = c % 2
        b0 = psb[3]
        sc = SC[S]; sk = 'SC%d' % S
        wend_ = sc[:, 16:20]; winter_ = sc[:, 20:24]; enm_ = sc[:, 24:28]
        decay_ = sc[:, 28:32]; wendb_ = sc[:, 32:34].bitcast(BF16)
        kM_, vaug_, qT_, kT_, sigo_, DT_ = kM_s[S], vaug_s[S], qT_s[S], kT_s[S], sigo_s[S], DT_s[S]
        kMk, vak, qTk, kTk, sgk = 'kM%d' % S, 'vaug%d' % S, 'qT%d' % S, 'kT%d' % S, 'sigo%d' % S
        dk = 'DT0' if S == 0 else 'gainB1'
        dve(lambda e: e.tensor_tensor(vw, vaug_, wend_.unsqueeze(2).to_broadcast([128, 4, 128]), ALU.mult), [vak, sk], ['vw'])
        if full:
            for h in range(4):
                mm(psb[4][:, h * 128:(h + 1) * 128], kT_[:, h, :], qT_[:, h, :], True, True, [kTk, qTk], ['b4'])
            dve(lambda e: e.tensor_tensor(SwT, v4(psb[4][:, :]), DT_, ALU.mult), ['b4', dk], ['SwT'])
            for h in range(4):
                mm(psb[5][:, h * 128:(h + 1) * 128], qT_[:, h, :], CTb[:, h, :], True, True, [qTk, 'CTb'], ['b5'])
            for h in range(4):
                mm(b0[:, 284 + h:285 + h], qT_[:, h, :], nTb[:, h:h + 1], True, True, [qTk, 'nTb'], ['b3'])
            yield
            for h in range(4):
                mm(psb[4][:, h * 128:(h + 1) * 128], SwT[:, h, :], vaug_[:, h, :], True, True, ['SwT', vak], ['b4'])
            for h in range(4):
                mm(b0[:, 280 + h:281 + h], SwT[:, h, :], onesb[:, 0:1], True, True, ['SwT', 'onesb'], ['b3'])
            yield
            act(hnum, v4(psb[4][:, :]), AF.Copy, ['b4'], ['hnum'])
            for h in range(4):
                dve(lambda e, h=h: e.scalar_tensor_tensor(hnum[:, h, :], psb[5][:, h * 128:(h + 1) * 128], winter_[:, h:h + 1],
                                                         hnum[:, h, :], ALU.mult, ALU.add), ['b5', sk, 'hnum'], ['hnum'])
            dve(lambda e: e.tensor_copy(den, b0[:, 280:284]), ['b3'], ['den'])
            dve(lambda e: e.tensor_tensor(t4a, b0[:, 284:288], winter_, ALU.mult), ['b3', sk], ['t4a'])
            yield
        if not full:
            return
        dve(lambda e: e.tensor_tensor(den, den, t4a, ALU.add), ['den', 't4a'], ['den'])
        dve(lambda e: e.tensor_scalar(t4a, den, -1.0, None, ALU.mult), ['den'], ['t4a'])
        dve(lambda e: e.tensor_tensor(den, den, t4a, ALU.max), ['den', 't4a'], ['den'])
        dve(lambda e: e.tensor_tensor(den, den, enm_, ALU.max), ['den', sk], ['den'])
        dve(lambda e: e.reciprocal(den, den), ['den'], ['den'])
        yield
        for h in range(4):
            act(qsq[:, 0:128], hnum[:, h, :], AF.Square, ['hnum', 'den'], ['sg0', 'ssh%d' % h], scale=den[:, h:h + 1], accum=ssh[:, h:h + 1])
        act(t4b, ssh, AF.Ln, ['ssh0', 'ssh1', 'ssh2', 'ssh3', 'epsb'], ['t4b'], bias=epsb[:, :], scale=1.0 / 128)
        act(t4b, t4b, AF.Exp, ['t4b'], ['t4b'], scale=-0.5)
        dve(lambda e: e.tensor_tensor(comb, t4b, den, ALU.mult), ['t4b', 'den'], ['comb'])
        yield
        for h in range(4):
            dve(lambda e, h=h: e.scalar_tensor_tensor(ybf[:, h * 128:(h + 1) * 128], hnum[:, h, :], comb[:, h:h + 1],
                                                     sigo_[:, h * 128:(h + 1) * 128], ALU.mult, ALU.mult), ['hnum', 'comb', sgk], ['ybm'])
        yield
        for h in range(4):
            mm(psb[4][:, h * 128:(h + 1) * 128], kM_[:, h * 128:(h + 1) * 128], vw[:, h, :], True, True, [kMk, 'vw'], ['b4'])
        for h in range(4):
            mm(b0[:, 288 + h:289 + h], kM_[:, h * 128:(h + 1) * 128], wendb_[:, h:h + 1], True, True, [kMk, sk], ['b3'])
        for h in range(4):
            dve(lambda e, h=h: e.scalar_tensor_tensor(CTv[:, h, :], CTv[:, h, :], decay_[:, h:h + 1], psb[4][:, h * 128:(h + 1) * 128],
                                                     ALU.mult, ALU.add), ['CT', sk, 'b4'], ['CT'])
        act(CTb, CTv, AF.Copy, ['CT'], ['CTb'])
        dve(lambda e: e.tensor_tensor(nT, nT, decay_, ALU.mult), ['nT', sk], ['nT'])
        dve(lambda e: e.tensor_tensor(nT, nT, b0[:, 288:292], ALU.add), ['nT', 'b3'], ['nT'])
        dve(lambda e: e.tensor_copy(nTb, nT), ['nT'], ['nTb'])
        yield

    def stage2s(c, first_block, gc):
        S = c % 2
        qaT_ = qaT_s[S]; qaTk = 'qaT%d' % S
        cur, prv = gc % 3, (gc - 1) % 3
        msk = swam0 if first_block else swam
        mkey = 'swam0' if first_block else 'swam'
        msk2 = msk[:, :].unsqueeze(1).to_broadcast([128, 2, 256])
        for half in range(2):
            pr = slice(half * 64, (half + 1) * 64)
            hs_ = slice(4 * half, 4 * half + 4)
            for g_ in range(4):
                bank = psb[6 + g_ // 2]; off = (g_ % 2) * 256; bk = 'b%d' % (6 + g_ // 2)
                mm(bank[:, off:off + 128], qaT_[pr, g_, :], kaT3[prv][pr, :], True, True, [qaTk, 'kaT%d' % prv], [bk])
                mm(bank[:, off + 128:off + 256], qaT_[pr, g_, :], kaT3[cur][pr, :], True, True, [qaTk, 'kaT%d' % cur], [bk])
            for j in range(2):
                dve(lambda e, j=j: e.scalar_tensor_tensor(smv4[:, 2 * j:2 * j + 2, :], psb[6 + j][:, :].rearrange("p (h n) -> p h n", n=256),
                                                         0.125, msk2, ALU.mult, ALU.add), ['b%d' % (6 + j), mkey], ['tmpA'])
            yield
            dve(lambda e, hs_=hs_: e.tensor_reduce(mx[:, hs_], smv4, AX.X, ALU.max, negate=True), ['tmpA'], ['mx'])
            dve(lambda e, hs_=hs_: e.tensor_tensor(negm2[:, hs_], mx[:, hs_], nsnkb[:, hs_], ALU.min), ['mx', 'nsnkb'], ['negm2'])
            yield
            for g_ in range(4):
                hd = 4 * half + g_
                act(Pbv4[:, g_, :], smv4[:, g_, :], AF.Exp, ['tmpA', 'negm2'], ['Pb', 'rsum%d' % hd], bias=negm2[:, hd:hd + 1],
                    accum=rsum[:, hd:hd + 1])
            yield
            tpp = psb[6][:, :].bitcast(BF16).rearrange("p (h s n) -> p h s n", h=4, s=2)
            for g_ in range(4):
                for s_ in range(2):
                    tr(tpp[:, g_, s_, :], Pbv4[:, g_, s_ * 128:(s_ + 1) * 128], identb[:, :], ['Pb', 'identb'], ['b6'])
            act(PTv4, tpp, AF.Copy, ['b6'], ['PT'])
            dve(lambda e, hs_=hs_: e.tensor_tensor(esnk[:, hs_], snkb[:, hs_], negm2[:, hs_], ALU.add), ['snkb', 'negm2'], ['esnk'])
            act(esnk[:, hs_], esnk[:, hs_], AF.Exp, ['esnk'], ['esnk'])
            dve(lambda e, hs_=hs_: e.tensor_tensor(esnk[:, hs_], esnk[:, hs_], rsum[:, hs_], ALU.add),
                ['esnk'] + ['rsum%d' % h for h in range(4 * half, 4 * half + 4)], ['esnk'])
            dve(lambda e, hs_=hs_: e.reciprocal(esnk[:, hs_], esnk[:, hs_]), ['esnk'], ['esnk'])
            yield
            ops_ = psb[3][:, 0:256]
            for g_ in range(4):
                mm(ops_[:, g_ * 64:(g_ + 1) * 64], PTv4[:, g_, 0, :], vA3[prv][:, half * 64:(half + 1) * 64], True, False,
                   ['PT', 'vA%d' % prv], ['b3'])
                mm(ops_[:, g_ * 64:(g_ + 1) * 64], PTv4[:, g_, 1, :], vA3[cur][:, half * 64:(half + 1) * 64], False, True,
                   ['PT', 'vA%d' % cur], ['b3'])
            dve(lambda e, hs_=hs_, half=half, ops_=ops_: e.tensor_tensor(
                ybf[:, 512 + half * 256:512 + (half + 1) * 256].rearrange("p (h n) -> p h n", n=64),
                ops_.rearrange("p (h n) -> p h n", n=64), esnk[:, hs_].unsqueeze(2).to_broadcast([128, 4, 64]), ALU.mult),
                ['b3', 'esnk'], ['yba'])
            yield

    def stage2o(c, xap, xkey):
        tpy = psb[1][:, :].bitcast(BF16).rearrange("p (c n) -> p c n", n=128)
        for fc in range(8):
            tr(tpy[:, fc, :], ybf[:, fc * 128:(fc + 1) * 128], identb[:, :], ['ybm', 'yba', 'identb'], ['b1'])
        act(yTt, tpy, AF.Copy, ['b1'], ['hn1'])
        yield
        for half in range(2):
            for fc in range(8):
                mm(psb[1 + half][:, :], yTt[:, fc, :], woutv[:, fc, half * 512:(half + 1) * 512], fc == 0, fc == 7,
                   ['hn1', 'wout'], ['b%d' % (1 + half)])
        for half in range(2):
            xh = xap[:, half * 512:(half + 1) * 512]
            pb = psb[1 + half][:, :]
            dve(lambda e, xh=xh, pb=pb: e.tensor_tensor(xh, xh, pb, ALU.add), ['b%d' % (1 + half), xkey], [xkey])
        yield

    kMA = [[kM_s[0], kM_s[1]], [mixb[:, 1536:2048], mixb[:, 2048:2560]]]
    kMAk = [['kM0', 'kM1'], ['qT0', 'kT0']]
    vaugA = [[vaug_s[0], vaug_s[1]], [qaT_s[0], qaT_s[1]]]
    vaugAk = [['vaug0', 'vaug1'], ['qaT0', 'qaT1']]
    vwA = [vw, SwT]
    vwAk = ['vw', 'SwT']

    def stage1A(t2, kv_need):
        S = t2 % 2
        b0 = psb[0]
        sc = SC[S]; sk = 'SC%d' % S
        atok = sc[:, 0:8]; mu_ = sc[:, 8:12]; wend_ = sc[:, 12:20]; decay_ = sc[:, 20:24]
        wendb_ = sc[:, 24:28].bitcast(BF16)
        hkk = ['hTr0', 'hTr1']
        for k_, cc in ((0, C_I), (1, C_F)):
            for sub in range(2):
                for dc in range(8):
                    mm(b0[0:4, k_ * 256 + sub * 128:k_ * 256 + (sub + 1) * 128], winv[:, dc, cc:cc + 4], hTr[sub][:, dc, :],
                       dc == 0, dc == 7, ['win_g', hkk[sub]], ['b0'])
        e_, B_, a_, G_, nm_, on_ = [gtsA[:, i, :] for i in range(6)]
        GK = ['gts2_0', 'gts2_1']
        dve(lambda e: e.memset(on_, 1.0), [], GK)
        act(e_, b0[0:4, 256:512], AF.Exp, ['b0', 'car2'], GK, bias=car[:, 2:3], scale=-1.0)
        act(e_, e_, AF.Ln, GK, GK, bias=1.0)
        dve(lambda e: e.tensor_scalar(e_, e_, car2[:, 0:1], None, ALU.mult), GK + ['car4'], GK)
        dve(lambda e: e.tensor_tensor_scan(B_, on_, e_, car[:, 0:1], ALU.mult, ALU.add), GK + ['Bcar'], GK)
        dve(lambda e: e.scalar_tensor_tensor(a_, b0[0:4, 0:256], car[:, 3:4], B_, ALU.add, ALU.subtract), ['b0', 'car3'] + GK, GK)
        dve(lambda e: e.tensor_tensor_scan(G_, a_, a_, car[:, 1:2], ALU.max, ALU.max), GK + ['Gcar'], GK)
        dve(lambda e: e.tensor_copy(car[:, 0:1], B_[:, 255:256]), GK, ['Bcar'])
        dve(lambda e: e.tensor_copy(car[:, 1:2], G_[:, 255:256]), GK, ['Gcar'])
        yield
        for sub in range(2):
            hk = hkk[sub]
            for (bk, c0, dst, dkey, isk) in ((1, C_K, kMA[S][sub], kMAk[S][sub], True), (2, C_V, vaugA[S][sub], vaugAk[S][sub], False)):
                for dc in range(8):
                    mm(psb[bk][:, 0:512], hTr[sub][:, dc, :], winv[:, dc, c0:c0 + 512], dc == 0, dc == 7, [wkey(c0), hk], ['b%d' % bk])
                if isk:
                    act(dst, psb[bk][:, :], AF.Copy, ['b%d' % bk], [dkey], scale=128.0 ** -0.5)
                else:
                    dve(lambda e, dst=dst, bk=bk: e.tensor_copy(dst, v4(psb[bk][:, :])), ['b%d' % bk], [dkey])
            yield
        b3 = psb[3]
        for sub in range(2):
            tr(b3[:, sub * 4:sub * 4 + 4], a_[:, sub * 128:(sub + 1) * 128], identf[0:4, 0:4], GK + ['identf'], ['b3'])
        for h in range(4):
            mm(b3[:, 8 + h:9 + h], selv[:, h, :], G_[:, 255:256], True, True, ['selT'] + GK, ['b3'])
        dve(lambda e: e.tensor_copy(sc[:, 0:12], b3[:, 0:12]), ['b3'], [sk])
        dve(lambda e: e.tensor_tensor(wend_.rearrange("p (s h) -> p s h", h=4), atok.rearrange("p (s h) -> p s h", h=4),
                                      mu_.unsqueeze(1).to_broadcast([128, 2, 4]), ALU.subtract), [sk], [sk])
        act(wend_, wend_, AF.Exp, [sk], [sk])
        dve(lambda e: e.tensor_copy(wendb_, wend_), [sk], [sk])
        dve(lambda e: e.tensor_tensor(decay_, mu_p, mu_, ALU.subtract), ['mu_p', sk], [sk])
        act(decay_, decay_, AF.Exp, [sk], [sk])
        dve(lambda e: e.tensor_copy(mu_p, mu_), [sk], ['mu_p'])
        yield
        if kv_need:
            gc = NCH - 1
            K3 = gc % 3
            sub = 1
            for dc in range(8):
                mm(psb[1][:, 0:256], hTr[sub][:, dc, :], winv[:, dc, C_KA:C_KA + 256], dc == 0, dc == 7, ['win_ka', hkk[sub]], ['b1'])
            kaps = psb[1][:, 0:128]
            act(qsqS[:, 0:128], kaps, AF.Square, ['b1'], ['qsqS'])
            dve(lambda e: e.tensor_reduce(ssk, qsqS[:, 0:128].rearrange("p (h n) -> p h n", n=64), AX.X, ALU.add), ['qsqS'], ['ssk'])
            act(rk, ssk, AF.Ln, ['ssk', 'epsb'], ['rk'], bias=epsb[:, :], scale=1.0 / 64)
            act(rk, rk, AF.Exp, ['rk'], ['rk'], scale=-0.5)
            for h in range(2):
                dve(lambda e, h=h: e.scalar_tensor_tensor(knfS[:, h * 64:(h + 1) * 64], kaps[:, h * 64:(h + 1) * 64], rk[:, h:h + 1], gkb,
                                                         ALU.mult, ALU.mult), ['b1', 'rk', 'gkb'], ['knS'])
            dve(lambda e: e.tensor_copy(knbS, knfS), ['knS'], ['knS'])
            dve(lambda e: e.tensor_copy(vA3[K3], psb[1][:, 128:256]), ['b1'], ['vA%d' % K3])
            dve(lambda e: e.tensor_copy(kvout[:, 0:128], knfS), ['knS'], ['sg1'])
            dve(lambda e: e.tensor_copy(kvout[:, 128:256], psb[1][:, 128:256]), ['b1'], ['sg1'])
            tpk = psb[2][:, :].bitcast(BF16)
            tr(tpk[:, 0:128], knbS, identb[:, :], ['knS', 'identb'], ['b2'])
            act(kaT3[K3], tpk[:, 0:128], AF.Copy, ['b2'], ['kaT%d' % K3])
            yield

    def stage2A(t2):
        S = t2 % 2
        b3 = psb[3]
        sc = SC[S]; sk = 'SC%d' % S
        wend_ = sc[:, 12:20]; decay_ = sc[:, 20:24]; wendb_ = sc[:, 24:28].bitcast(BF16)
        for sub in range(2):
            dve(lambda e, sub=sub: e.tensor_tensor(vwA[sub], vaugA[S][sub], wend_[:, sub * 4:sub * 4 + 4].unsqueeze(2).to_broadcast([128, 4, 128]),
                                                  ALU.mult), [vaugAk[S][sub], sk], [vwAk[sub]])
        yield
        for h in range(4):
            for sub in range(2):
                mm(psb[4][:, h * 128:(h + 1) * 128], kMA[S][sub][:, h * 128:(h + 1) * 128], vwA[sub][:, h, :], sub == 0, sub == 1,
                   [kMAk[S][sub], vwAk[sub]], ['b4'])
        for h in range(4):
            for sub in range(2):
                mm(b3[:, 288 + h:289 + h], kMA[S][sub][:, h * 128:(h + 1) * 128], wendb_[:, sub * 4 + h:sub * 4 + h + 1], sub == 0, sub == 1,
                   [kMAk[S][sub], sk], ['b3'])
        for h in range(4):
            dve(lambda e, h=h: e.scalar_tensor_tensor(CTv[:, h, :], CTv[:, h, :], decay_[:, h:h + 1], psb[4][:, h * 128:(h + 1) * 128],
                                                     ALU.mult, ALU.add), ['CT', sk, 'b4'], ['CT'])
        act(CTb, CTv, AF.Copy, ['CT'], ['CTb'])
        dve(lambda e: e.tensor_tensor(nT, nT, decay_, ALU.mult), ['nT', sk], ['nT'])
        dve(lambda e: e.tensor_tensor(nT, nT, b3[:, 288:292], ALU.add), ['nT', 'b3'], ['nT'])
        dve(lambda e: e.tensor_copy(nTb, nT), ['nT'], ['nTb'])
        yield

    def stage0A(t2):
        for sub in range(2):
            c = 2 * t2 + sub
            for _ in stage0(c, x[:, c, :], 'x%d' % c):
                yield

    def mixer_loop_A():
        pipe_begin()
        NSTEP = NCH // 2
        for t in range(NSTEP + 2):
            early = []
            rest = []
            if 0 <= t - 2 < NSTEP:
                early.append(stage2A(t - 2))
            if 0 <= t - 1 < NSTEP:
                rest.append(stage1A(t - 1, t - 1 == NSTEP - 1))
            if t < NSTEP:
                rest.append(stage0A(t))
            active = rest + early
            while active:
                for g in list(active):
                    try:
                        next(g)
                        if g in rest:
                            for _ in range(REST_STEPS - 1):
                                next(g)
                    except StopIteration:
                        active.remove(g)

    def mixer_loop(full):
        pipe_begin()
        if full:
            for _ in stage0(0, xs[:, :], 'xs', nt=NS, dst=hTs, dkey='hTs'):
                pass

        def rr(gens):
            gens = list(gens)
            while gens:
                for g in list(gens):
                    try:
                        next(g)
                    except StopIteration:
                        gens.remove(g)
        pending = []
        for t in range(NCH + 3):
            early = []
            late = []
            if t - 2 >= 0 and t - 2 < NCH:
                c = t - 2
                gcx = (NCH if full else 0) + c
                early.append(stage2m(c, full, gcx))
                if full:
                    early.append(stage2s(c, c == 0, gcx))
                    late.append(stage2o(c, x[:, c, :], 'x%d' % c))
            rest = []
            if t - 1 >= 0 and t - 1 < NCH:
                c = t - 1
                rest.append(stage1(c, full, c == NCH - 1, (NCH if full else 0) + c))
            if t < NCH:
                rest.append(stage0(t, x[:, t, :], 'x%d' % t))
            active = pending + rest + early
            pending = late
            late = []
            late_added = True
            rnd = 0
            while active or (late and not late_added):
                rnd += 1
                for g in list(active):
                    try:
                        next(g)
                        if g in rest and REST_STEPS > 1:
                            next(g)
                    except StopIteration:
                        active.remove(g)

    def sample_mixer():
        hs = lambda dc: hTs[:, dc, :]
        mixbf = mixb[:, :].bitcast(F32)
        vTs = mixbf[:, 0:64].rearrange("p (h b) -> p h b", b=NS)
        oTs = mixbf[:, 64:128].rearrange("p (h b) -> p h b", b=NS)
        Cq = mixbf[:, 128:192]; rep = mixbf[:, 192:448]; numT = mixbf[:, 448:512]; wv = mixbf[:, 512:576]
        sqT = mixbf[:, 576:640]; gonT = mixbf[:, 640:644]; gon4 = mixbf[0:4, 644:772]
        qx = mixbf[:, 772:836]; kxn = mixbf[:, 836:900]; vxn = mixbf[:, 900:964]; Sx = mixbf[:, 964:1092]
        Px = mixbf[:, 1092:1220]; ox = mixbf[:, 1220:1284]; sst = mixbf[:, 1284:1300]; junk = mixbf[:, 1300:1428]
        sinkx = mixbf[:, 1428:1429]
        yTs = smallb[:, 0:128].rearrange("p (c b) -> p c b", b=NS)
        ystok = mixbf[0:NS, 1432:1432 + 256].bitcast(BF16)
        qk_t = tmpB[0:NS, :]
        pa = tmpA[0:NS, :]
        qa_t = pa[:, 0:512]; ka_t = pa[:, 512:640]; va_t = pa[:, 640:768]; if_t = pa[:, 768:776]
        sc4 = lambda i: pa[:, 776 + 4 * i:780 + 4 * i]
        ip, lfm, mt, wi, we, enm_, qk_, nq_, sc_, den_, t1_, t2_ = [sc4(i) for i in range(12)]
        bib = sc4(12); bfb = sc4(13); m0t = sc4(14); repsrc = pa[:, 840:856]
        n0t = pa[:, 856:856 + 0]
        bigf = big[:, :].bitcast(F32)
        Cst = bigf[:, 0:2048].rearrange("p (j k) -> p j k", k=128)
        qrow = bigf[:, 2048:4096].rearrange("p (j k) -> p j k", k=128)
        krow = bigf[:, 4096:6144].rearrange("p (j k) -> p j k", k=128)
        prod = bigf[:, 6144:8192].rearrange("p (j k) -> p j k", k=128)
        n0s = bigf[0:NS, 8192:8704]; nns = bigf[0:NS, 8704:9216]; tq = bigf[0:NS, 9216:9728]
        qns = bigf[0:NS, 9728:10240]; kns = bigf[0:NS, 10240:10368]
        Kx = hT[:, 0:16384].bitcast(F32).rearrange("p (s d) -> p s d", d=64)
        HK = HKALL + PIPEKEYS
        def sproj(bk, c0, n):
            for dc in range(8):
                mm(psb[bk][0:NS, 0:n], hs(dc), winv[:, dc, c0:c0 + n], dc == 0, dc == 7, WINKEYS + ['hTs'], ['b%d' % bk])
        sproj(1, C_Q, 512)
        dve(lambda e: e.tensor_copy(qk_t[:, 0:512], psb[1][0:NS, :]), ['b1', 'hnum', 'sigo0'], ['hnum', 'sigo0'])
        sproj(2, C_K, 512)
        dve(lambda e: e.tensor_copy(qk_t[:, 512:1024], psb[2][0:NS, :]), ['b2'], ['hnum', 'sigo0'])
        sproj(3, C_QA, 512)
        dve(lambda e: e.tensor_copy(qa_t, psb[3][0:NS, :]), ['b3', 'tmpA'], ['tmpA'])
        sproj(4, C_KA, 256)
        dve(lambda e: e.tensor_copy(pa[:, 512:768], psb[4][0:NS, 0:256]), ['b4'], ['tmpA'])
        sproj(5, C_I, 8)
        dve(lambda e: e.tensor_copy(if_t, psb[5][0:NS, 0:8]), ['b5'], ['tmpA'])
        for k_, cc, dst in ((0, C_V, vTs), (1, C_O, oTs)):
            for h in range(4):
                for dc in range(8):
                    mm(psb[6 + k_][:, h * NS:(h + 1) * NS], winv[:, dc, cc + h * 128:cc + (h + 1) * 128], hs(dc), dc == 0, dc == 7,
                       WINKEYS + ['hTs'], ['b%d' % (6 + k_)])
        dve(lambda e: e.tensor_copy(mixbf[:, 0:64], psb[6][:, 0:64]), ['b6', 'kM0', 'vaug0', 'vw', 'qT0', 'kT0', 'SwT'], ['mixb'])
        act(mixbf[:, 64:128], psb[7][:, 0:64], AF.Exp, ['b7'], ['mixb'], scale=-1.0)
        act(mixbf[:, 64:128], mixbf[:, 64:128], AF.Ln, ['mixb'], ['mixb'], bias=1.0)
        act(mixbf[:, 64:128], mixbf[:, 64:128], AF.Exp, ['mixb'], ['mixb'], scale=-1.0)
        dma('sp', bib, bi_d.to_broadcast([NS, 4]), ['tmpA'], ['tmpA'], 'G_sin')
        dma('sp', bfb, bf_d.to_broadcast([NS, 4]), ['tmpA'], ['tmpA'], 'G_sin')
        dma('sp', m0t, sm0_d, ['tmpA'], ['tmpA'], 'G_sin')
        dma('sp', n0s, sn0_d, [], WINKEYS + ACTKEYS + ['bigs'], 'sio')
        T = ['tmpA']
        dve(lambda e: e.tensor_tensor(ip, if_t[:, 0:4], bib, ALU.add), T, T)
        dve(lambda e: e.tensor_tensor(t1_, if_t[:, 4:8], bfb, ALU.add), T, T)
        act(t1_, t1_, AF.Exp, T, T, scale=-1.0)
        act(t1_, t1_, AF.Ln, T, T, bias=1.0)
        dve(lambda e: e.tensor_tensor(lfm, m0t, t1_, ALU.subtract), T, T)
        dve(lambda e: e.tensor_tensor(mt, lfm, ip, ALU.max), T, T)
        dve(lambda e: e.tensor_tensor(wi, lfm, mt, ALU.subtract), T, T)
        act(wi, wi, AF.Exp, T, T)
        dve(lambda e: e.tensor_tensor(we, ip, mt, ALU.subtract), T, T)
        act(we, we, AF.Exp, T, T)
        act(enm_, mt, AF.Exp, T, T, scale=-1.0)
        dve(lambda e: e.tensor_tensor(tq, qk_t[:, 0:512], qk_t[:, 512:1024], ALU.mult), ['hnum', 'sigo0', 'bigs'], ['bigs'])
        dve(lambda e: e.tensor_reduce(qk_, tq.rearrange("p (h k) -> p h k", k=128), AX.X, ALU.add), ['bigs'] + T, T)
        dve(lambda e: e.tensor_tensor(tq, qk_t[:, 0:512], n0s, ALU.mult), ['hnum', 'sigo0', 'bigs'] + T, ['bigs'])
        dve(lambda e: e.tensor_reduce(nq_, tq.rearrange("p (h k) -> p h k", k=128), AX.X, ALU.add), ['bigs'] + T, T)
        dve(lambda e: e.tensor_scalar(t2_, we, 128.0 ** -0.5, None, ALU.mult), T, T)
        dve(lambda e: e.tensor_tensor(sc_, qk_, t2_, ALU.mult), T, T)
        dve(lambda e: e.tensor_tensor(den_, wi, nq_, ALU.mult), T, T)
        dve(lambda e: e.tensor_tensor(den_, den_, sc_, ALU.add), T, T)
        dve(lambda e: e.tensor_scalar(t1_, den_, -1.0, None, ALU.mult), T, T)
        dve(lambda e: e.tensor_tensor(den_, den_, t1_, ALU.max), T, T)
        dve(lambda e: e.tensor_tensor(den_, den_, enm_, ALU.max), T, T)
        dve(lambda e: e.reciprocal(den_, den_), T, T)
        for h in range(4):
            dve(lambda e, h=h: e.tensor_scalar(nns[:, h * 128:(h + 1) * 128], qk_t[:, 512 + h * 128:512 + (h + 1) * 128], t2_[:, h:h + 1], None, ALU.mult),
                T + ['hnum', 'sigo0', 'bigs'], ['bigs'])
            dve(lambda e, h=h: e.scalar_tensor_tensor(nns[:, h * 128:(h + 1) * 128], n0s[:, h * 128:(h + 1) * 128], wi[:, h:h + 1],
                                                     nns[:, h * 128:(h + 1) * 128], ALU.mult, ALU.add), T + ['bigs'], ['bigs'])
        dma('sp', sn_d, nns, ['bigs'], [], 'sout')
        dma('sp', sm_d, mt, T, [], 'sout')
        for i_, src in enumerate((wi, t2_, sc_, den_)):
            dve(lambda e, i_=i_, src=src: e.tensor_copy(repsrc[:, 4 * i_:4 * i_ + 4], src), T, T)
        scr2 = scr_d[2]
        dma('sp', scr2[:, 0:16], repsrc, T, ['scr2'], 'sio')
        dma('sp', rep.rearrange("p (b s) -> p b s", s=16), scr2[:, 0:16].rearrange("(o b) s -> o b s", o=1).to_broadcast([128, NS, 16]),
            ['scr2', 'mixb'], ['mixb'], 'sio')
        rep3 = rep.rearrange("p (b s) -> p b s", s=16)
        R = lambda s: rep3[:, :, 4 * s:4 * s + 4]
        vT_bh = vTs.rearrange("p h b -> p b h"); oT_bh = oTs.rearrange("p h b -> p b h")
        Cq3 = Cq.rearrange("p (b h) -> p b h", h=4); wv3 = wv.rearrange("p (b h) -> p b h", h=4)
        num3 = numT.rearrange("p (b h) -> p b h", h=4); sq3 = sqT.rearrange("p (b h) -> p b h", h=4)
        M = ['mixb']
        dve(lambda e: e.tensor_tensor(wv3, R(1), vT_bh, ALU.mult), M, M)
        def gen_ml():
            CstS = [bigf[:, s_ * 1024:(s_ + 1) * 1024].rearrange("p (j k) -> p j k", k=128) for s_ in range(2)]
            prod2 = bigf[:, 2048:3072].rearrange("p (j k) -> p j k", k=128)
            QmS = [bigf[0:NS, 3072 + s_ * 512:3584 + s_ * 512].bitcast(BF16).rearrange("p (l f) -> p l f", f=512) for s_ in range(2)]
            KmS = [bigf[0:NS, 4096 + s_ * 512:4608 + s_ * 512].bitcast(BF16).rearrange("p (l f) -> p l f", f=512) for s_ in range(2)]
            qkb = bigf[0:NS, 5120:5632].bitcast(BF16)
            dve(lambda e: e.tensor_copy(qkb, qk_t), ['hnum', 'sigo0', 'bigs'], ['qkb'])
            for bt in range(8):
                b0_ = bt * 2
                s_ = bt % 2
                ck = 'Cst%d' % s_
                dma('sp', CstS[s_], sC0_d[b0_:b0_ + 2].rearrange("b h v k -> v (b h) k"), ['bigs'], [ck], ck)
                selc = identf[0:NS, b0_:b0_ + 2].unsqueeze(2).to_broadcast([NS, 2, 512])
                dve(lambda e, s_=s_, selc=selc: e.tensor_tensor(QmS[s_], qkb[:, 0:512].unsqueeze(1).to_broadcast([NS, 2, 512]), selc, ALU.mult),
                    ['qkb', 'identf'], ['Qm%d' % s_])
                dve(lambda e, s_=s_, selc=selc: e.tensor_tensor(KmS[s_], qkb[:, 512:1024].unsqueeze(1).to_broadcast([NS, 2, 512]), selc, ALU.mult),
                    ['qkb', 'identf'], ['Km%d' % s_])
                for l_ in range(2):
                    mm(psb[l_][:, :], onesb[0:NS, :], QmS[s_][:, l_, :], True, True, ['onesb', 'Qm%d' % s_], ['b%d' % l_])
                for l_ in range(2):
                    mm(psb[2 + l_][:, :], onesb[0:NS, :], KmS[s_][:, l_, :], True, True, ['onesb', 'Km%d' % s_], ['b%d' % (2 + l_)])
                for l_ in range(2):
                    dve(lambda e, s_=s_, l_=l_: e.tensor_tensor(prod2[:, 4 * l_:4 * l_ + 4, :], CstS[s_][:, 4 * l_:4 * l_ + 4, :],
                                                               psb[l_][:, :].rearrange("p (j k) -> p j k", k=128), ALU.mult),
                        [ck, 'b%d' % l_], ['prod'])
                dve(lambda e, bt=bt: e.tensor_reduce(Cq[:, bt * 8:(bt + 1) * 8], prod2, AX.X, ALU.add), ['prod'] + M, M)
                for j in range(8):
                    jj = bt * 8 + j
                    kr = psb[2 + j // 4][:, (j % 4) * 128:(j % 4 + 1) * 128]
                    dve(lambda e, j=j, jj=jj, s_=s_: e.tensor_scalar(CstS[s_][:, j, :], CstS[s_][:, j, :], rep3[:, jj // 4, jj % 4:jj % 4 + 1], None,
                                                                     ALU.mult), [ck] + M, [ck])
                    dve(lambda e, j=j, jj=jj, s_=s_, kr=kr: e.scalar_tensor_tensor(CstS[s_][:, j, :], kr, wv[:, jj:jj + 1], CstS[s_][:, j, :],
                                                                                   ALU.mult, ALU.add), [ck, 'b%d' % (2 + j // 4)] + M, [ck])
                dma('pool', sC_d[b0_:b0_ + 2].rearrange("b h v k -> v (b h) k"), CstS[s_], [ck], [], 'cout%d' % s_)
                yield
            dve(lambda e: e.tensor_tensor(num3, R(0), Cq3, ALU.mult), M, M)
            dve(lambda e: e.tensor_tensor(sq3, R(2), vT_bh, ALU.mult), M, M)
            dve(lambda e: e.tensor_tensor(numT, numT, sqT, ALU.add), M, M)
            dve(lambda e: e.tensor_tensor(num3, num3, R(3), ALU.mult), M, M)
            dve(lambda e: e.tensor_tensor(sqT, numT, numT, ALU.mult), M, M)
            yield
            mm(psb[1][:, 0:64], onesf[:, :], sqT, True, True, ['onesf'] + M, ['b1'])
            act(sqT, psb[1][:, 0:64], AF.Ln, ['b1', 'epsb'], M, bias=epsb[:, :], scale=1.0 / 128)
            act(sqT, sqT, AF.Exp, M, M, scale=-0.5)
            dve(lambda e: e.tensor_tensor(numT, numT, sqT, ALU.mult), M, M)
            dve(lambda e: e.tensor_tensor(num3, num3, oT_bh, ALU.mult), M, M)
            dma('sp', gon4, gon_d.rearrange("o (h v) -> (o h) v", h=4), M, M, 'sio')
            tr(psb[2][:, 0:4], gon4, identf[0:4, 0:4], M + ['identf'], ['b2'])
            dve(lambda e: e.tensor_copy(gonT, psb[2][:, 0:4]), ['b2'], M)
            for h in range(4):
                dve(lambda e, h=h: e.tensor_scalar(yTs[:, h, :], num3[:, :, h], gonT[:, h:h + 1], None, ALU.mult), M + ['CTb'], ['yTs'])
            yield

        def gen_sw():
            for nm_, src, n_, gk_, dst in (('q', qa_t, 8, gqb, qns), ('k', ka_t, 2, gkb, kns)):
                w_ = n_ * 64
                dve(lambda e, src=src, w_=w_: e.tensor_tensor(tq[:, 0:w_], src, src, ALU.mult), T + ['qns'], ['qns'])
                dve(lambda e, n_=n_, w_=w_: e.tensor_reduce(sst[0:NS, 0:n_], tq[:, 0:w_].rearrange("p (h n) -> p h n", n=64), AX.X, ALU.add),
                    ['qns'] + M, M)
                act(sst[0:NS, 0:n_], sst[0:NS, 0:n_], AF.Ln, M + ['epsb'], M, bias=epsb[0:NS, :], scale=1.0 / 64)
                act(sst[0:NS, 0:n_], sst[0:NS, 0:n_], AF.Exp, M, M, scale=-0.5)
                for h in range(n_):
                    dve(lambda e, h=h, src=src, dst=dst, gk_=gk_: e.scalar_tensor_tensor(
                        dst[:, h * 64:(h + 1) * 64], src[:, h * 64:(h + 1) * 64], sst[0:NS, h:h + 1], gk_[0:NS, :], ALU.mult, ALU.mult),
                        T + M + ['gqb', 'gkb', 'qns'], ['qns'])
            yield
            W = ['SW']
            R_ = hT[:, :]
            K_all = R_[:, 0:4096].bitcast(F32).rearrange("p (b f) -> p b f", f=128)
            V_all = R_[:, 4096:8192].bitcast(F32).rearrange("p (b f) -> p b f", f=128)
            KT = R_[:, 8192:10240].rearrange("p (b s) -> p b s", s=128)
            Vdup = R_[:, 10240:14336].rearrange("p (b k r d) -> p b k r d", k=2, r=2, d=64)
            PTb = R_[:, 14336:14464]; qsT = R_[:, 14464:14528].rearrange("p (g b) -> p g b", b=NS)
            STs = R_[:, 14528:14784].bitcast(F32); Pnb = R_[:, 14784:14912]; psel = R_[:, 14912:14944].bitcast(F32)
            qpm = bigf[0:NS, 10368:10624].bitcast(BF16)
            tq2 = bigf[0:NS, 10624:11136]; snw = bigf[0:NS, 11136:11144]; Ls = bigf[0:NS, 11144:11272]
            Vnd = bigf[0:NS, 11272:11400].bitcast(BF16).rearrange("p (k r d) -> p k r d", k=2, r=2)
            PnT = bigf[0:NS, 11400:11464].bitcast(BF16)
            snew = sst[:, 8:9]; mxx = sst[:, 9:10]; nmx = sst[:, 10:11]; rs_ = sst[:, 11:12]; pn_ = sst[:, 12:13]; es_ = sst[:, 13:14]
            dma('sp', sk_d[:, 0:127, :], cK_d[:, 1:128, :], [], [], 'sout')
            dma('sp', sv_d[:, 0:127, :], cV_d[:, 1:128, :], [], [], 'sout')
            dma('sp', sk_d[:, 127, :], kns, ['qns'], [], 'sout')
            dma('sp', sv_d[:, 127, :], va_t, T, [], 'sout')
            for b in range(NS):
                dma('sp', sinkx[b * 8:(b + 1) * 8, :], snk_d.rearrange("o h -> h o"), [], W, 'G_snk')
            dma('sp', K_all, cK_d.rearrange("b s f -> s b f"), [], HK + ['Kall'], 'kall')
            dma('sp', V_all, cV_d.rearrange("b s f -> s b f"), [], HK + ['Vall'], 'vall')
            dve(lambda e: e.tensor_copy(qpm.rearrange("p (g k d) -> p g k d", g=4, k=2), qns.rearrange("p (k g d) -> p g k d", k=2, g=4)),
                ['qns'], ['qpm'])
            tq_ps = psb[6][:, :].bitcast(BF16)
            for g_ in range(4):
                tr(tq_ps[:, g_ * NS:(g_ + 1) * NS], qpm[:, g_ * 128:(g_ + 1) * 128], identb[0:NS, 0:NS], ['qpm', 'identb'], ['b6'])
            act(qsT, tq_ps[:, 0:64].rearrange("p (g b) -> p g b", b=NS), AF.Copy, ['b6'] + HK, ['qsT'])
            dve(lambda e: e.tensor_tensor(tq2.rearrange("p (k g d) -> p k g d", k=2, g=4), qns.rearrange("p (k g d) -> p k g d", k=2, g=4),
                                          kns.rearrange("p (k d) -> p k d", d=64).unsqueeze(2).to_broadcast([NS, 2, 4, 64]), ALU.mult),
                ['qns'], ['tq2'])
            dve(lambda e: e.tensor_reduce(snw, tq2.rearrange("p (h d) -> p h d", d=64), AX.X, ALU.add), ['tq2'], ['snw'])
            dve(lambda e: e.tensor_tensor(Ls.rearrange("p (b h) -> p b h", h=8), snw.unsqueeze(1).to_broadcast([NS, NS, 8]),
                                          identf[0:NS, 0:NS].unsqueeze(2).to_broadcast([NS, NS, 8]), ALU.mult), ['snw', 'identf'], ['Ls'])
            mm(psb[6][:, 256:257], Ls, onesf[0:NS, 0:1], True, True, ['Ls', 'onesf'], ['b6'])
            dve(lambda e: e.tensor_copy(snew, psb[6][:, 256:257]), ['b6'], W)
            for r_ in range(2):
                dve(lambda e, r_=r_: e.tensor_copy(Vnd[:, :, r_, :], va_t.rearrange("p (k d) -> p k d", d=64)), T, ['Vnd'])
            yield
            for quad in range(4):
                bk = 4 + quad % 2
                for j in range(4):
                    tr(psb[bk][:, j * 128:(j + 1) * 128], K_all[:, quad * 4 + j, :], identf[:, :], ['Kall', 'identf'], ['b%d' % bk])
                act(KT[:, quad * 4:(quad + 1) * 4, :], psb[bk][:, :].rearrange("p (j s) -> p j s", s=128), AF.Copy, ['b%d' % bk] + HK, ['KT'])
            for r_ in range(2):
                dve(lambda e, r_=r_: e.tensor_copy(Vdup[:, :, :, r_, :], V_all.rearrange("p b (k d) -> p b k d", d=64)), ['Vall'] + HK, ['Vdup'])
            yield
            for kv in range(2):
                pr = slice(kv * 64, (kv + 1) * 64)
                for b in range(NS):
                    mm(psb[4 + kv][:, b * 4:b * 4 + 4], KT[pr, b, :], qsT[pr, :, b], True, True, ['KT', 'qsT'], ['b%d' % (4 + kv)])
            STs4 = STs.rearrange("p (b k g) -> p b k g", k=2, g=4)
            for kv in range(2):
                dve(lambda e, kv=kv: e.tensor_copy(STs4[:, :, kv, :], psb[4 + kv][:, 0:64].rearrange("p (b g) -> p b g", g=4)),
                    ['b%d' % (4 + kv)] + HK, ['STs'])
            yield
            S_ps = psb[6][:, 0:128]
            tr(S_ps, STs, identf[:, :], ['STs', 'identf'], ['b6'])
            dve(lambda e: e.tensor_reduce(mxx, S_ps, AX.X, ALU.max), ['b6'] + W, W)
            dve(lambda e: e.tensor_tensor(mxx, mxx, snew, ALU.max), W, W)
            dve(lambda e: e.tensor_scalar(mxx, mxx, 0.125, None, ALU.mult), W, W)
            dve(lambda e: e.tensor_tensor(mxx, mxx, sinkx, ALU.max), W, W)
            dve(lambda e: e.tensor_scalar(nmx, mxx, -1.0, None, ALU.mult), W, W)
            act(Px, S_ps, AF.Exp, ['b6'] + W, W, bias=nmx, scale=0.125, accum=rs_)
            act(pn_, snew, AF.Exp, W, W, bias=nmx, scale=0.125)
            act(es_, sinkx, AF.Exp, W, W, bias=nmx)
            dve(lambda e: e.tensor_tensor(rs_, rs_, pn_, ALU.add), W, W)
            dve(lambda e: e.tensor_tensor(rs_, rs_, es_, ALU.add), W, W)
            dve(lambda e: e.reciprocal(rs_, rs_), W, W)
            dve(lambda e: e.tensor_scalar(Pnb, Px, rs_, None, ALU.mult), W + HK, ['Pnb'])
            dve(lambda e: e.tensor_tensor(pn_, pn_, rs_, ALU.mult), W, W)
            dve(lambda e: e.tensor_scalar(psel, esel[:, :], pn_, None, ALU.mult), W + ['esel'] + HK, ['psel'])
            yield
            tp_ps = psb[6][:, 128:192].bitcast(BF16)
            tr(tp_ps, Pnb, identb[:, :], ['Pnb', 'identb'], ['b6'])
            act(PTb, tp_ps, AF.Copy, ['b6'] + HK, ['PTb'])
            tr(psb[7][0:NS, 0:128], psel, identf[:, :], ['psel', 'identf'], ['b7'])
            dve(lambda e: e.tensor_copy(PnT, psb[7][0:NS, 0:128]), ['b7'], ['PnT'])
            yield
            OT_ps = psb[4][:, 256:384]
            for b in range(NS):
                for kv in range(2):
                    c4 = b * 8 + kv * 4
                    mm(OT_ps[:, c4:c4 + 4], Vdup[:, b, kv, :, :].rearrange("p r d -> p (r d)"), PTb[:, c4:c4 + 4], True, False,
                       ['Vdup', 'PTb'], ['b4'])
                    mm(OT_ps[:, c4:c4 + 4], Vnd[:, kv, :, :].rearrange("p r d -> p (r d)"), PnT[:, c4:c4 + 4], False, True,
                       ['Vnd', 'PnT'], ['b4'])
            OTv = OT_ps.rearrange("p (b j r) -> p r j b", j=4, r=2)
            for r_ in range(2):
                pr = slice(r_ * 64, (r_ + 1) * 64)
                act(yTs[pr, 4:8, :], OTv[pr, r_, :, :], AF.Copy, ['b4'], ['yTs'])
            yield

        gens = [gen_ml(), gen_sw()]
        while gens:
            for g in list(gens):
                try:
                    next(g)
                except StopIteration:
                    gens.remove(g)
        for half in range(2):
            for fc in range(8):
                mm(psb[6 + half][0:NS, :], yTs[:, fc, :], woutv[:, fc, half * 512:(half + 1) * 512], fc == 0, fc == 7,
                   ['yTs', 'wout'], ['b%d' % (6 + half)])
        for half in range(2):
            xh = xs[:, half * 512:(half + 1) * 512]
            pb = psb[6 + half][0:NS, :]
            dve(lambda e, xh=xh, pb=pb: e.tensor_tensor(xh, xh, pb, ALU.add), ['b%d' % (6 + half), 'xs'], ['xs'])
        dve(lambda e: e.memset(rowst[0:NS, 0:1], 0.0), [], ['bigs', 'Cst', 'Cst0', 'Cst1', 'Qm0', 'Qm1', 'Km0', 'Km1', 'qkb', 'qrow', 'krow', 'prod', 'Kx', 'mixb', 'yTs', 'SW', 'qns', 'Kall', 'Vall', 'KT', 'Vdup', 'PTb', 'qsT', 'STs', 'Pnb', 'psel', 'qpm', 'tq2', 'snw', 'Ls', 'Vnd', 'PnT'] + ACTKEYS + HK + ['tmpA'])

    def mixer_done():
        pipe_end()
        dve(lambda e: e.memset(rowst[:, 0:1], 0.0), [],
            WINKEYS + ['wout', 'gainA', 'gainB', 'hn0', 'ybm', 'yba', 'hn1', 'sg0', 'sg1'] + ACTKEYS + WGKEYS + ['gain', 'tmpA'])

    consts_mixer()
    ybv = yb_d.rearrange("(c p) d -> p c d", p=128)
    for ph in range(2):
        full = ph == 1
        xv = (xb_d if full else xa_d).rearrange("(c p) d -> p c d", p=128)
        for c in range(NCH):
            dma('sp', x[:, c, :], xv[:, c, :], [], ['x%d' % c], 'xl%d' % c)
        subt = [(x[:, c, :], 128, c * 128, 'x%d' % c, 'hT%d' % c) for c in range(NCH)]
        if full:
            dma('sp', xs[:, :], xs_d, [], ['xs'], 'xl16')
            subt.append((xs[:, :], NS, 2048, 'xs', 'hT16'))
        ffn('f1', subt, n1_d, wg1_d, wu1_d, wd1_d)
        mixer_setup(not full)
        if full or not PHASE_A_256:
            mixer_loop(full)
        else:
            mixer_loop_A()
        if not full:
            for h in range(4):
                mm(psb[3][:, 300 + h:301 + h], selv[:, h, :], car[:, 0:1], True, True, ['selT', 'Bcar'], ['b3'])
            dve(lambda e: e.tensor_tensor(mu_p, mu_p, psb[3][:, 300:304], ALU.add), ['mu_p', 'b3'], ['mu_p'])
            dve(lambda e: e.tensor_tensor(car[:, 1:2], car[:, 1:2], car[:, 0:1], ALU.add), ['Gcar', 'Bcar'], ['Gcar'])
            dve(lambda e: e.memset(car[:, 0:1], 0.0), ['b3'], ['Bcar'])
            dma('sp', car[:, 3:4], bi_d.rearrange("o h -> h o"), [], ['car3'], 'sio')
            dve(lambda e: e.memset(car2[:, 0:1], -1.0), [], ['car4'])
        else:
            dma('sp', pk_d, kvout[:, 0:128], ['sg1'], [], 'pout')
            dma('sp', pv_d, kvout[:, 128:256], ['sg1'], [], 'pout')
            for h in range(4):
                tr(psb[1][:, h * 128:(h + 1) * 128], CTv[:, h, :], identf[:, :], ['CT', 'identf'], ['b1'])
            dve(lambda e: e.tensor_copy(hnum, psb[1][:, :].rearrange("p (h n) -> p h n", n=128)), ['b1'], ['hnum'])
            dma('sp', pC_d.rearrange("h v k -> v h k"), hnum, ['hnum'], [], 'pout')
            tr(psb[0][0:4, 0:128], nT, identf[:, :], ['nT', 'identf'], ['b0'])
            dve(lambda e: e.tensor_copy(gts[:, 0, :], psb[0][0:4, 0:128]), ['b0'], ['tmpA'])
            dma('sp', pn_d, gts[:, 0, :], ['tmpA'], [], 'pout')
            dve(lambda e: e.tensor_tensor(car2[:, 1:2], car[:, 1:2], car[:, 0:1], ALU.add), ['Gcar', 'Bcar'], ['mfin'])
            dma('sp', pm_d.rearrange("o h -> h o"), car2[:, 1:2], ['mfin'], [], 'pout')
            sample_mixer()
        mixer_done()
    ffn('f2', subt, n2_d, wg2_d, wu2_d, wd2_d)
    for c in range(NCH):
        dma('sp', ybv[:, c, :], x[:, c, :], ['x%d' % c], [], 'yout')
    dma('sp', ys_d, xs[:, :], ['xs'], [], 'yout')
    P.add('sp', lambda e: e.nop(), r=[], w=['x%d' % c for c in range(NCH)] + ['xs', 'sg1', 'hnum', 'tmpA', 'mfin', 'hn0', 'bigs', 'Cst0', 'Cst1', 'sigo'])

    P.emit(nc, es)
    print("sbuf bytes remaining", nc.sbuf_bytes_remaining, "ops", len(P.ops))
    return nc, es


def make_in_maps(inp):
    f = lambda a: np.ascontiguousarray(np.asarray(a, dtype=np.float32))
    xp = f(inp['x_prompt']); xsm = f(inp['x_sample'])[:, 0, :]
    cK = f(inp['cache_swa_k'])[0].reshape(128, 128, 128); cV = f(inp['cache_swa_v'])[0].reshape(128, 128, 128)
    sC = f(inp['state_mlstm_C'])[0]; sn = f(inp['state_mlstm_n'])[0].reshape(128, 512); sm = f(inp['state_mlstm_m'])[0]
    ident = np.eye(128, dtype=np.float32)
    s_ = np.arange(128)[:, None]; t_ = np.arange(128)[None, :]
    maskbig = np.where(s_ > t_, BIG, 0.0).astype(np.float32)
    swam = np.concatenate([np.where(t_ >= s_, 0.0, -BIG), np.where(t_ <= s_, 0.0, -BIG)], axis=1).astype(np.float32)
    swam_first = swam.copy(); swam_first[:, :128] = -BIG
    sel = np.zeros((4, 4, 128), np.float32)
    for h in range(4):
        sel[h, h, :] = 1.0
    sel = sel.reshape(4, 512)
    shared = dict(
        n1=f(inp['ffn1_norm']), wg1=f(inp['ffn1_w_gate'])[0], wu1=f(inp['ffn1_w_up'])[0], wd1=f(inp['ffn1_w_down'])[0],
        nm=f(inp['mix_norm']), win=f(inp['w_in'])[0], bi=f(inp['mlstm_b_i']), bf=f(inp['mlstm_b_f']),
        gon=f(inp['mlstm_out_norm']), gq=f(inp['swa_q_norm']), gk=f(inp['swa_k_norm']), snk=f(inp['swa_sinks']),
        wout=f(inp['w_out'])[0], n2=f(inp['ffn2_norm']), wg2=f(inp['ffn2_w_gate'])[0], wu2=f(inp['ffn2_w_up'])[0],
        wd2=f(inp['ffn2_w_down'])[0], ident=ident, maskbig=maskbig, swam=swam, sel=sel,
        esel=np.repeat(np.eye(16, dtype=np.float32), 8, axis=0))
    maps = []
    for core in range(8):
        b, g = core // 2, core % 2
        m = dict(shared)
        m['xa'] = np.ascontiguousarray(xp[b, 0:TOK]); m['xb'] = np.ascontiguousarray(xp[b, g * TOK:(g + 1) * TOK])
        sl = slice(core * NS, (core + 1) * NS)
        m['xs'] = np.ascontiguousarray(xsm[sl]); m['cK'] = np.ascontiguousarray(cK[sl]); m['cV'] = np.ascontiguousarray(cV[sl])
        m['sC0'] = np.ascontiguousarray(sC[sl]); m['sn0'] = np.ascontiguousarray(sn[sl]); m['sm0'] = np.ascontiguousarray(sm[sl])
        m['swam0'] = swam if g == 1 else swam_first
        am = np.zeros((128, 2), np.float32)
        am[:, 0] = 0.0 if g == 1 else -BIG
        am[:, 1] = 1.0 if g == 1 else 0.0
        m['amask'] = am
        maps.append(m)
    return maps


_NC_CACHE = {}


def kernel(**inputs):
    maps = make_in_maps(inputs)
    if 'nc' not in _NC_CACHE:
        _NC_CACHE['nc'] = build_nc()
    nc, es = _NC_CACHE['nc']
    res = run_bass_kernel_spmd(nc, maps, core_ids=list(range(8)))
    R = res.results
    yp = np.zeros((4, 4096, D), np.float32)
    for core in range(8):
        b, g = core // 2, core % 2
        yp[b, g * TOK:(g + 1) * TOK] = R[core]['yb']
    ysm = np.concatenate([R[c]['ys'] for c in range(8)], 0).reshape(128, 1, D)
    last = [R[2 * b + 1] for b in range(4)]
    pk = np.stack([r['pk'] for r in last]).reshape(1, 4, 128, 2, 64)
    pv = np.stack([r['pv'] for r in last]).reshape(1, 4, 128, 2, 64)
    pC = np.stack([r['pC'] for r in last]).reshape(1, 4, 4, 128, 128)
    pn = np.stack([r['pn'] for r in last]).reshape(1, 4, 4, 128)
    pm = np.stack([r['pm'] for r in last]).reshape(1, 4, 4)
    cat = lambda k: np.concatenate([R[c][k] for c in range(8)], 0)
    sk = cat('sk').reshape(1, 128, 128, 2, 64); sv = cat('sv').reshape(1, 128, 128, 2, 64)
    sC = cat('sC').reshape(1, 128, 4, 128, 128); sn = cat('sn').reshape(1, 128, 4, 128); sm = cat('sm').reshape(1, 128, 4)
    return tuple(np.ascontiguousarray(a, dtype=np.float32) for a in (yp, ysm, pk, pv, pC, pn, pm, sk, sv, sC, sn, sm))
```

```python
from contextlib import ExitStack
import numpy as np
import concourse.bass as bass
import concourse.mybir as mybir
from concourse.bass_utils import run_bass_kernel_spmd

F32 = mybir.dt.float32
BF16 = mybir.dt.bfloat16
AF = mybir.ActivationFunctionType
ALU = mybir.AluOpType
AX = mybir.AxisListType

D = 1024
DFF = 2816
NF = 22
PW = 2824
TOK = 2048
NCH = 16
NS = 16
REST_STEPS = 3
PHASE_A_256 = True
EPS = 1e-6
BIG = 1e30
C_Q, C_K, C_V, C_O, C_I, C_F, C_QA, C_KA, C_VA = 0, 512, 1024, 1536, 2048, 2052, 2056, 2568, 2696


class Prog:
    def __init__(self):
        self.ops = []

    def add(self, eng, fn, r=(), w=(), dma=None):
        r = list(r); w = list(w)
        ps = [k for k in r if len(k) == 2 and k[0] == 'b' and k[1].isdigit()]
        r = [k for k in r if k not in ps]
        w = w + [k for k in ps if k not in w]
        self.ops.append(dict(eng=eng, fn=fn, r=r, w=w, dma=dma))

    def analyze(self):
        last_w = {}
        readers = {}
        for i, op in enumerate(self.ops):
            deps = set()
            for k in op['r']:
                if k in last_w:
                    deps.add(last_w[k])
            for k in op['w']:
                if k in last_w:
                    deps.add(last_w[k])
                rd = readers.get(k)
                if rd:
                    deps.update(rd['c'].values())
                    deps.update(rd['d'])
            deps.discard(i)
            if op['dma'] and op['dma'].startswith('G_'):
                deps = {j for j in deps if self.ops[j]['dma'] != op['dma']}
            op['deps'] = deps
            for k in op['r']:
                rd = readers.setdefault(k, {'c': {}, 'd': []})
                if op['dma']:
                    rd['d'].append(i)
                else:
                    rd['c'][op['eng']] = i
            for k in op['w']:
                last_w[k] = i
                readers[k] = {'c': {}, 'd': []}
        for op in self.ops:
            op['need'] = False
        for i, op in enumerate(self.ops):
            nd = set()
            for j in op['deps']:
                pj = self.ops[j]
                if (not pj['dma']) and (not op['dma']) and pj['eng'] == 'pe' and op['eng'] == 'pe':
                    continue
                nd.add(j)
                if not pj['dma']:
                    pj['need'] = True
            op['deps'] = nd
        seq = {}
        dcnt = {}
        tot = {}
        for op in self.ops:
            if op['dma']:
                tot[op['dma']] = tot.get(op['dma'], 0) + 16
        for op in self.ops:
            if op['dma']:
                dcnt[op['dma']] = dcnt.get(op['dma'], 0) + 16
                op['val'] = tot[op['dma']] if op['dma'].startswith('G_') else dcnt[op['dma']]
            elif op['need']:
                seq[op['eng']] = seq.get(op['eng'], 0) + 1
                op['val'] = seq[op['eng']]
        waited = {}
        for op in self.ops:
            e = op['eng']
            wl = {}
            for j in op['deps']:
                pj = self.ops[j]
                key = ('d', pj['dma']) if pj['dma'] else ('c', pj['eng'])
                wl[key] = max(wl.get(key, 0), pj['val'])
            out = []
            we = waited.setdefault(e, {})
            for key, v in wl.items():
                if we.get(key, 0) >= v:
                    continue
                we[key] = v
                out.append((key, v))
            op['waits'] = out
        self.dma_keys = sorted(dcnt.keys())

    def emit(self, nc, es):
        self.analyze()
        sems = {}
        for e in ['pe', 'act', 'dve', 'pool']:
            sems[('c', e)] = es.enter_context(nc.semaphore("s_" + e))
        for k in self.dma_keys:
            sems[('d', k)] = es.enter_context(nc.semaphore("d_" + k))
        block = es.enter_context(nc.Block())
        ops = self.ops

        def run(engname):
            def body(eng):
                for op in ops:
                    if op['eng'] != engname:
                        continue
                    for key, v in op['waits']:
                        eng.wait_ge(sems[key], v)
                    ins = op['fn'](eng)
                    if op['dma']:
                        ins.then_inc(sems[('d', op['dma'])], 16)
                    elif op['need']:
                        ins.then_inc(sems[('c', engname)], 1)
            return body

        block.sync(run('sp'))
        block.tensor(run('pe'))
        block.scalar(run('act'))
        block.vector(run('dve'))
        block.gpsimd(run('pool'))


class Ctx:
    pass


def build_nc(debug=None):
    nc = bass.Bass("TRN2", target_bir_lowering=False)
    es = ExitStack()
    P = Prog()
    K = Ctx()

    def din(name, shape, dt=F32):
        return nc.dram_tensor(name, list(shape), dt, kind="ExternalInput").ap()

    def dout(name, shape, dt=F32):
        return nc.dram_tensor(name, list(shape), dt, kind="ExternalOutput").ap()

    def sb(name, shape, dt=F32):
        return es.enter_context(nc.sbuf_tensor("sb_" + name, list(shape), dt))

    xa_d = din("xa", [TOK, D]); xb_d = din("xb", [TOK, D]); xs_d = din("xs", [NS, D])
    cK_d = din("cK", [NS, 128, 128]); cV_d = din("cV", [NS, 128, 128])
    sC0_d = din("sC0", [NS, 4, 128, 128]); sn0_d = din("sn0", [NS, 512]); sm0_d = din("sm0", [NS, 4])
    n1_d = din("n1", [1, D]); wg1_d = din("wg1", [D, DFF]); wu1_d = din("wu1", [D, DFF]); wd1_d = din("wd1", [DFF, D])
    nm_d = din("nm", [1, D]); win_d = din("win", [D, PW]); bi_d = din("bi", [1, 4]); bf_d = din("bf", [1, 4])
    gon_d = din("gon", [1, 512]); gq_d = din("gq", [1, 64]); gk_d = din("gk", [1, 64]); snk_d = din("snk", [1, 8])
    wout_d = din("wout", [D, D])
    n2_d = din("n2", [1, D]); wg2_d = din("wg2", [D, DFF]); wu2_d = din("wu2", [D, DFF]); wd2_d = din("wd2", [DFF, D])
    ident_d = din("ident", [128, 128]); maskbig_d = din("maskbig", [128, 128])
    swam_d = din("swam", [128, 256]); swam0_d = din("swam0", [128, 256]); amask_d = din("amask", [128, 2])
    sel_d = din("sel", [4, 512]); esel_d = din("esel", [128, 16])

    yb_d = dout("yb", [TOK, D]); ys_d = dout("ys", [NS, D])
    pk_d = dout("pk", [128, 128]); pv_d = dout("pv", [128, 128])
    pC_d = dout("pC", [4, 128, 128]); pn_d = dout("pn", [4, 128]); pm_d = dout("pm", [1, 4])
    sk_d = dout("sk", [NS, 128, 128]); sv_d = dout("sv", [NS, 128, 128])
    sC_d = dout("sC", [NS, 4, 128, 128]); sn_d = dout("sn", [NS, 512]); sm_d = dout("sm", [NS, 4])
    scr_d = nc.dram_tensor("scr", [8, NS, 512], F32, kind="Internal").ap()
    dbg_d = dout("dbg", [TOK, D]) if debug else None
    dbg2_d = dout("dbg2", [128, 1024]) if debug else None

    x = sb("x", [128, NCH, D])
    xs = sb("xs_t", [NS, D])
    hT = sb("hT", [128, 8 * 2064], BF16)
    hTv = hT[:, :].rearrange("p (c n) -> p c n", n=2064)
    big = sb("big", [128, 24704], BF16)
    actv = big[:, 0:8 * 2064].rearrange("p (c n) -> p c n", n=2064)
    wdv = big[:, 8 * 2064:8 * 2064 + 8 * 1024].rearrange("p (c n) -> p c n", n=1024)
    winv = big[:, 0:8 * PW].rearrange("p (c n) -> p c n", n=PW)
    wgu = sb("wgu", [128, 2 * 2 * 8 * 256], BF16)
    wguv = wgu[:, :].rearrange("p (s g c n) -> p s g c n", s=2, g=2, c=8)
    woutv = wgu[:, :].rearrange("p (c n) -> p c n", n=1024)
    gain = sb("gain", [128, D])
    identf = sb("identf", [128, 128]); identb = sb("identb", [128, 128], BF16)
    maskbig = sb("maskbig", [128, 128]); swam = sb("swam_t", [128, 256]); swam0 = sb("swam0_t", [128, 256])
    amask = sb("amask_t", [128, 2]); selT = sb("selT", [4, 512]); esel = sb("esel_t", [128, 16])
    epsb = sb("epsb", [128, 1]); onesb = sb("onesb", [128, 128], BF16); onesf = sb("onesf", [128, 128])
    hn = sb("hn", [128, 2, D], BF16)
    stat = sb("stat", [128, 2, 4])
    statF = sb("statF", [128, 36])
    sg = sb("sg", [128, 2, 512])

    psb = [es.enter_context(nc.psum_tensor("ps%d" % i, [128, 512], F32)) for i in range(8)]
    psb_bf = [es.enter_context(nc.psum_tensor("psb%d" % i, [128, 1024], BF16)) if False else None for i in range(8)]

    K.nc, K.P = nc, P

    ucnt = [0]

    def dma(q, out, in_, r, w, key):
        if key in ('c0', 'pout', 'yout', 'dbg', 'dbg2'):
            key = 'G_' + key
        elif key in ('sio', 'sout'):
            ucnt[0] += 1
            key = 's%d' % ucnt[0]
        P.add(q, lambda e: e.dma_start(out=out, in_=in_), r=r, w=w, dma=key)

    def mm(out, lhsT, rhs, start, stop, r, w):
        P.add('pe', lambda e: e.matmul(out, lhsT, rhs, start=start, stop=stop), r=r, w=w)

    def tr(out, in_, ident, r, w):
        P.add('pe', lambda e: e.transpose(out, in_, ident), r=r, w=w)

    def act(out, in_, func, r, w, bias=None, scale=None, accum=None):
        kw = {}
        if bias is not None:
            kw['bias'] = bias
        if scale is not None:
            kw['scale'] = scale
        if accum is not None:
            kw['accum_out'] = accum
        P.add('act', lambda e: e.activation(out, in_, func, **kw), r=r, w=w)

    def dve(fn, r, w):
        P.add('dve', fn, r=r, w=w)

    def pool(fn, r, w):
        P.add('pool', fn, r=r, w=w)

    K.dma, K.mm, K.tr, K.act, K.dve = dma, mm, tr, act, dve

    dma('sp', identf[:, :], ident_d, [], ['identf'], 'c0')
    dma('sp', maskbig[:, :], maskbig_d, [], ['maskbig'], 'c0')
    dma('sp', swam[:, :], swam_d, [], ['swam'], 'c0')
    dma('sp', swam0[:, :], swam0_d, [], ['swam0'], 'c0')
    dma('sp', amask[:, :], amask_d, [], ['amask'], 'c0')
    dma('sp', esel[:, :], esel_d, [], ['esel'], 'c0')
    dve(lambda e: e.memset(selT[:, :], 0.0), [], ['selT'])
    dma('sp', selT[0:4, :], sel_d, ['selT'], ['selT'], 'c0')
    dve(lambda e: e.memset(epsb[:, :], EPS), [], ['epsb'])
    dve(lambda e: e.memset(onesb[:, :], 1.0), [], ['onesb'])
    dve(lambda e: e.memset(onesf[:, :], 1.0), [], ['onesf'])
    dve(lambda e: e.tensor_copy(identb[:, :], identf[:, :]), ['identf'], ['identb'])

    def norm_T(xap, nt, col0, xkey, hkey, slot):
        hs = hn[0:nt, slot, :]
        ss = stat[0:nt, slot, 0:1]; rt = stat[0:nt, slot, 1:2]; rs = stat[0:nt, slot, 2:3]
        sk_ = 'st%d' % slot
        act(hs, xap, AF.Square, [xkey], ['hn%d' % slot, sk_], accum=ss)
        act(rt, ss, AF.Ln, [sk_, 'epsb'], [sk_], bias=epsb[0:nt, :], scale=1.0 / D)
        act(rs, rt, AF.Exp, [sk_], [sk_], scale=-0.5)
        dve(lambda e: e.scalar_tensor_tensor(hs, xap, rs, gain[0:nt, :], ALU.mult, ALU.mult),
            [xkey, sk_, 'gain'], ['hn%d' % slot])
        bank = psb[slot]
        tp = bank[:, :].bitcast(BF16).rearrange("p (c n) -> p c n", n=128)
        for dc in range(8):
            tr(tp[:, dc, 0:nt], hn[0:nt, slot, dc * 128:(dc + 1) * 128], identb[0:nt, 0:nt],
               ['hn%d' % slot, 'identb'], ['b%d' % slot])
        act(hTv[:, :, col0:col0 + nt], tp[:, :, 0:nt], AF.Copy, ['b%d' % slot], [hkey])

    def ffn(tag, subt, nrm_d, wg_d, wu_d, wd_d):
        dma('sp', gain[:, :], nrm_d.to_broadcast([128, D]), [], ['gain'], 'gain')
        ns_ = len(subt)
        dve(lambda e: e.memset(statF[:, 0:18], 1.0), [], ['statF'])
        for i, (xap, nt, col0, xkey, hkey) in enumerate(subt):
            act(hn[0:nt, i % 2, :], xap, AF.Square, [xkey], ['hn%d' % (i % 2), 'statF'], accum=statF[0:nt, i:i + 1])
        act(statF[:, 18:18 + ns_], statF[:, 0:ns_], AF.Ln, ['statF', 'epsb'], ['statF'], bias=epsb[:, :], scale=1.0 / D)
        act(statF[:, 18:18 + ns_], statF[:, 18:18 + ns_], AF.Exp, ['statF'], ['statF'], scale=-0.5)
        for i, (xap, nt, col0, xkey, hkey) in enumerate(subt):
            slot = i % 2
            hs = hn[0:nt, slot, :]
            rs = statF[0:nt, 18 + i:19 + i]
            dve(lambda e, hs=hs, xap=xap, rs=rs, nt=nt: e.scalar_tensor_tensor(hs, xap, rs, gain[0:nt, :], ALU.mult, ALU.mult),
                [xkey, 'statF', 'gain'], ['hn%d' % slot])
            tp = psb[slot][:, :].bitcast(BF16).rearrange("p (c n) -> p c n", n=128)
            for dc in range(8):
                tr(tp[:, dc, 0:nt], hn[0:nt, slot, dc * 128:(dc + 1) * 128], identb[0:nt, 0:nt], ['hn%d' % slot, 'identb'], ['b%d' % slot])
            act(hTv[:, :, col0:col0 + nt], tp[:, :, 0:nt], AF.Copy, ['b%d' % slot], [hkey])
        tts = []
        for (xap, nt, col0, xkey, hkey) in subt:
            t0 = (col0 // 512) * 512
            if not tts or tts[-1][0] != t0:
                tts.append([t0, 0, []])
            tts[-1][1] = col0 + nt - t0
            tts[-1][2].append(hkey)
        wgv = wg_d.rearrange("(c p) f -> p c f", p=128)
        wuv = wu_d.rearrange("(c p) f -> p c f", p=128)
        groups = [[0, 1, 2, 3], [4, 5, 6, 7], [8, 9, 10]]
        cnt = 0
        pcnt = 0
        dcnt = 0
        for gi, pairs in enumerate(groups):
            for jp in pairs:
                slot = pcnt % 2
                pcnt += 1
                dma('pool', wguv[:, slot, 0, :, :], wgv[:, :, jp * 256:(jp + 1) * 256], [], ['wg%d' % slot], 'wg%d' % slot)
                dma('pool', wguv[:, slot, 1, :, :], wuv[:, :, jp * 256:(jp + 1) * 256], [], ['wu%d' % slot], 'wu%d' % slot)
                for jj in range(2):
                    j = jp * 2 + jj
                    jl = (jp - pairs[0]) * 2 + jj
                    dma('pool', wdv[:, jl, :], wd_d[j * 128:(j + 1) * 128, :], [], ['wd%d' % jl], 'wd%d' % jl)
                    for ti, (t0, n, hkeys) in enumerate(tts):
                        s2 = cnt % 2
                        cnt += 1
                        gps = psb[s2][:, 0:n]; ups = psb[2 + s2][:, 0:n]
                        for dc in range(8):
                            mm(gps, wguv[:, slot, 0, dc, jj * 128:(jj + 1) * 128], hTv[:, dc, t0:t0 + n],
                               dc == 0, dc == 7, ['wg%d' % slot] + hkeys, ['b%d' % s2])
                        for dc in range(8):
                            mm(ups, wguv[:, slot, 1, dc, jj * 128:(jj + 1) * 128], hTv[:, dc, t0:t0 + n],
                               dc == 0, dc == 7, ['wu%d' % slot] + hkeys, ['b%d' % (2 + s2)])
                        act(sg[:, s2, 0:n], gps, AF.Silu, ['b%d' % s2], ['sg%d' % s2])
                        ao = actv[:, jl, t0:t0 + n]
                        sgs = sg[:, s2, 0:n]
                        dve(lambda e, ao=ao, sgs=sgs, ups=ups: e.tensor_tensor(ao, sgs, ups, ALU.mult),
                            ['sg%d' % s2, 'b%d' % (2 + s2)], ['act%d.%d' % (jl, ti)])
            nj = len(pairs) * 2
            for (xap, nt, col0, xkey, hkey) in subt:
                ti = [k for k, t in enumerate(tts) if t[0] == (col0 // 512) * 512][0]
                s2 = dcnt % 2
                dcnt += 1
                for half in range(2):
                    bk = 4 + 2 * s2 + half
                    for jl in range(nj):
                        mm(psb[bk][0:nt, :], actv[:, jl, col0:col0 + nt], wdv[:, jl, half * 512:(half + 1) * 512],
                           jl == 0, jl == nj - 1, ['act%d.%d' % (jl, ti), 'wd%d' % jl], ['b%d' % bk])
                for half in range(2):
                    bk = 4 + 2 * s2 + half
                    xh = xap[:, half * 512:(half + 1) * 512]
                    pb = psb[bk][0:nt, :]
                    dve(lambda e, xh=xh, pb=pb: e.scalar_tensor_tensor(xh, pb, 0.5, xh, ALU.mult, ALU.add),
                        ['b%d' % bk, xkey], [xkey])

    K.ffn = ffn
    tmpA = sb("tmpA", [128, 1280])
    gts = tmpA[0:4, 0:768].rearrange("p (a n) -> p a n", n=128)
    smv = tmpA[:, 0:512].rearrange("p (h n) -> p h n", n=256)
    Pbv = tmpA[:, 512:768].bitcast(BF16).rearrange("p (h n) -> p h n", n=256)
    PTv = tmpA[:, 768:1024].bitcast(BF16).rearrange("p (h s n) -> p h s n", h=2, s=2)
    rowst = tmpA[:, 1024:1280]
    tmpB = sb("tmpB", [128, 1024])
    hnum = tmpB[:, 0:512].rearrange("p (h n) -> p h n", n=128)
    sigo = tmpB[:, 512:1024]
    mixb = sb("mixb", [128, 7 * 512], BF16)
    kM = mixb[:, 0:512]; vaug = mixb[:, 512:1024].rearrange("p (h n) -> p h n", n=128)
    vw = mixb[:, 1024:1536].rearrange("p (h n) -> p h n", n=128)
    qT = mixb[:, 1536:2048].rearrange("p (h n) -> p h n", n=128)
    kT = mixb[:, 2048:2560].rearrange("p (h n) -> p h n", n=128)
    SwT = mixb[:, 2560:3072].rearrange("p (h n) -> p h n", n=128)
    qn = mixb[:, 3072:3584]
    CT = sb("CT", [128, 512]); CTv = CT[:, :].rearrange("p (h n) -> p h n", n=128)
    smallb = sb("smallb", [128, 2048], BF16)
    CTb = smallb[:, 0:512].rearrange("p (h n) -> p h n", n=128)
    qaT = smallb[:, 512:1024].rearrange("p (h n) -> p h n", n=128)
    kaT = smallb[:, 1024:1280].rearrange("p (s n) -> p s n", n=128)
    vA = smallb[:, 1280:1536].rearrange("p (s n) -> p s n", n=128)
    knb = smallb[:, 1536:1664]; nTb = smallb[:, 1664:1668]; wendb = smallb[:, 1668:1672]
    scal = sb("scal", [128, 256])
    gtok = scal[:, 0:12]; mu_c = scal[:, 12:16]; mu_p = scal[:, 16:20]; wend = scal[:, 20:24]
    winter = scal[:, 24:28]; enm = scal[:, 28:32]; decay = scal[:, 32:36]; nT = scal[:, 36:40]
    den = scal[:, 40:44]; t4a = scal[:, 44:48]; t4b = scal[:, 48:52]; ssh = scal[:, 52:56]; comb = scal[:, 56:60]
    car = scal[0:4, 60:64]
    car2 = scal[0:4, 64:68]
    ssq = scal[:, 68:76]; rq = scal[:, 76:84]; ssk = scal[:, 84:86]; rk = scal[:, 86:88]
    mx = scal[:, 88:96]; negm2 = scal[:, 96:104]; rsum = scal[:, 104:112]; esnk = scal[:, 112:120]
    snkb = scal[:, 120:128]; gqb = scal[:, 128:192]; gkb = scal[:, 192:256]
    qsq = sg[:, 0, :]; knf = sg[:, 1, 0:128]; kvout = sg[:, 1, 128:384]
    ybf = hn[:, 0, :]; yTt = hn[:, 1, :].rearrange("p (c n) -> p c n", n=128)
    gonb = gain[:, 0:512]; DT = gain[:, 512:1024].rearrange("p (h n) -> p h n", n=128)
    ACTKEYS = ['act%d.%d' % (a, b) for a in range(8) for b in range(5)] + ['wd%d' % a for a in range(8)]
    WINBLK = [(C_K, C_V, 'win_k'), (C_I, C_QA, 'win_g'), (C_V, C_O, 'win_v'), (C_O, C_I, 'win_o'), (C_Q, C_K, 'win_q'),
              (C_KA, PW, 'win_ka'), (C_QA, C_KA, 'win_qa')]
    WINKEYS = [b_[2] for b_ in WINBLK]

    def wkey(c0):
        for (a0, a1, kk) in WINBLK:
            if a0 <= c0 < a1:
                return kk
        raise KeyError(c0)
    WGKEYS = ['wg0', 'wg1', 'wu0', 'wu1']
    selv = selT[0:4, :].rearrange("p (h n) -> p h n", n=128)

    def bank(i, h=None):
        return psb[i]

    def mixer_setup(first):
        winD = win_d.rearrange("(c p) f -> p c f", p=128)
        first_blk = True
        for (c0, c1, kk) in WINBLK:
            dma('pool', winv[:, :, c0:c1], winD[:, :, c0:c1], [], (ACTKEYS if first_blk else []) + [kk], kk)
            first_blk = False
        if not first:
            dma('pool', woutv[:, :, :], wout_d.rearrange("(c p) f -> p c f", p=128), [], WGKEYS + ['wout'], 'wout')
        dve(lambda e: e.memset(rowst[:, 0:1], 0.0), [], ['gain', 'gainA', 'gainB', 'tmpA'])
        dma('sp', gain[:, :], nm_d.to_broadcast([128, D]), [], ['gain', 'gainA', 'gainB'], 'gain')

    def mixer_setup2():
        dma('sp', gonb, gon_d.to_broadcast([128, 512]), [], ['gain', 'gainA', 'gainB'], 'gain')

    def consts_mixer():
        dma('sp', gqb, gq_d.to_broadcast([128, 64]), [], ['gqb'], 'c0')
        dma('sp', gkb, gk_d.to_broadcast([128, 64]), [], ['gkb'], 'c0')
        dma('sp', snkb, snk_d.to_broadcast([128, 8]), [], ['snkb'], 'c0')
        dve(lambda e: e.tensor_scalar(nsnkb, snkb, -1.0, None, ALU.mult), ['snkb'], ['nsnkb'])
        dma('sp', car[:, 2:3], bf_d.rearrange("o h -> h o"), [], ['car2'], 'c0')
        dma('sp', car[:, 3:4], bi_d.rearrange("o h -> h o"), [], ['car3'], 'c0')
        dve(lambda e: e.tensor_scalar(car[:, 2:3], car[:, 2:3], -1.0, None, ALU.mult), ['car2'], ['car2'])
        dve(lambda e: e.tensor_tensor(car[:, 3:4], car[:, 3:4], amask[0:4, 0:1], ALU.add), ['car3', 'amask'], ['car3'])
        dve(lambda e: e.tensor_scalar(car2[:, 0:1], amask[0:4, 1:2], -1.0, None, ALU.mult), ['amask'], ['car4'])
        dve(lambda e: e.memset(car[:, 0:2], 0.0), [], ['Bcar', 'Gcar'])
        dve(lambda e: e.memset(CT[:, :], 0.0), [], ['CT'])
        dve(lambda e: e.memset(nT, 0.0), [], ['nT'])
        dve(lambda e: e.memset(mu_p, 0.0), [], ['mu_p'])
        dve(lambda e: e.memset(smallb[:, :], 0.0), [], ['CTb', 'qaT', 'kaT0', 'kaT1', 'vA0', 'vA1', 'knb', 'nTb', 'wendb'])

    def chunk(c, full, xap, xkey, hkey, col0, kv_need, first_block):
        hcols = lambda dc: hTv[:, dc, col0:col0 + 128]
        b0 = psb[0]
        for k_, cc in ((0, C_I), (1, C_F)):
            for dc in range(8):
                mm(b0[0:4, k_ * 128:(k_ + 1) * 128], winv[:, dc, cc:cc + 4], hcols(dc), dc == 0, dc == 7,
                   ['win', hkey], ['b0'])
        e_, B_, a_, G_, nm_, on_ = [gts[:, i, :] for i in range(6)]
        dve(lambda e: e.memset(on_, 1.0), [], ['tmpA'])
        act(e_, b0[0:4, 128:256], AF.Exp, ['b0', 'car2'], ['tmpA'], bias=car[:, 2:3], scale=-1.0)
        act(e_, e_, AF.Ln, ['tmpA'], ['tmpA'], bias=1.0)
        dve(lambda e: e.tensor_scalar(e_, e_, car2[:, 0:1], None, ALU.mult), ['tmpA', 'car4'], ['tmpA'])
        dve(lambda e: e.tensor_tensor_scan(B_, on_, e_, car[:, 0:1], ALU.mult, ALU.add), ['tmpA', 'Bcar'], ['tmpA'])
        dve(lambda e: e.scalar_tensor_tensor(a_, b0[0:4, 0:128], car[:, 3:4], B_, ALU.add, ALU.subtract),
            ['b0', 'car3', 'tmpA'], ['tmpA'])
        dve(lambda e: e.tensor_tensor_scan(G_, a_, a_, car[:, 1:2], ALU.max, ALU.max), ['tmpA', 'Gcar'], ['tmpA'])
        dve(lambda e: e.scalar_tensor_tensor(nm_, B_, -1.0, G_, ALU.mult, ALU.subtract), ['tmpA'], ['tmpA'])
        dve(lambda e: e.tensor_copy(car[:, 0:1], B_[:, 127:128]), ['tmpA'], ['Bcar'])
        dve(lambda e: e.tensor_copy(car[:, 1:2], G_[:, 127:128]), ['tmpA'], ['Gcar'])
        for i_, src in enumerate((a_, G_, nm_)):
            tr(b0[:, 256 + 4 * i_:260 + 4 * i_], src, identf[0:4, 0:4], ['tmpA', 'identf'], ['b0'])
        for h in range(4):
            mm(b0[:, 272 + h:273 + h], selv[:, h, :], G_[:, 127:128], True, True, ['selT', 'tmpA'], ['b0'])
        dve(lambda e: e.tensor_copy(scal[:, 0:12], b0[:, 256:268]), ['b0'], ['gtok'])
        dve(lambda e: e.tensor_copy(mu_c, b0[:, 272:276]), ['b0'], ['gtok'])
        if full:
            for h in range(4):
                mm(psb[2][:, h * 128:(h + 1) * 128], selv[:, h, :], G_, True, False, ['selT', 'tmpA'], ['b2'])
                mm(psb[2][:, h * 128:(h + 1) * 128], identf[:, :], maskbig[:, :], False, True, ['identf', 'maskbig'], ['b2'])
        def proj(bk, c0, n):
            for dc in range(8):
                mm(psb[bk][:, 0:n], hcols(dc), winv[:, dc, c0:c0 + n], dc == 0, dc == 7, ['win', hkey], ['b%d' % bk])
        proj(1, C_K, 512)
        act(kM, psb[1][:, :], AF.Copy, ['b1'], ['kM'], scale=128.0 ** -0.5)
        proj(3, C_V, 512)
        dve(lambda e: e.tensor_copy(vaug, psb[3][:, :].rearrange("p (h n) -> p h n", n=128)), ['b3'], ['vaug'])
        dve(lambda e: e.tensor_tensor(wend, gtok[:, 0:4], mu_c, ALU.subtract), ['gtok'], ['wend'])
        act(wend, wend, AF.Exp, ['wend'], ['wend'])
        dve(lambda e: e.tensor_copy(wendb, wend), ['wend'], ['wendb'])
        dve(lambda e: e.tensor_tensor(decay, mu_p, mu_c, ALU.subtract), ['mu_p', 'gtok'], ['decay'])
        act(decay, decay, AF.Exp, ['decay'], ['decay'])
        if full:
            dve(lambda e: e.tensor_tensor(winter, mu_p, gtok[:, 4:8], ALU.subtract), ['mu_p', 'gtok'], ['winter'])
            act(winter, winter, AF.Exp, ['winter'], ['winter'])
            act(enm, gtok[:, 8:12], AF.Exp, ['gtok'], ['enm'])
        dve(lambda e: e.tensor_copy(mu_p, mu_c), ['gtok', 'decay', 'winter'], ['mu_p'])
        if full:
            proj(4, C_O, 512)
            act(sigo, psb[4][:, :], AF.Sigmoid, ['b4'], ['sigo'])
            dve(lambda e: e.tensor_tensor(sigo, sigo, gonb, ALU.mult), ['sigo', 'gainA'], ['sigo'])
            for h in range(4):
                for dc in range(8):
                    mm(psb[5][:, h * 128:(h + 1) * 128], winv[:, dc, C_Q + h * 128:C_Q + (h + 1) * 128], hcols(dc),
                       dc == 0, dc == 7, ['win', hkey], ['b5'])
            for h in range(4):
                for dc in range(8):
                    mm(psb[6][:, h * 128:(h + 1) * 128], winv[:, dc, C_K + h * 128:C_K + (h + 1) * 128], hcols(dc),
                       dc == 0, dc == 7, ['win', hkey], ['b6'])
            act(qT, psb[5][:, :].rearrange("p (h n) -> p h n", n=128), AF.Copy, ['b5'], ['qT'])
            act(kT, psb[6][:, :].rearrange("p (h n) -> p h n", n=128), AF.Copy, ['b6'], ['kT'], scale=128.0 ** -0.5)
            for h in range(4):
                mm(psb[1][:, h * 128:(h + 1) * 128], kT[:, h, :], qT[:, h, :], True, True, ['kT', 'qT'], ['b1'])
            for h in range(4):
                act(DT[:, h, :], psb[2][:, h * 128:(h + 1) * 128], AF.Exp, ['b2', 'gtok'], ['gainB'],
                    bias=gtok[:, h:h + 1], scale=-1.0)
            dve(lambda e: e.tensor_tensor(SwT, psb[1][:, :].rearrange("p (h n) -> p h n", n=128), DT, ALU.mult),
                ['b1', 'gainB'], ['SwT'])
            for h in range(4):
                mm(psb[3][:, h * 128:(h + 1) * 128], SwT[:, h, :], vaug[:, h, :], True, True, ['SwT', 'vaug'], ['b3'])
            for h in range(4):
                mm(psb[5][:, h * 128:(h + 1) * 128], qT[:, h, :], CTb[:, h, :], True, True, ['qT', 'CTb'], ['b5'])
            for h in range(4):
                mm(b0[:, 280 + h:281 + h], SwT[:, h, :], onesb[:, 0:1], True, True, ['SwT', 'onesb'], ['b0'])
            for h in range(4):
                mm(b0[:, 284 + h:285 + h], qT[:, h, :], nTb[:, h:h + 1], True, True, ['qT', 'nTb'], ['b0'])
            act(hnum, psb[3][:, :].rearrange("p (h n) -> p h n", n=128), AF.Copy, ['b3'], ['hnum'])
            for h in range(4):
                dve(lambda e, h=h: e.scalar_tensor_tensor(hnum[:, h, :], psb[5][:, h * 128:(h + 1) * 128], winter[:, h:h + 1],
                                                         hnum[:, h, :], ALU.mult, ALU.add), ['b5', 'winter', 'hnum'], ['hnum'])
            dve(lambda e: e.tensor_copy(den, b0[:, 280:284]), ['b0'], ['den'])
            dve(lambda e: e.tensor_tensor(t4a, b0[:, 284:288], winter, ALU.mult), ['b0', 'winter'], ['t4a'])
            dve(lambda e: e.tensor_tensor(den, den, t4a, ALU.add), ['den', 't4a'], ['den'])
            dve(lambda e: e.tensor_scalar(t4a, den, -1.0, None, ALU.mult), ['den'], ['t4a'])
            dve(lambda e: e.tensor_tensor(den, den, t4a, ALU.max), ['den', 't4a'], ['den'])
            dve(lambda e: e.tensor_tensor(den, den, enm, ALU.max), ['den', 'enm'], ['den'])
            dve(lambda e: e.reciprocal(den, den), ['den'], ['den'])
            for h in range(4):
                act(qsq[:, 0:128], hnum[:, h, :], AF.Square, ['hnum', 'den'], ['sg0', 'ssh%d' % h], scale=den[:, h:h + 1],
                    accum=ssh[:, h:h + 1])
            act(t4b, ssh, AF.Sqrt, ['ssh0', 'ssh1', 'ssh2', 'ssh3', 'epsb'], ['t4b'], bias=epsb[:, :], scale=1.0 / 128)
            dve(lambda e: e.reciprocal(t4b, t4b), ['t4b'], ['t4b'])
            dve(lambda e: e.tensor_tensor(comb, t4b, den, ALU.mult), ['t4b', 'den'], ['comb'])
            for h in range(4):
                dve(lambda e, h=h: e.scalar_tensor_tensor(ybf[:, h * 128:(h + 1) * 128], hnum[:, h, :], comb[:, h:h + 1],
                                                         sigo[:, h * 128:(h + 1) * 128], ALU.mult, ALU.mult),
                    ['hnum', 'comb', 'sigo'], ['hn0'])
        for h in range(4):
            dve(lambda e, h=h: e.tensor_scalar(vw[:, h, :], vaug[:, h, :], wend[:, h:h + 1], None, ALU.mult), ['vaug', 'wend'], ['vw'])
        for h in range(4):
            mm(psb[7][:, h * 128:(h + 1) * 128], kM[:, h * 128:(h + 1) * 128], vw[:, h, :], True, True, ['kM', 'vw'], ['b7'])
        for h in range(4):
            mm(b0[:, 288 + h:289 + h], kM[:, h * 128:(h + 1) * 128], wendb[:, h:h + 1], True, True, ['kM', 'wendb'], ['b0'])
        for h in range(4):
            dve(lambda e, h=h: e.scalar_tensor_tensor(CTv[:, h, :], CTv[:, h, :], decay[:, h:h + 1], psb[7][:, h * 128:(h + 1) * 128],
                                                     ALU.mult, ALU.add), ['CT', 'decay', 'b7'], ['CT'])
        act(CTb, CTv, AF.Copy, ['CT'], ['CTb'])
        dve(lambda e: e.tensor_tensor(nT, nT, decay, ALU.mult), ['nT', 'decay', 'nTb'], ['nT'])
        dve(lambda e: e.tensor_tensor(nT, nT, b0[:, 288:292], ALU.add), ['nT', 'b0'], ['nT'])
        dve(lambda e: e.tensor_copy(nTb, nT), ['nT'], ['nTb'])
        if dbg2_d is not None and full:
            dma('sp', dbg2_d[:, c * 64:(c + 1) * 64], scal[:, 0:64], ['gtok', 'wend', 'decay', 'winter', 'enm', 'nT', 'den', 'comb'], [], 'dbg2')
        cur = c % 2
        prv = 1 - cur
        if full or kv_need:
            proj(4, C_KA, 256)
            kaps = psb[4][:, 0:128]
            act(qsq[:, 0:128], kaps, AF.Square, ['b4'], ['sg0'])
            dve(lambda e: e.tensor_reduce(ssk, qsq[:, 0:128].rearrange("p (h n) -> p h n", n=64), AX.X, ALU.add), ['sg0'], ['ssk'])
            act(rk, ssk, AF.Sqrt, ['ssk', 'epsb'], ['rk'], bias=epsb[:, :], scale=1.0 / 64)
            dve(lambda e: e.reciprocal(rk, rk), ['rk'], ['rk'])
            for h in range(2):
                dve(lambda e, h=h: e.scalar_tensor_tensor(knf[:, h * 64:(h + 1) * 64], kaps[:, h * 64:(h + 1) * 64], rk[:, h:h + 1], gkb,
                                                         ALU.mult, ALU.mult), ['b4', 'rk', 'gkb'], ['sg1'])
            dve(lambda e: e.tensor_copy(knb, knf), ['sg1'], ['knb'])
            dve(lambda e: e.tensor_copy(vA[:, cur, :], psb[4][:, 128:256]), ['b4'], ['vA%d' % cur])
            if kv_need:
                dve(lambda e: e.tensor_copy(kvout[:, 0:128], knf), ['sg1'], ['sg1'])
                dve(lambda e: e.tensor_copy(kvout[:, 128:256], psb[4][:, 128:256]), ['b4'], ['sg1'])
            tpk = psb[6][:, :].bitcast(BF16)
            tr(tpk[:, 0:128], knb, identb[:, :], ['knb', 'identb'], ['b6'])
            act(kaT[:, cur, :], tpk[:, 0:128], AF.Copy, ['b6'], ['kaT%d' % cur])
        if full:
            proj(4, C_QA, 512)
            qaps = psb[4][:, :]
            act(qsq, qaps, AF.Square, ['b4'], ['sg0'])
            dve(lambda e: e.tensor_reduce(ssq, qsq.rearrange("p (h n) -> p h n", n=64), AX.X, ALU.add), ['sg0'], ['ssq'])
            act(rq, ssq, AF.Sqrt, ['ssq', 'epsb'], ['rq'], bias=epsb[:, :], scale=1.0 / 64)
            dve(lambda e: e.reciprocal(rq, rq), ['rq'], ['rq'])
            for h in range(8):
                qo = (h % 4) * 128 + (h // 4) * 64
                dve(lambda e, h=h, qo=qo: e.scalar_tensor_tensor(qn[:, qo:qo + 64], qaps[:, h * 64:(h + 1) * 64], rq[:, h:h + 1], gqb,
                                                                ALU.mult, ALU.mult), ['b4', 'rq', 'gqb'], ['qn'])
            tpq = psb[5][:, :].bitcast(BF16).rearrange("p (c n) -> p c n", n=128)
            for p_ in range(4):
                tr(tpq[:, p_, :], qn[:, p_ * 128:(p_ + 1) * 128], identb[:, :], ['qn', 'identb'], ['b5'])
            act(qaT, tpq[:, 0:4, :], AF.Copy, ['b5'], ['qaT'])
            msk = swam0 if first_block else swam
            mkey = 'swam0' if first_block else 'swam'
            ops_ = psb[3]
            for grp in range(4):
                heads = [2 * grp, 2 * grp + 1]
                scb = psb[1 + (grp % 2)]
                bkey = 'b%d' % (1 + (grp % 2))
                for i_, hd in enumerate(heads):
                    kvh, g_ = hd // 4, hd % 4
                    pr = slice(kvh * 64, (kvh + 1) * 64)
                    mm(scb[:, i_ * 256:i_ * 256 + 128], qaT[pr, g_, :], kaT[pr, prv, :], True, True,
                       ['qaT', 'kaT%d' % prv], [bkey])
                    mm(scb[:, i_ * 256 + 128:i_ * 256 + 256], qaT[pr, g_, :], kaT[pr, cur, :], True, True,
                       ['qaT', 'kaT%d' % cur], [bkey])
                for i_ in range(2):
                    dve(lambda e, scb=scb, msk=msk, i_=i_: e.scalar_tensor_tensor(
                        smv[:, i_, :], scb[:, i_ * 256:(i_ + 1) * 256], 0.125, msk[:, :], ALU.mult, ALU.add), [bkey, mkey], ['tmpA'])
                dve(lambda e, grp=grp: e.tensor_reduce(mx[:, 2 * grp:2 * grp + 2], smv, AX.X, ALU.max), ['tmpA'], ['mx'])
                dve(lambda e, grp=grp: e.tensor_tensor(mx[:, 2 * grp:2 * grp + 2], mx[:, 2 * grp:2 * grp + 2],
                                                      snkb[:, 2 * grp:2 * grp + 2], ALU.max), ['mx', 'snkb'], ['mx'])
                dve(lambda e, grp=grp: e.tensor_scalar(negm2[:, 2 * grp:2 * grp + 2], mx[:, 2 * grp:2 * grp + 2], -1.0, None, ALU.mult),
                    ['mx'], ['negm2'])
                for i_, hd in enumerate(heads):
                    act(Pbv[:, i_, :], smv[:, i_, :], AF.Exp, ['tmpA', 'negm2'], ['Pb', 'rsum%d' % hd], bias=negm2[:, hd:hd + 1],
                        accum=rsum[:, hd:hd + 1])
                tpp = psb[5 + (grp % 2)][:, :].bitcast(BF16).rearrange("p (h s n) -> p h s n", h=4, s=2)
                pkey = 'b%d' % (5 + (grp % 2))
                for i_, hd in enumerate(heads):
                    for s_ in range(2):
                        tr(tpp[:, i_, s_, :], Pbv[:, i_, s_ * 128:(s_ + 1) * 128], identb[:, :], ['Pb', 'identb'], [pkey])
                act(PTv, tpp[:, 0:2, :, :], AF.Copy, [pkey], ['PT'])
                for i_, hd in enumerate(heads):
                    kvh = hd // 4
                    mm(ops_[:, hd * 64:(hd + 1) * 64], PTv[:, i_, 0, :], vA[:, prv, kvh * 64:(kvh + 1) * 64], True, False,
                       ['PT', 'vA%d' % prv], ['b3'])
                    mm(ops_[:, hd * 64:(hd + 1) * 64], PTv[:, i_, 1, :], vA[:, cur, kvh * 64:(kvh + 1) * 64], False, True,
                       ['PT', 'vA%d' % cur], ['b3'])
            dve(lambda e: e.tensor_tensor(esnk, snkb, negm2, ALU.add), ['snkb', 'negm2'], ['esnk'])
            act(esnk, esnk, AF.Exp, ['esnk'], ['esnk'])
            dve(lambda e: e.tensor_tensor(esnk, esnk, rsum, ALU.add), ['esnk'] + ['rsum%d' % h for h in range(8)], ['esnk'])
            dve(lambda e: e.reciprocal(esnk, esnk), ['esnk'], ['esnk'])
            for h in range(8):
                dve(lambda e, h=h: e.tensor_scalar(ybf[:, 512 + h * 64:512 + (h + 1) * 64], ops_[:, h * 64:(h + 1) * 64],
                                                  esnk[:, h:h + 1], None, ALU.mult), ['b3', 'esnk'], ['hn0'])
            if dbg_d is not None:
                dma('pool', dbg_d[col0:col0 + 128, :], ybf, ['hn0'], [], 'dbg')
            tpy = psb[1][:, :].bitcast(BF16).rearrange("p (c n) -> p c n", n=128)
            for fc in range(8):
                tr(tpy[:, fc, :], ybf[:, fc * 128:(fc + 1) * 128], identb[:, :], ['hn0', 'identb'], ['b1'])
            act(yTt, tpy, AF.Copy, ['b1'], ['hn1'])
            for half in range(2):
                for fc in range(8):
                    mm(psb[6 + half][:, :], yTt[:, fc, :], woutv[:, fc, half * 512:(half + 1) * 512], fc == 0, fc == 7,
                       ['hn1', 'wout'], ['b%d' % (6 + half)])
            for half in range(2):
                xh = xap[:, half * 512:(half + 1) * 512]
                pb = psb[6 + half][:, :]
                dve(lambda e, xh=xh, pb=pb: e.tensor_tensor(xh, xh, pb, ALU.add), ['b%d' % (6 + half), xkey], [xkey])

    hTr = [hT[:, s * 1024:(s + 1) * 1024].rearrange("p (c n) -> p c n", n=128) for s in range(2)]
    hTs = hT[:, 2048:2176].rearrange("p (c n) -> p c n", n=NS)
    hoff = [2176]

    def hb(n):
        a = hoff[0]
        hoff[0] += n
        assert hoff[0] <= 16512
        return hT[:, a:a + n]

    def hbf(n):
        return hb(2 * n).bitcast(F32)

    v4 = lambda ap: ap.rearrange("p (h n) -> p h n", n=128)
    kM_s = [kM, hb(512)]; vaug_s = [vaug, v4(hb(512))]; qT_s = [qT, v4(hb(512))]; kT_s = [kT, v4(hb(512))]
    qaT_s = [qaT, v4(hb(512))]; sigo_s = [sigo, hbf(512)]; DT_s = [DT, v4(hbf(512))]
    gts_off = hoff[0]
    gts2_s = [hbf(768)[0:4, :].rearrange("p (a n) -> p a n", n=128) for _ in range(2)]
    gtsA = hT[:, gts_off:gts_off + 3072].bitcast(F32)[0:4, :].rearrange("p (a n) -> p a n", n=256)
    SC = [hbf(64), hbf(64)]
    kaT3 = [kaT[:, 0, :], kaT[:, 1, :], hb(128)]; vA3 = [vA[:, 0, :], vA[:, 1, :], hb(128)]
    qsqS = hbf(512); knfS = hbf(128); knbS = hb(128); qnS = hb(512); hnS = hb(1024); statS = hbf(4)
    gonbS = hbf(512); DT_s[0] = v4(hbf(512))
    smv4 = tmpA[:, 0:1024].rearrange("p (h n) -> p h n", n=256)
    nsnkb = tmpA[:, 1032:1040]
    tmpC = sb("tmpC", [128, 512])
    Pbv4 = tmpC[:, :].bitcast(BF16).rearrange("p (h n) -> p h n", n=256)
    PTv4 = hb(1024).rearrange("p (h s n) -> p h s n", h=4, s=2)
    PIPEKEYS = (['kM1', 'vaug1', 'qT1', 'kT1', 'qaT1', 'sigo1', 'gainB1', 'gts2_0', 'gts2_1', 'SC0', 'SC1', 'kaT2', 'vA2', 'qsqS', 'knS', 'qnS',
                 'hnS', 'hTr0', 'hTr1', 'hTs', 'gonbS', 'DT0', 'PT'])
    HKALL = ['hT%d' % c for c in range(17)]

    def pipe_begin():
        dve(lambda e: e.memset(statS, 0.0), [], HKALL + PIPEKEYS + ['hn0', 'ybm', 'yba'])
        dma('sp', gonbS, gon_d.to_broadcast([128, 512]), [], ['gonbS'], 'gain2')

    def pipe_end():
        dve(lambda e: e.memset(statS, 0.0), [], PIPEKEYS + HKALL)

    def stage0(c, xap, xkey, nt=128, dst=None, dkey=None):
        S = c % 2
        dst = hTr[S] if dst is None else dst
        dkey = ('hTr%d' % S) if dkey is None else dkey
        hs = hnS[0:nt, :]
        ss = statS[0:nt, 0:1]; rt = statS[0:nt, 1:2]; rs = statS[0:nt, 2:3]
        act(hs, xap, AF.Square, [xkey], ['hnS', 'stS'], accum=ss)
        act(rt, ss, AF.Ln, ['stS', 'epsb'], ['stS'], bias=epsb[0:nt, :], scale=1.0 / D)
        act(rs, rt, AF.Exp, ['stS'], ['stS'], scale=-0.5)
        dve(lambda e: e.scalar_tensor_tensor(hs, xap, rs, gain[0:nt, :], ALU.mult, ALU.mult), [xkey, 'stS', 'gain'], ['hnS'])
        yield
        tp = psb[1 + S][:, :].bitcast(BF16).rearrange("p (c n) -> p c n", n=128)
        for dc in range(8):
            tr(tp[:, dc, 0:nt], hnS[0:nt, dc * 128:(dc + 1) * 128], identb[0:nt, 0:nt], ['hnS', 'identb'], ['b%d' % (1 + S)])
        act(dst, tp[:, :, 0:nt], AF.Copy, ['b%d' % (1 + S)], [dkey])
        yield

    def stage1(c, full, kv_need, gc):
        S = c % 2
        hk = 'hTr%d' % S
        hcols = lambda dc: hTr[S][:, dc, :]
        b0 = psb[0]
        sc = SC[S]; sk = 'SC%d' % S
        gtok_ = sc[:, 0:12]; mu_c_ = sc[:, 12:16]; wend_ = sc[:, 16:20]; winter_ = sc[:, 20:24]; enm_ = sc[:, 24:28]
        decay_ = sc[:, 28:32]; wendb_ = sc[:, 32:34].bitcast(BF16)
        K3 = gc % 3
        for k_, cc in ((0, C_I), (1, C_F)):
            for dc in range(8):
                mm(b0[0:4, k_ * 128:(k_ + 1) * 128], winv[:, dc, cc:cc + 4], hcols(dc), dc == 0, dc == 7, ['win_g', hk], ['b0'])
        e_, B_, a_, G_, nm_, on_ = [gts2_s[S][:, i, :] for i in range(6)]
        GK = ['gts2_%d' % S]
        dve(lambda e: e.memset(on_, 1.0), [], GK)
        act(e_, b0[0:4, 128:256], AF.Exp, ['b0', 'car2'], GK, bias=car[:, 2:3], scale=-1.0)
        act(e_, e_, AF.Ln, GK, GK, bias=1.0)
        dve(lambda e: e.tensor_scalar(e_, e_, car2[:, 0:1], None, ALU.mult), GK + ['car4'], GK)
        dve(lambda e: e.tensor_tensor_scan(B_, on_, e_, car[:, 0:1], ALU.mult, ALU.add), GK + ['Bcar'], GK)
        dve(lambda e: e.scalar_tensor_tensor(a_, b0[0:4, 0:128], car[:, 3:4], B_, ALU.add, ALU.subtract), ['b0', 'car3'] + GK, GK)
        dve(lambda e: e.tensor_tensor_scan(G_, a_, a_, car[:, 1:2], ALU.max, ALU.max), GK + ['Gcar'], GK)
        dve(lambda e: e.scalar_tensor_tensor(nm_, B_, -1.0, G_, ALU.mult, ALU.subtract), GK, GK)
        dve(lambda e: e.tensor_copy(car[:, 0:1], B_[:, 127:128]), GK, ['Bcar'])
        dve(lambda e: e.tensor_copy(car[:, 1:2], G_[:, 127:128]), GK, ['Gcar'])
        yield

        def proj(bk, c0, n):
            for dc in range(8):
                mm(psb[bk][:, 0:n], hcols(dc), winv[:, dc, c0:c0 + n], dc == 0, dc == 7, [wkey(c0), hk], ['b%d' % bk])
        proj(1, C_K, 512)
        act(kM_s[S], psb[1][:, :], AF.Copy, ['b1'], ['kM%d' % S], scale=128.0 ** -0.5)
        proj(2, C_V, 512)
        dve(lambda e: e.tensor_copy(vaug_s[S], v4(psb[2][:, :])), ['b2'], ['vaug%d' % S])
        yield
        for i_, src in enumerate((a_, G_, nm_)):
            tr(b0[:, 256 + 4 * i_:260 + 4 * i_], src, identf[0:4, 0:4], GK + ['identf'], ['b0'])
        for h in range(4):
            mm(b0[:, 272 + h:273 + h], selv[:, h, :], G_[:, 127:128], True, True, ['selT'] + GK, ['b0'])
        dve(lambda e: e.tensor_copy(sc[:, 0:16], b0[:, 256:272]), ['b0'], [sk])
        dve(lambda e: e.tensor_copy(mu_c_, b0[:, 272:276]), ['b0'], [sk])
        dve(lambda e: e.tensor_tensor(wend_, gtok_[:, 0:4], mu_c_, ALU.subtract), [sk], [sk])
        act(wend_, wend_, AF.Exp, [sk], [sk])
        dve(lambda e: e.tensor_copy(wendb_, wend_), [sk], [sk])
        dve(lambda e: e.tensor_tensor(decay_, mu_p, mu_c_, ALU.subtract), ['mu_p', sk], [sk])
        act(decay_, decay_, AF.Exp, [sk], [sk])
        if full:
            dve(lambda e: e.tensor_tensor(winter_, mu_p, gtok_[:, 4:8], ALU.subtract), ['mu_p', sk], [sk])
            act(winter_, winter_, AF.Exp, [sk], [sk])
            act(enm_, gtok_[:, 8:12], AF.Exp, [sk], [sk])
        dve(lambda e: e.tensor_copy(mu_p, mu_c_), [sk], ['mu_p'])
        yield
        if full:
            proj(1, C_O, 512)
            act(sigo_s[S], psb[1][:, :], AF.Exp, ['b1'], ['sigo%d' % S], scale=-1.0)
            act(sigo_s[S], sigo_s[S], AF.Ln, ['sigo%d' % S], ['sigo%d' % S], bias=1.0)
            act(sigo_s[S], sigo_s[S], AF.Exp, ['sigo%d' % S], ['sigo%d' % S], scale=-1.0)
            dve(lambda e: e.tensor_tensor(sigo_s[S], sigo_s[S], gonbS, ALU.mult), ['sigo%d' % S, 'gonbS'], ['sigo%d' % S])
            for h in range(4):
                for dc in range(8):
                    mm(psb[2][:, h * 128:(h + 1) * 128], winv[:, dc, C_Q + h * 128:C_Q + (h + 1) * 128], hcols(dc),
                       dc == 0, dc == 7, ['win_q', hk], ['b2'])
            act(qT_s[S], v4(psb[2][:, :]), AF.Copy, ['b2'], ['qT%d' % S])
            yield
            for h in range(4):
                for dc in range(8):
                    mm(psb[1][:, h * 128:(h + 1) * 128], winv[:, dc, C_K + h * 128:C_K + (h + 1) * 128], hcols(dc),
                       dc == 0, dc == 7, ['win_k', hk], ['b1'])
            act(kT_s[S], v4(psb[1][:, :]), AF.Copy, ['b1'], ['kT%d' % S], scale=128.0 ** -0.5)
            for h in range(4):
                mm(psb[2][:, h * 128:(h + 1) * 128], selv[:, h, :], G_, True, False, ['selT'] + GK, ['b2'])
                mm(psb[2][:, h * 128:(h + 1) * 128], identf[:, :], maskbig[:, :], False, True, ['identf', 'maskbig'], ['b2'])
            dk = 'DT0' if S == 0 else 'gainB1'
            for h in range(4):
                act(DT_s[S][:, h, :], psb[2][:, h * 128:(h + 1) * 128], AF.Exp, ['b2', sk], [dk], bias=gtok_[:, h:h + 1], scale=-1.0)
            yield
        if full or kv_need:
            proj(1, C_KA, 256)
            kaps = psb[1][:, 0:128]
            act(qsqS[:, 0:128], kaps, AF.Square, ['b1'], ['qsqS'])
            dve(lambda e: e.tensor_reduce(ssk, qsqS[:, 0:128].rearrange("p (h n) -> p h n", n=64), AX.X, ALU.add), ['qsqS'], ['ssk'])
            act(rk, ssk, AF.Ln, ['ssk', 'epsb'], ['rk'], bias=epsb[:, :], scale=1.0 / 64)
            act(rk, rk, AF.Exp, ['rk'], ['rk'], scale=-0.5)
            for h in range(2):
                dve(lambda e, h=h: e.scalar_tensor_tensor(knfS[:, h * 64:(h + 1) * 64], kaps[:, h * 64:(h + 1) * 64], rk[:, h:h + 1], gkb,
                                                         ALU.mult, ALU.mult), ['b1', 'rk', 'gkb'], ['knS'])
            dve(lambda e: e.tensor_copy(knbS, knfS), ['knS'], ['knS'])
            dve(lambda e: e.tensor_copy(vA3[K3], psb[1][:, 128:256]), ['b1'], ['vA%d' % K3])
            if kv_need:
                dve(lambda e: e.tensor_copy(kvout[:, 0:128], knfS), ['knS'], ['sg1'])
                dve(lambda e: e.tensor_copy(kvout[:, 128:256], psb[1][:, 128:256]), ['b1'], ['sg1'])
            tpk = psb[2][:, :].bitcast(BF16)
            tr(tpk[:, 0:128], knbS, identb[:, :], ['knS', 'identb'], ['b2'])
            act(kaT3[K3], tpk[:, 0:128], AF.Copy, ['b2'], ['kaT%d' % K3])
            yield
        if full:
            proj(1, C_QA, 512)
            qaps = psb[1][:, :]
            act(qsqS, qaps, AF.Square, ['b1'], ['qsqS'])
            dve(lambda e: e.tensor_reduce(ssq, qsqS.rearrange("p (h n) -> p h n", n=64), AX.X, ALU.add), ['qsqS'], ['ssq'])
            act(rq, ssq, AF.Ln, ['ssq', 'epsb'], ['rq'], bias=epsb[:, :], scale=1.0 / 64)
            act(rq, rq, AF.Exp, ['rq'], ['rq'], scale=-0.5)
            for h in range(8):
                qo = (h % 4) * 128 + (h // 4) * 64
                dve(lambda e, h=h, qo=qo: e.scalar_tensor_tensor(qnS[:, qo:qo + 64], qaps[:, h * 64:(h + 1) * 64], rq[:, h:h + 1], gqb,
                                                                ALU.mult, ALU.mult), ['b1', 'rq', 'gqb'], ['qnS'])
            yield
            tpq = psb[2][:, :].bitcast(BF16).rearrange("p (c n) -> p c n", n=128)
            for p_ in range(4):
                tr(tpq[:, p_, :], qnS[:, p_ * 128:(p_ + 1) * 128], identb[:, :], ['qnS', 'identb'], ['b2'])
            act(qaT_s[S], tpq[:, 0:4, :], AF.Copy, ['b2'], ['qaT%d' % S])
            yield

    def stage2m(c, full, gc):
        S = c % 2
        b0 = psb[3]
        sc = SC[S]; sk = 'SC%d' % S
        wend_ = sc[:, 16:20]; winter_ = sc[:, 20:24]; enm_ = sc[:, 24:28]
        decay_ = sc[:, 28:32]; wendb_ = sc[:, 32:34].bitcast(BF16)
        kM_, vaug_, qT_, kT_, sigo_, DT_ = kM_s[S], vaug_s[S], qT_s[S], kT_s[S], sigo_s[S], DT_s[S]
        kMk, vak, qTk, kTk, sgk = 'kM%d' % S, 'vaug%d' % S, 'qT%d' % S, 'kT%d' % S, 'sigo%d' % S
        dk = 'DT0' if S == 0 else 'gainB1'
        dve(lambda e: e.tensor_tensor(vw, vaug_, wend_.unsqueeze(2).to_broadcast([128, 4, 128]), ALU.mult), [vak, sk], ['vw'])
        if full:
            for h in range(4):
                mm(psb[4][:, h * 128:(h + 1) * 128], kT_[:, h, :], qT_[:, h, :], True, True, [kTk, qTk], ['b4'])
            dve(lambda e: e.tensor_tensor(SwT, v4(psb[4][:, :]), DT_, ALU.mult), ['b4', dk], ['SwT'])
            yield
            for h in range(4):
                mm(psb[4][:, h * 128:(h + 1) * 128], SwT[:, h, :], vaug_[:, h, :], True, True, ['SwT', vak], ['b4'])
            for h in range(4):
                mm(b0[:, 280 + h:281 + h], SwT[:, h, :], onesb[:, 0:1], True, True, ['SwT', 'onesb'], ['b3'])
            for h in range(4):
                mm(psb[5][:, h * 128:(h + 1) * 128], qT_[:, h, :], CTb[:, h, :], True, True, [qTk, 'CTb'], ['b5'])
            for h in range(4):
                mm(b0[:, 284 + h:285 + h], qT_[:, h, :], nTb[:, h:h + 1], True, True, [qTk, 'nTb'], ['b3'])
            yield
            act(hnum, v4(psb[4][:, :]), AF.Copy, ['b4'], ['hnum'])
            for h in range(4):
                dve(lambda e, h=h: e.scalar_tensor_tensor(hnum[:, h, :], psb[5][:, h * 128:(h + 1) * 128], winter_[:, h:h + 1],
                                                         hnum[:, h, :], ALU.mult, ALU.add), ['b5', sk, 'hnum'], ['hnum'])
            dve(lambda e: e.tensor_copy(den, b0[:, 280:284]), ['b3'], ['den'])
            dve(lambda e: e.tensor_tensor(t4a, b0[:, 284:288], winter_, ALU.mult), ['b3', sk], ['t4a'])
            yield
        if not full:
            return
        dve(lambda e: e.tensor_tensor(den, den, t4a, ALU.add), ['den', 't4a'], ['den'])
        dve(lambda e: e.tensor_scalar(t4a, den, -1.0, None, ALU.mult), ['den'], ['t4a'])
        dve(lambda e: e.tensor_tensor(den, den, t4a, ALU.max), ['den', 't4a'], ['den'])
        dve(lambda e: e.tensor_tensor(den, den, enm_, ALU.max), ['den', sk], ['den'])
        dve(lambda e: e.reciprocal(den, den), ['den'], ['den'])
        yield
        for h in range(4):
            act(qsq[:, 0:128], hnum[:, h, :], AF.Square, ['hnum', 'den'], ['sg0', 'ssh%d' % h], scale=den[:, h:h + 1], accum=ssh[:, h:h + 1])
        act(t4b, ssh, AF.Ln, ['ssh0', 'ssh1', 'ssh2', 'ssh3', 'epsb'], ['t4b'], bias=epsb[:, :], scale=1.0 / 128)
        act(t4b, t4b, AF.Exp, ['t4b'], ['t4b'], scale=-0.5)
        dve(lambda e: e.tensor_tensor(comb, t4b, den, ALU.mult), ['t4b', 'den'], ['comb'])
        yield
        for h in range(4):
            dve(lambda e, h=h: e.scalar_tensor_tensor(ybf[:, h * 128:(h + 1) * 128], hnum[:, h, :], comb[:, h:h + 1],
                                                     sigo_[:, h * 128:(h + 1) * 128], ALU.mult, ALU.mult), ['hnum', 'comb', sgk], ['ybm'])
        yield
        for h in range(4):
            mm(psb[4][:, h * 128:(h + 1) * 128], kM_[:, h * 128:(h + 1) * 128], vw[:, h, :], True, True, [kMk, 'vw'], ['b4'])
        for h in range(4):
            mm(b0[:, 288 + h:289 + h], kM_[:, h * 128:(h + 1) * 128], wendb_[:, h:h + 1], True, True, [kMk, sk], ['b3'])
        for h in range(4):
            dve(lambda e, h=h: e.scalar_tensor_tensor(CTv[:, h, :], CTv[:, h, :], decay_[:, h:h + 1], psb[4][:, h * 128:(h + 1) * 128],
                                                     ALU.mult, ALU.add), ['CT', sk, 'b4'], ['CT'])
        act(CTb, CTv, AF.Copy, ['CT'], ['CTb'])
        dve(lambda e: e.tensor_tensor(nT, nT, decay_, ALU.mult), ['nT', sk], ['nT'])
        dve(lambda e: e.tensor_tensor(nT, nT, b0[:, 288:292], ALU.add), ['nT', 'b3'], ['nT'])
        dve(lambda e: e.tensor_copy(nTb, nT), ['nT'], ['nTb'])
        yield

    def stage2s(c, first_block, gc):
        S = c % 2
        qaT_ = qaT_s[S]; qaTk = 'qaT%d' % S
        cur, prv = gc % 3, (gc - 1) % 3
        msk = swam0 if first_block else swam
        mkey = 'swam0' if first_block else 'swam'
        msk2 = msk[:, :].unsqueeze(1).to_broadcast([128, 2, 256])
        for half in range(2):
            pr = slice(half * 64, (half + 1) * 64)
            hs_ = slice(4 * half, 4 * half + 4)
            for g_ in range(4):
                bank = psb[6 + g_ // 2]; off = (g_ % 2) * 256; bk = 'b%d' % (6 + g_ // 2)
                mm(bank[:, off:off + 128], qaT_[pr, g_, :], kaT3[prv][pr, :], True, True, [qaTk, 'kaT%d' % prv], [bk])
                mm(bank[:, off + 128:off + 256], qaT_[pr, g_, :], kaT3[cur][pr, :], True, True, [qaTk, 'kaT%d' % cur], [bk])
            for j in range(2):
                dve(lambda e, j=j: e.scalar_tensor_tensor(smv4[:, 2 * j:2 * j + 2, :], psb[6 + j][:, :].rearrange("p (h n) -> p h n", n=256),
                                                         0.125, msk2, ALU.mult, ALU.add), ['b%d' % (6 + j), mkey], ['tmpA'])
            dve(lambda e, hs_=hs_: e.tensor_reduce(mx[:, hs_], smv4, AX.X, ALU.max, negate=True), ['tmpA'], ['mx'])
            dve(lambda e, hs_=hs_: e.tensor_tensor(negm2[:, hs_], mx[:, hs_], nsnkb[:, hs_], ALU.min), ['mx', 'nsnkb'], ['negm2'])
            yield
            for g_ in range(4):
                hd = 4 * half + g_
                act(Pbv4[:, g_, :], smv4[:, g_, :], AF.Exp, ['tmpA', 'negm2'], ['Pb', 'rsum%d' % hd], bias=negm2[:, hd:hd + 1],
                    accum=rsum[:, hd:hd + 1])
            yield
            tpp = psb[6][:, :].bitcast(BF16).rearrange("p (h s n) -> p h s n", h=4, s=2)
            for g_ in range(4):
                for s_ in range(2):
                    tr(tpp[:, g_, s_, :], Pbv4[:, g_, s_ * 128:(s_ + 1) * 128], identb[:, :], ['Pb', 'identb'], ['b6'])
            act(PTv4, tpp, AF.Copy, ['b6'], ['PT'])
            dve(lambda e, hs_=hs_: e.tensor_tensor(esnk[:, hs_], snkb[:, hs_], negm2[:, hs_], ALU.add), ['snkb', 'negm2'], ['esnk'])
            act(esnk[:, hs_], esnk[:, hs_], AF.Exp, ['esnk'], ['esnk'])
            dve(lambda e, hs_=hs_: e.tensor_tensor(esnk[:, hs_], esnk[:, hs_], rsum[:, hs_], ALU.add),
                ['esnk'] + ['rsum%d' % h for h in range(4 * half, 4 * half + 4)], ['esnk'])
            dve(lambda e, hs_=hs_: e.reciprocal(esnk[:, hs_], esnk[:, hs_]), ['esnk'], ['esnk'])
            yield
            ops_ = psb[3][:, 0:256]
            for g_ in range(4):
                mm(ops_[:, g_ * 64:(g_ + 1) * 64], PTv4[:, g_, 0, :], vA3[prv][:, half * 64:(half + 1) * 64], True, False,
                   ['PT', 'vA%d' % prv], ['b3'])
                mm(ops_[:, g_ * 64:(g_ + 1) * 64], PTv4[:, g_, 1, :], vA3[cur][:, half * 64:(half + 1) * 64], False, True,
                   ['PT', 'vA%d' % cur], ['b3'])
            dve(lambda e, hs_=hs_, half=half, ops_=ops_: e.tensor_tensor(
                ybf[:, 512 + half * 256:512 + (half + 1) * 256].rearrange("p (h n) -> p h n", n=64),
                ops_.rearrange("p (h n) -> p h n", n=64), esnk[:, hs_].unsqueeze(2).to_broadcast([128, 4, 64]), ALU.mult),
                ['b3', 'esnk'], ['yba'])
            yield

    def stage2o(c, xap, xkey):
        tpy = psb[1][:, :].bitcast(BF16).rearrange("p (c n) -> p c n", n=128)
        for fc in range(8):
            tr(tpy[:, fc, :], ybf[:, fc * 128:(fc + 1) * 128], identb[:, :], ['ybm', 'yba', 'identb'], ['b1'])
        act(yTt, tpy, AF.Copy, ['b1'], ['hn1'])
        yield
        for half in range(2):
            for fc in range(8):
                mm(psb[1 + half][:, :], yTt[:, fc, :], woutv[:, fc, half * 512:(half + 1) * 512], fc == 0, fc == 7,
                   ['hn1', 'wout'], ['b%d' % (1 + half)])
        for half in range(2):
            xh = xap[:, half * 512:(half + 1) * 512]
            pb = psb[1 + half][:, :]
            dve(lambda e, xh=xh, pb=pb: e.tensor_tensor(xh, xh, pb, ALU.add), ['b%d' % (1 + half), xkey], [xkey])
        yield

    kMA = [[kM_s[0], kM_s[1]], [mixb[:, 1536:2048], mixb[:, 2048:2560]]]
    kMAk = [['kM0', 'kM1'], ['qT0', 'kT0']]
    vaugA = [[vaug_s[0], vaug_s[1]], [qaT_s[0], qaT_s[1]]]
    vaugAk = [['vaug0', 'vaug1'], ['qaT0', 'qaT1']]
    vwA = [vw, SwT]
    vwAk = ['vw', 'SwT']

    def stage1A(t2, kv_need):
        S = t2 % 2
        b0 = psb[0]
        sc = SC[S]; sk = 'SC%d' % S
        atok = sc[:, 0:8]; mu_ = sc[:, 8:12]; wend_ = sc[:, 12:20]; decay_ = sc[:, 20:24]
        wendb_ = sc[:, 24:28].bitcast(BF16)
        hkk = ['hTr0', 'hTr1']
        for k_, cc in ((0, C_I), (1, C_F)):
            for sub in range(2):
                for dc in range(8):
                    mm(b0[0:4, k_ * 256 + sub * 128:k_ * 256 + (sub + 1) * 128], winv[:, dc, cc:cc + 4], hTr[sub][:, dc, :],
                       dc == 0, dc == 7, ['win_g', hkk[sub]], ['b0'])
        e_, B_, a_, G_, nm_, on_ = [gtsA[:, i, :] for i in range(6)]
        GK = ['gts2_0', 'gts2_1']
        dve(lambda e: e.memset(on_, 1.0), [], GK)
        act(e_, b0[0:4, 256:512], AF.Exp, ['b0', 'car2'], GK, bias=car[:, 2:3], scale=-1.0)
        act(e_, e_, AF.Ln, GK, GK, bias=1.0)
        dve(lambda e: e.tensor_scalar(e_, e_, car2[:, 0:1], None, ALU.mult), GK + ['car4'], GK)
        dve(lambda e: e.tensor_tensor_scan(B_, on_, e_, car[:, 0:1], ALU.mult, ALU.add), GK + ['Bcar'], GK)
        dve(lambda e: e.scalar_tensor_tensor(a_, b0[0:4, 0:256], car[:, 3:4], B_, ALU.add, ALU.subtract), ['b0', 'car3'] + GK, GK)
        dve(lambda e: e.tensor_tensor_scan(G_, a_, a_, car[:, 1:2], ALU.max, ALU.max), GK + ['Gcar'], GK)
        dve(lambda e: e.tensor_copy(car[:, 0:1], B_[:, 255:256]), GK, ['Bcar'])
        dve(lambda e: e.tensor_copy(car[:, 1:2], G_[:, 255:256]), GK, ['Gcar'])
        yield
        for sub in range(2):
            hk = hkk[sub]
            for (bk, c0, dst, dkey, isk) in ((1, C_K, kMA[S][sub], kMAk[S][sub], True), (2, C_V, vaugA[S][sub], vaugAk[S][sub], False)):
                for dc in range(8):
                    mm(psb[bk][:, 0:512], hTr[sub][:, dc, :], winv[:, dc, c0:c0 + 512], dc == 0, dc == 7, [wkey(c0), hk], ['b%d' % bk])
                if isk:
                    act(dst, psb[bk][:, :], AF.Copy, ['b%d' % bk], [dkey], scale=128.0 ** -0.5)
                else:
                    dve(lambda e, dst=dst, bk=bk: e.tensor_copy(dst, v4(psb[bk][:, :])), ['b%d' % bk], [dkey])
            yield
        b3 = psb[3]
        for sub in range(2):
            tr(b3[:, sub * 4:sub * 4 + 4], a_[:, sub * 128:(sub + 1) * 128], identf[0:4, 0:4], GK + ['identf'], ['b3'])
        for h in range(4):
            mm(b3[:, 8 + h:9 + h], selv[:, h, :], G_[:, 255:256], True, True, ['selT'] + GK, ['b3'])
        dve(lambda e: e.tensor_copy(sc[:, 0:12], b3[:, 0:12]), ['b3'], [sk])
        dve(lambda e: e.tensor_tensor(wend_.rearrange("p (s h) -> p s h", h=4), atok.rearrange("p (s h) -> p s h", h=4),
                                      mu_.unsqueeze(1).to_broadcast([128, 2, 4]), ALU.subtract), [sk], [sk])
        act(wend_, wend_, AF.Exp, [sk], [sk])
        dve(lambda e: e.tensor_copy(wendb_, wend_), [sk], [sk])
        dve(lambda e: e.tensor_tensor(decay_, mu_p, mu_, ALU.subtract), ['mu_p', sk], [sk])
        act(decay_, decay_, AF.Exp, [sk], [sk])
        dve(lambda e: e.tensor_copy(mu_p, mu_), [sk], ['mu_p'])
        yield
        if kv_need:
            gc = NCH - 1
            K3 = gc % 3
            sub = 1
            for dc in range(8):
                mm(psb[1][:, 0:256], hTr[sub][:, dc, :], winv[:, dc, C_KA:C_KA + 256], dc == 0, dc == 7, ['win_ka', hkk[sub]], ['b1'])
            kaps = psb[1][:, 0:128]
            act(qsqS[:, 0:128], kaps, AF.Square, ['b1'], ['qsqS'])
            dve(lambda e: e.tensor_reduce(ssk, qsqS[:, 0:128].rearrange("p (h n) -> p h n", n=64), AX.X, ALU.add), ['qsqS'], ['ssk'])
            act(rk, ssk, AF.Ln, ['ssk', 'epsb'], ['rk'], bias=epsb[:, :], scale=1.0 / 64)
            act(rk, rk, AF.Exp, ['rk'], ['rk'], scale=-0.5)
            for h in range(2):
                dve(lambda e, h=h: e.scalar_tensor_tensor(knfS[:, h * 64:(h + 1) * 64], kaps[:, h * 64:(h + 1) * 64], rk[:, h:h + 1], gkb,
                                                         ALU.mult, ALU.mult), ['b1', 'rk', 'gkb'], ['knS'])
            dve(lambda e: e.tensor_copy(knbS, knfS), ['knS'], ['knS'])
            dve(lambda e: e.tensor_copy(vA3[K3], psb[1][:, 128:256]), ['b1'], ['vA%d' % K3])
            dve(lambda e: e.tensor_copy(kvout[:, 0:128], knfS), ['knS'], ['sg1'])
            dve(lambda e: e.tensor_copy(kvout[:, 128:256], psb[1][:, 128:256]), ['b1'], ['sg1'])
            tpk = psb[2][:, :].bitcast(BF16)
            tr(tpk[:, 0:128], knbS, identb[:, :], ['knS', 'identb'], ['b2'])
            act(kaT3[K3], tpk[:, 0:128], AF.Copy, ['b2'], ['kaT%d' % K3])
            yield

    def stage2A(t2):
        S = t2 % 2
        b3 = psb[3]
        sc = SC[S]; sk = 'SC%d' % S
        wend_ = sc[:, 12:20]; decay_ = sc[:, 20:24]; wendb_ = sc[:, 24:28].bitcast(BF16)
        for sub in range(2):
            dve(lambda e, sub=sub: e.tensor_tensor(vwA[sub], vaugA[S][sub], wend_[:, sub * 4:sub * 4 + 4].unsqueeze(2).to_broadcast([128, 4, 128]),
                                                  ALU.mult), [vaugAk[S][sub], sk], [vwAk[sub]])
        yield
        for h in range(4):
            for sub in range(2):
                mm(psb[4][:, h * 128:(h + 1) * 128], kMA[S][sub][:, h * 128:(h + 1) * 128], vwA[sub][:, h, :], sub == 0, sub == 1,
                   [kMAk[S][sub], vwAk[sub]], ['b4'])
        for h in range(4):
            for sub in range(2):
                mm(b3[:, 288 + h:289 + h], kMA[S][sub][:, h * 128:(h + 1) * 128], wendb_[:, sub * 4 + h:sub * 4 + h + 1], sub == 0, sub == 1,
                   [kMAk[S][sub], sk], ['b3'])
        for h in range(4):
            dve(lambda e, h=h: e.scalar_tensor_tensor(CTv[:, h, :], CTv[:, h, :], decay_[:, h:h + 1], psb[4][:, h * 128:(h + 1) * 128],
                                                     ALU.mult, ALU.add), ['CT', sk, 'b4'], ['CT'])
        act(CTb, CTv, AF.Copy, ['CT'], ['CTb'])
        dve(lambda e: e.tensor_tensor(nT, nT, decay_, ALU.mult), ['nT', sk], ['nT'])
        dve(lambda e: e.tensor_tensor(nT, nT, b3[:, 288:292], ALU.add), ['nT', 'b3'], ['nT'])
        dve(lambda e: e.tensor_copy(nTb, nT), ['nT'], ['nTb'])
        yield

    def stage0A(t2):
        for sub in range(2):
            c = 2 * t2 + sub
            for _ in stage0(c, x[:, c, :], 'x%d' % c):
                yield

    def mixer_loop_A():
        pipe_begin()
        NSTEP = NCH // 2
        for t in range(NSTEP + 2):
            early = []
            rest = []
            if 0 <= t - 2 < NSTEP:
                early.append(stage2A(t - 2))
            if 0 <= t - 1 < NSTEP:
                rest.append(stage1A(t - 1, t - 1 == NSTEP - 1))
            if t < NSTEP:
                rest.append(stage0A(t))
            active = rest + early
            while active:
                for g in list(active):
                    try:
                        next(g)
                        if g in rest:
                            for _ in range(REST_STEPS - 1):
                                next(g)
                    except StopIteration:
                        active.remove(g)

    def mixer_loop(full):
        pipe_begin()
        if full:
            for _ in stage0(0, xs[:, :], 'xs', nt=NS, dst=hTs, dkey='hTs'):
                pass

        def rr(gens):
            gens = list(gens)
            while gens:
                for g in list(gens):
                    try:
                        next(g)
                    except StopIteration:
                        gens.remove(g)
        pending = []
        for t in range(NCH + 3):
            early = []
            late = []
            if t - 2 >= 0 and t - 2 < NCH:
                c = t - 2
                gcx = (NCH if full else 0) + c
                early.append(stage2m(c, full, gcx))
                if full:
                    early.append(stage2s(c, c == 0, gcx))
                    late.append(stage2o(c, x[:, c, :], 'x%d' % c))
            rest = []
            if t - 1 >= 0 and t - 1 < NCH:
                c = t - 1
                rest.append(stage1(c, full, c == NCH - 1, (NCH if full else 0) + c))
            if t < NCH:
                rest.append(stage0(t, x[:, t, :], 'x%d' % t))
            active = pending + rest + early
            pending = late
            late = []
            late_added = True
            rnd = 0
            while active or (late and not late_added):
                rnd += 1
                for g in list(active):
                    try:
                        next(g)
                        if g in rest and REST_STEPS > 1:
                            next(g)
                    except StopIteration:
                        active.remove(g)

    def sample_mixer():
        hs = lambda dc: hTs[:, dc, :]
        mixbf = mixb[:, :].bitcast(F32)
        vTs = mixbf[:, 0:64].rearrange("p (h b) -> p h b", b=NS)
        oTs = mixbf[:, 64:128].rearrange("p (h b) -> p h b", b=NS)
        Cq = mixbf[:, 128:192]; rep = mixbf[:, 192:448]; numT = mixbf[:, 448:512]; wv = mixbf[:, 512:576]
        sqT = mixbf[:, 576:640]; gonT = mixbf[:, 640:644]; gon4 = mixbf[0:4, 644:772]
        qx = mixbf[:, 772:836]; kxn = mixbf[:, 836:900]; vxn = mixbf[:, 900:964]; Sx = mixbf[:, 964:1092]
        Px = mixbf[:, 1092:1220]; ox = mixbf[:, 1220:1284]; sst = mixbf[:, 1284:1300]; junk = mixbf[:, 1300:1428]
        sinkx = mixbf[:, 1428:1429]
        yTs = smallb[:, 0:128].rearrange("p (c b) -> p c b", b=NS)
        ystok = mixbf[0:NS, 1432:1432 + 256].bitcast(BF16)
        qk_t = tmpB[0:NS, :]
        pa = tmpA[0:NS, :]
        qa_t = pa[:, 0:512]; ka_t = pa[:, 512:640]; va_t = pa[:, 640:768]; if_t = pa[:, 768:776]
        sc4 = lambda i: pa[:, 776 + 4 * i:780 + 4 * i]
        ip, lfm, mt, wi, we, enm_, qk_, nq_, sc_, den_, t1_, t2_ = [sc4(i) for i in range(12)]
        bib = sc4(12); bfb = sc4(13); m0t = sc4(14); repsrc = pa[:, 840:856]
        n0t = pa[:, 856:856 + 0]
        bigf = big[:, :].bitcast(F32)
        Cst = bigf[:, 0:2048].rearrange("p (j k) -> p j k", k=128)
        qrow = bigf[:, 2048:4096].rearrange("p (j k) -> p j k", k=128)
        krow = bigf[:, 4096:6144].rearrange("p (j k) -> p j k", k=128)
        prod = bigf[:, 6144:8192].rearrange("p (j k) -> p j k", k=128)
        n0s = bigf[0:NS, 8192:8704]; nns = bigf[0:NS, 8704:9216]; tq = bigf[0:NS, 9216:9728]
        qns = bigf[0:NS, 9728:10240]; kns = bigf[0:NS, 10240:10368]
        Kx = hT[:, 0:16384].bitcast(F32).rearrange("p (s d) -> p s d", d=64)
        HK = HKALL + PIPEKEYS
        def sproj(bk, c0, n):
            for dc in range(8):
                mm(psb[bk][0:NS, 0:n], hs(dc), winv[:, dc, c0:c0 + n], dc == 0, dc == 7, WINKEYS + ['hTs'], ['b%d' % bk])
        sproj(1, C_Q, 512)
        dve(lambda e: e.tensor_copy(qk_t[:, 0:512], psb[1][0:NS, :]), ['b1', 'hnum', 'sigo0'], ['hnum', 'sigo0'])
        sproj(2, C_K, 512)
        dve(lambda e: e.tensor_copy(qk_t[:, 512:1024], psb[2][0:NS, :]), ['b2'], ['hnum', 'sigo0'])
        sproj(3, C_QA, 512)
        dve(lambda e: e.tensor_copy(qa_t, psb[3][0:NS, :]), ['b3', 'tmpA'], ['tmpA'])
        sproj(4, C_KA, 256)
        dve(lambda e: e.tensor_copy(pa[:, 512:768], psb[4][0:NS, 0:256]), ['b4'], ['tmpA'])
        sproj(5, C_I, 8)
        dve(lambda e: e.tensor_copy(if_t, psb[5][0:NS, 0:8]), ['b5'], ['tmpA'])
        for k_, cc, dst in ((0, C_V, vTs), (1, C_O, oTs)):
            for h in range(4):
                for dc in range(8):
                    mm(psb[6 + k_][:, h * NS:(h + 1) * NS], winv[:, dc, cc + h * 128:cc + (h + 1) * 128], hs(dc), dc == 0, dc == 7,
                       WINKEYS + ['hTs'], ['b%d' % (6 + k_)])
        dve(lambda e: e.tensor_copy(mixbf[:, 0:64], psb[6][:, 0:64]), ['b6', 'kM0', 'vaug0', 'vw', 'qT0', 'kT0', 'SwT'], ['mixb'])
        act(mixbf[:, 64:128], psb[7][:, 0:64], AF.Exp, ['b7'], ['mixb'], scale=-1.0)
        act(mixbf[:, 64:128], mixbf[:, 64:128], AF.Ln, ['mixb'], ['mixb'], bias=1.0)
        act(mixbf[:, 64:128], mixbf[:, 64:128], AF.Exp, ['mixb'], ['mixb'], scale=-1.0)
        dma('sp', bib, bi_d.to_broadcast([NS, 4]), ['tmpA'], ['tmpA'], 'G_sin')
        dma('sp', bfb, bf_d.to_broadcast([NS, 4]), ['tmpA'], ['tmpA'], 'G_sin')
        dma('sp', m0t, sm0_d, ['tmpA'], ['tmpA'], 'G_sin')
        dma('sp', n0s, sn0_d, [], WINKEYS + ACTKEYS + ['bigs'], 'sio')
        T = ['tmpA']
        dve(lambda e: e.tensor_tensor(ip, if_t[:, 0:4], bib, ALU.add), T, T)
        dve(lambda e: e.tensor_tensor(t1_, if_t[:, 4:8], bfb, ALU.add), T, T)
        act(t1_, t1_, AF.Exp, T, T, scale=-1.0)
        act(t1_, t1_, AF.Ln, T, T, bias=1.0)
        dve(lambda e: e.tensor_tensor(lfm, m0t, t1_, ALU.subtract), T, T)
        dve(lambda e: e.tensor_tensor(mt, lfm, ip, ALU.max), T, T)
        dve(lambda e: e.tensor_tensor(wi, lfm, mt, ALU.subtract), T, T)
        act(wi, wi, AF.Exp, T, T)
        dve(lambda e: e.tensor_tensor(we, ip, mt, ALU.subtract), T, T)
        act(we, we, AF.Exp, T, T)
        act(enm_, mt, AF.Exp, T, T, scale=-1.0)
        dve(lambda e: e.tensor_tensor(tq, qk_t[:, 0:512], qk_t[:, 512:1024], ALU.mult), ['hnum', 'sigo0', 'bigs'], ['bigs'])
        dve(lambda e: e.tensor_reduce(qk_, tq.rearrange("p (h k) -> p h k", k=128), AX.X, ALU.add), ['bigs'] + T, T)
        dve(lambda e: e.tensor_tensor(tq, qk_t[:, 0:512], n0s, ALU.mult), ['hnum', 'sigo0', 'bigs'] + T, ['bigs'])
        dve(lambda e: e.tensor_reduce(nq_, tq.rearrange("p (h k) -> p h k", k=128), AX.X, ALU.add), ['bigs'] + T, T)
        dve(lambda e: e.tensor_scalar(t2_, we, 128.0 ** -0.5, None, ALU.mult), T, T)
        dve(lambda e: e.tensor_tensor(sc_, qk_, t2_, ALU.mult), T, T)
        dve(lambda e: e.tensor_tensor(den_, wi, nq_, ALU.mult), T, T)
        dve(lambda e: e.tensor_tensor(den_, den_, sc_, ALU.add), T, T)
        dve(lambda e: e.tensor_scalar(t1_, den_, -1.0, None, ALU.mult), T, T)
        dve(lambda e: e.tensor_tensor(den_, den_, t1_, ALU.max), T, T)
        dve(lambda e: e.tensor_tensor(den_, den_, enm_, ALU.max), T, T)
        dve(lambda e: e.reciprocal(den_, den_), T, T)
        for h in range(4):
            dve(lambda e, h=h: e.tensor_scalar(nns[:, h * 128:(h + 1) * 128], qk_t[:, 512 + h * 128:512 + (h + 1) * 128], t2_[:, h:h + 1], None, ALU.mult),
                T + ['hnum', 'sigo0', 'bigs'], ['bigs'])
            dve(lambda e, h=h: e.scalar_tensor_tensor(nns[:, h * 128:(h + 1) * 128], n0s[:, h * 128:(h + 1) * 128], wi[:, h:h + 1],
                                                     nns[:, h * 128:(h + 1) * 128], ALU.mult, ALU.add), T + ['bigs'], ['bigs'])
        dma('sp', sn_d, nns, ['bigs'], [], 'sout')
        dma('sp', sm_d, mt, T, [], 'sout')
        for i_, src in enumerate((wi, t2_, sc_, den_)):
            dve(lambda e, i_=i_, src=src: e.tensor_copy(repsrc[:, 4 * i_:4 * i_ + 4], src), T, T)
        scr2 = scr_d[2]
        dma('sp', scr2[:, 0:16], repsrc, T, ['scr2'], 'sio')
        dma('sp', rep.rearrange("p (b s) -> p b s", s=16), scr2[:, 0:16].rearrange("(o b) s -> o b s", o=1).to_broadcast([128, NS, 16]),
            ['scr2', 'mixb'], ['mixb'], 'sio')
        rep3 = rep.rearrange("p (b s) -> p b s", s=16)
        R = lambda s: rep3[:, :, 4 * s:4 * s + 4]
        vT_bh = vTs.rearrange("p h b -> p b h"); oT_bh = oTs.rearrange("p h b -> p b h")
        Cq3 = Cq.rearrange("p (b h) -> p b h", h=4); wv3 = wv.rearrange("p (b h) -> p b h", h=4)
        num3 = numT.rearrange("p (b h) -> p b h", h=4); sq3 = sqT.rearrange("p (b h) -> p b h", h=4)
        M = ['mixb']
        dve(lambda e: e.tensor_tensor(wv3, R(1), vT_bh, ALU.mult), M, M)
        def gen_ml():
            CstS = [bigf[:, s_ * 1024:(s_ + 1) * 1024].rearrange("p (j k) -> p j k", k=128) for s_ in range(2)]
            prod2 = bigf[:, 2048:3072].rearrange("p (j k) -> p j k", k=128)
            QmS = [bigf[0:NS, 3072 + s_ * 512:3584 + s_ * 512].bitcast(BF16).rearrange("p (l f) -> p l f", f=512) for s_ in range(2)]
            KmS = [bigf[0:NS, 4096 + s_ * 512:4608 + s_ * 512].bitcast(BF16).rearrange("p (l f) -> p l f", f=512) for s_ in range(2)]
            qkb = bigf[0:NS, 5120:5632].bitcast(BF16)
            dve(lambda e: e.tensor_copy(qkb, qk_t), ['hnum', 'sigo0', 'bigs'], ['qkb'])
            for bt in range(8):
                b0_ = bt * 2
                s_ = bt % 2
                ck = 'Cst%d' % s_
                dma('sp', CstS[s_], sC0_d[b0_:b0_ + 2].rearrange("b h v k -> v (b h) k"), ['bigs'], [ck], ck)
                selc = identf[0:NS, b0_:b0_ + 2].unsqueeze(2).to_broadcast([NS, 2, 512])
                dve(lambda e, s_=s_, selc=selc: e.tensor_tensor(QmS[s_], qkb[:, 0:512].unsqueeze(1).to_broadcast([NS, 2, 512]), selc, ALU.mult),
                    ['qkb', 'identf'], ['Qm%d' % s_])
                dve(lambda e, s_=s_, selc=selc: e.tensor_tensor(KmS[s_], qkb[:, 512:1024].unsqueeze(1).to_broadcast([NS, 2, 512]), selc, ALU.mult),
                    ['qkb', 'identf'], ['Km%d' % s_])
                for l_ in range(2):
                    mm(psb[l_][:, :], onesb[0:NS, :], QmS[s_][:, l_, :], True, True, ['onesb', 'Qm%d' % s_], ['b%d' % l_])
                for l_ in range(2):
                    mm(psb[2 + l_][:, :], onesb[0:NS, :], KmS[s_][:, l_, :], True, True, ['onesb', 'Km%d' % s_], ['b%d' % (2 + l_)])
                for l_ in range(2):
                    dve(lambda e, s_=s_, l_=l_: e.tensor_tensor(prod2[:, 4 * l_:4 * l_ + 4, :], CstS[s_][:, 4 * l_:4 * l_ + 4, :],
                                                               psb[l_][:, :].rearrange("p (j k) -> p j k", k=128), ALU.mult),
                        [ck, 'b%d' % l_], ['prod'])
                dve(lambda e, bt=bt: e.tensor_reduce(Cq[:, bt * 8:(bt + 1) * 8], prod2, AX.X, ALU.add), ['prod'] + M, M)
                for j in range(8):
                    jj = bt * 8 + j
                    kr = psb[2 + j // 4][:, (j % 4) * 128:(j % 4 + 1) * 128]
                    dve(lambda e, j=j, jj=jj, s_=s_: e.tensor_scalar(CstS[s_][:, j, :], CstS[s_][:, j, :], rep3[:, jj // 4, jj % 4:jj % 4 + 1], None,
                                                                     ALU.mult), [ck] + M, [ck])
                    dve(lambda e, j=j, jj=jj, s_=s_, kr=kr: e.scalar_tensor_tensor(CstS[s_][:, j, :], kr, wv[:, jj:jj + 1], CstS[s_][:, j, :],
                                                                                   ALU.mult, ALU.add), [ck, 'b%d' % (2 + j // 4)] + M, [ck])
                dma('pool', sC_d[b0_:b0_ + 2].rearrange("b h v k -> v (b h) k"), CstS[s_], [ck], [], 'cout%d' % s_)
                yield
            dve(lambda e: e.tensor_tensor(num3, R(0), Cq3, ALU.mult), M, M)
            dve(lambda e: e.tensor_tensor(sq3, R(2), vT_bh, ALU.mult), M, M)
            dve(lambda e: e.tensor_tensor(numT, numT, sqT, ALU.add), M, M)
            dve(lambda e: e.tensor_tensor(num3, num3, R(3), ALU.mult), M, M)
            dve(lambda e: e.tensor_tensor(sqT, numT, numT, ALU.mult), M, M)
            yield
            mm(psb[1][:, 0:64], onesf[:, :], sqT, True, True, ['onesf'] + M, ['b1'])
            act(sqT, psb[1][:, 0:64], AF.Ln, ['b1', 'epsb'], M, bias=epsb[:, :], scale=1.0 / 128)
            act(sqT, sqT, AF.Exp, M, M, scale=-0.5)
            dve(lambda e: e.tensor_tensor(numT, numT, sqT, ALU.mult), M, M)
            dve(lambda e: e.tensor_tensor(num3, num3, oT_bh, ALU.mult), M, M)
            dma('sp', gon4, gon_d.rearrange("o (h v) -> (o h) v", h=4), M, M, 'sio')
            tr(psb[2][:, 0:4], gon4, identf[0:4, 0:4], M + ['identf'], ['b2'])
            dve(lambda e: e.tensor_copy(gonT, psb[2][:, 0:4]), ['b2'], M)
            for h in range(4):
                dve(lambda e, h=h: e.tensor_scalar(yTs[:, h, :], num3[:, :, h], gonT[:, h:h + 1], None, ALU.mult), M + ['CTb'], ['yTs'])
            yield

        def gen_sw():
            for nm_, src, n_, gk_, dst in (('q', qa_t, 8, gqb, qns), ('k', ka_t, 2, gkb, kns)):
                w_ = n_ * 64
                dve(lambda e, src=src, w_=w_: e.tensor_tensor(tq[:, 0:w_], src, src, ALU.mult), T + ['qns'], ['qns'])
                dve(lambda e, n_=n_, w_=w_: e.tensor_reduce(sst[0:NS, 0:n_], tq[:, 0:w_].rearrange("p (h n) -> p h n", n=64), AX.X, ALU.add),
                    ['qns'] + M, M)
                act(sst[0:NS, 0:n_], sst[0:NS, 0:n_], AF.Ln, M + ['epsb'], M, bias=epsb[0:NS, :], scale=1.0 / 64)
                act(sst[0:NS, 0:n_], sst[0:NS, 0:n_], AF.Exp, M, M, scale=-0.5)
                for h in range(n_):
                    dve(lambda e, h=h, src=src, dst=dst, gk_=gk_: e.scalar_tensor_tensor(
                        dst[:, h * 64:(h + 1) * 64], src[:, h * 64:(h + 1) * 64], sst[0:NS, h:h + 1], gk_[0:NS, :], ALU.mult, ALU.mult),
                        T + M + ['gqb', 'gkb', 'qns'], ['qns'])
            yield
            W = ['SW']
            R_ = hT[:, :]
            K_all = R_[:, 0:4096].bitcast(F32).rearrange("p (b f) -> p b f", f=128)
            V_all = R_[:, 4096:8192].bitcast(F32).rearrange("p (b f) -> p b f", f=128)
            KT = R_[:, 8192:10240].rearrange("p (b s) -> p b s", s=128)
            Vdup = R_[:, 10240:14336].rearrange("p (b k r d) -> p b k r d", k=2, r=2, d=64)
            PTb = R_[:, 14336:14464]; qsT = R_[:, 14464:14528].rearrange("p (g b) -> p g b", b=NS)
            STs = R_[:, 14528:14784].bitcast(F32); Pnb = R_[:, 14784:14912]; psel = R_[:, 14912:14944].bitcast(F32)
            qpm = bigf[0:NS, 10368:10624].bitcast(BF16)
            tq2 = bigf[0:NS, 10624:11136]; snw = bigf[0:NS, 11136:11144]; Ls = bigf[0:NS, 11144:11272]
            Vnd = bigf[0:NS, 11272:11400].bitcast(BF16).rearrange("p (k r d) -> p k r d", k=2, r=2)
            PnT = bigf[0:NS, 11400:11464].bitcast(BF16)
            snew = sst[:, 8:9]; mxx = sst[:, 9:10]; nmx = sst[:, 10:11]; rs_ = sst[:, 11:12]; pn_ = sst[:, 12:13]; es_ = sst[:, 13:14]
            dma('sp', sk_d[:, 0:127, :], cK_d[:, 1:128, :], [], [], 'sout')
            dma('sp', sv_d[:, 0:127, :], cV_d[:, 1:128, :], [], [], 'sout')
            dma('sp', sk_d[:, 127, :], kns, ['qns'], [], 'sout')
            dma('sp', sv_d[:, 127, :], va_t, T, [], 'sout')
            for b in range(NS):
                dma('sp', sinkx[b * 8:(b + 1) * 8, :], snk_d.rearrange("o h -> h o"), [], W, 'G_snk')
            dma('sp', K_all, cK_d.rearrange("b s f -> s b f"), [], HK + ['Kall'], 'kall')
            dma('sp', V_all, cV_d.rearrange("b s f -> s b f"), [], HK + ['Vall'], 'vall')
            dve(lambda e: e.tensor_copy(qpm.rearrange("p (g k d) -> p g k d", g=4, k=2), qns.rearrange("p (k g d) -> p g k d", k=2, g=4)),
                ['qns'], ['qpm'])
            tq_ps = psb[6][:, :].bitcast(BF16)
            for g_ in range(4):
                tr(tq_ps[:, g_ * NS:(g_ + 1) * NS], qpm[:, g_ * 128:(g_ + 1) * 128], identb[0:NS, 0:NS], ['qpm', 'identb'], ['b6'])
            act(qsT, tq_ps[:, 0:64].rearrange("p (g b) -> p g b", b=NS), AF.Copy, ['b6'] + HK, ['qsT'])
            dve(lambda e: e.tensor_tensor(tq2.rearrange("p (k g d) -> p k g d", k=2, g=4), qns.rearrange("p (k g d) -> p k g d", k=2, g=4),
                                          kns.rearrange("p (k d) -> p k d", d=64).unsqueeze(2).to_broadcast([NS, 2, 4, 64]), ALU.mult),
                ['qns'], ['tq2'])
            dve(lambda e: e.tensor_reduce(snw, tq2.rearrange("p (h d) -> p h d", d=64), AX.X, ALU.add), ['tq2'], ['snw'])
            dve(lambda e: e.tensor_tensor(Ls.rearrange("p (b h) -> p b h", h=8), snw.unsqueeze(1).to_broadcast([NS, NS, 8]),
                                          identf[0:NS, 0:NS].unsqueeze(2).to_broadcast([NS, NS, 8]), ALU.mult), ['snw', 'identf'], ['Ls'])
            mm(psb[6][:, 256:257], Ls, onesf[0:NS, 0:1], True, True, ['Ls', 'onesf'], ['b6'])
            dve(lambda e: e.tensor_copy(snew, psb[6][:, 256:257]), ['b6'], W)
            for r_ in range(2):
                dve(lambda e, r_=r_: e.tensor_copy(Vnd[:, :, r_, :], va_t.rearrange("p (k d) -> p k d", d=64)), T, ['Vnd'])
            yield
            for quad in range(4):
                bk = 4 + quad % 2
                for j in range(4):
                    tr(psb[bk][:, j * 128:(j + 1) * 128], K_all[:, quad * 4 + j, :], identf[:, :], ['Kall', 'identf'], ['b%d' % bk])
                act(KT[:, quad * 4:(quad + 1) * 4, :], psb[bk][:, :].rearrange("p (j s) -> p j s", s=128), AF.Copy, ['b%d' % bk] + HK, ['KT'])
            for r_ in range(2):
                dve(lambda e, r_=r_: e.tensor_copy(Vdup[:, :, :, r_, :], V_all.rearrange("p b (k d) -> p b k d", d=64)), ['Vall'] + HK, ['Vdup'])
            yield
            for kv in range(2):
                pr = slice(kv * 64, (kv + 1) * 64)
                for b in range(NS):
                    mm(psb[4 + kv][:, b * 4:b * 4 + 4], KT[pr, b, :], qsT[pr, :, b], True, True, ['KT', 'qsT'], ['b%d' % (4 + kv)])
            STs4 = STs.rearrange("p (b k g) -> p b k g", k=2, g=4)
            for kv in range(2):
                dve(lambda e, kv=kv: e.tensor_copy(STs4[:, :, kv, :], psb[4 + kv][:, 0:64].rearrange("p (b g) -> p b g", g=4)),
                    ['b%d' % (4 + kv)] + HK, ['STs'])
            yield
            S_ps = psb[6][:, 0:128]
            tr(S_ps, STs, identf[:, :], ['STs', 'identf'], ['b6'])
            dve(lambda e: e.tensor_reduce(mxx, S_ps, AX.X, ALU.max), ['b6'] + W, W)
            dve(lambda e: e.tensor_tensor(mxx, mxx, snew, ALU.max), W, W)
            dve(lambda e: e.tensor_scalar(mxx, mxx, 0.125, None, ALU.mult), W, W)
            dve(lambda e: e.tensor_tensor(mxx, mxx, sinkx, ALU.max), W, W)
            dve(lambda e: e.tensor_scalar(nmx, mxx, -1.0, None, ALU.mult), W, W)
            act(Px, S_ps, AF.Exp, ['b6'] + W, W, bias=nmx, scale=0.125, accum=rs_)
            act(pn_, snew, AF.Exp, W, W, bias=nmx, scale=0.125)
            act(es_, sinkx, AF.Exp, W, W, bias=nmx)
            dve(lambda e: e.tensor_tensor(rs_, rs_, pn_, ALU.add), W, W)
            dve(lambda e: e.tensor_tensor(rs_, rs_, es_, ALU.add), W, W)
            dve(lambda e: e.reciprocal(rs_, rs_), W, W)
            dve(lambda e: e.tensor_scalar(Pnb, Px, rs_, None, ALU.mult), W + HK, ['Pnb'])
            dve(lambda e: e.tensor_tensor(pn_, pn_, rs_, ALU.mult), W, W)
            dve(lambda e: e.tensor_scalar(psel, esel[:, :], pn_, None, ALU.mult), W + ['esel'] + HK, ['psel'])
            yield
            tp_ps = psb[6][:, 128:192].bitcast(BF16)
            tr(tp_ps, Pnb, identb[:, :], ['Pnb', 'identb'], ['b6'])
            act(PTb, tp_ps, AF.Copy, ['b6'] + HK, ['PTb'])
            tr(psb[7][0:NS, 0:128], psel, identf[:, :], ['psel', 'identf'], ['b7'])
            dve(lambda e: e.tensor_copy(PnT, psb[7][0:NS, 0:128]), ['b7'], ['PnT'])
            yield
            OT_ps = psb[4][:, 256:384]
            for b in range(NS):
                for kv in range(2):
                    c4 = b * 8 + kv * 4
                    mm(OT_ps[:, c4:c4 + 4], Vdup[:, b, kv, :, :].rearrange("p r d -> p (r d)"), PTb[:, c4:c4 + 4], True, False,
                       ['Vdup', 'PTb'], ['b4'])
                    mm(OT_ps[:, c4:c4 + 4], Vnd[:, kv, :, :].rearrange("p r d -> p (r d)"), PnT[:, c4:c4 + 4], False, True,
                       ['Vnd', 'PnT'], ['b4'])
            OTv = OT_ps.rearrange("p (b j r) -> p r j b", j=4, r=2)
            for r_ in range(2):
                pr = slice(r_ * 64, (r_ + 1) * 64)
                act(yTs[pr, 4:8, :], OTv[pr, r_, :, :], AF.Copy, ['b4'], ['yTs'])
            yield

        gens = [gen_ml(), gen_sw()]
        while gens:
            for g in list(gens):
                try:
                    next(g)
                except StopIteration:
                    gens.remove(g)
        for half in range(2):
            for fc in range(8):
                mm(psb[6 + half][0:NS, :], yTs[:, fc, :], woutv[:, fc, half * 512:(half + 1) * 512], fc == 0, fc == 7,
                   ['yTs', 'wout'], ['b%d' % (6 + half)])
        for half in range(2):
            xh = xs[:, half * 512:(half + 1) * 512]
            pb = psb[6 + half][0:NS, :]
            dve(lambda e, xh=xh, pb=pb: e.tensor_tensor(xh, xh, pb, ALU.add), ['b%d' % (6 + half), 'xs'], ['xs'])
        dve(lambda e: e.memset(rowst[0:NS, 0:1], 0.0), [], ['bigs', 'Cst', 'Cst0', 'Cst1', 'Qm0', 'Qm1', 'Km0', 'Km1', 'qkb', 'qrow', 'krow', 'prod', 'Kx', 'mixb', 'yTs', 'SW', 'qns', 'Kall', 'Vall', 'KT', 'Vdup', 'PTb', 'qsT', 'STs', 'Pnb', 'psel', 'qpm', 'tq2', 'snw', 'Ls', 'Vnd', 'PnT'] + ACTKEYS + HK + ['tmpA'])

    def mixer_done():
        pipe_end()
        dve(lambda e: e.memset(rowst[:, 0:1], 0.0), [],
            WINKEYS + ['wout', 'gainA', 'gainB', 'hn0', 'ybm', 'yba', 'hn1', 'sg0', 'sg1'] + ACTKEYS + WGKEYS + ['gain', 'tmpA'])

    consts_mixer()
    ybv = yb_d.rearrange("(c p) d -> p c d", p=128)
    for ph in range(2):
        full = ph == 1
        xv = (xb_d if full else xa_d).rearrange("(c p) d -> p c d", p=128)
        for c in range(NCH):
            dma('sp', x[:, c, :], xv[:, c, :], [], ['x%d' % c], 'xl%d' % c)
        subt = [(x[:, c, :], 128, c * 128, 'x%d' % c, 'hT%d' % c) for c in range(NCH)]
        if full:
            dma('sp', xs[:, :], xs_d, [], ['xs'], 'xl16')
            subt.append((xs[:, :], NS, 2048, 'xs', 'hT16'))
        ffn('f1', subt, n1_d, wg1_d, wu1_d, wd1_d)
        mixer_setup(not full)
        if full or not PHASE_A_256:
            mixer_loop(full)
        else:
            mixer_loop_A()
        if not full:
            for h in range(4):
                mm(psb[3][:, 300 + h:301 + h], selv[:, h, :], car[:, 0:1], True, True, ['selT', 'Bcar'], ['b3'])
            dve(lambda e: e.tensor_tensor(mu_p, mu_p, psb[3][:, 300:304], ALU.add), ['mu_p', 'b3'], ['mu_p'])
            dve(lambda e: e.tensor_tensor(car[:, 1:2], car[:, 1:2], car[:, 0:1], ALU.add), ['Gcar', 'Bcar'], ['Gcar'])
            dve(lambda e: e.memset(car[:, 0:1], 0.0), ['b3'], ['Bcar'])
            dma('sp', car[:, 3:4], bi_d.rearrange("o h -> h o"), [], ['car3'], 'sio')
            dve(lambda e: e.memset(car2[:, 0:1], -1.0), [], ['car4'])
        else:
            dma('sp', pk_d, kvout[:, 0:128], ['sg1'], [], 'pout')
            dma('sp', pv_d, kvout[:, 128:256], ['sg1'], [], 'pout')
            for h in range(4):
                tr(psb[1][:, h * 128:(h + 1) * 128], CTv[:, h, :], identf[:, :], ['CT', 'identf'], ['b1'])
            dve(lambda e: e.tensor_copy(hnum, psb[1][:, :].rearrange("p (h n) -> p h n", n=128)), ['b1'], ['hnum'])
            dma('sp', pC_d.rearrange("h v k -> v h k"), hnum, ['hnum'], [], 'pout')
            tr(psb[0][0:4, 0:128], nT, identf[:, :], ['nT', 'identf'], ['b0'])
            dve(lambda e: e.tensor_copy(gts[:, 0, :], psb[0][0:4, 0:128]), ['b0'], ['tmpA'])
            dma('sp', pn_d, gts[:, 0, :], ['tmpA'], [], 'pout')
            dve(lambda e: e.tensor_tensor(car2[:, 1:2], car[:, 1:2], car[:, 0:1], ALU.add), ['Gcar', 'Bcar'], ['mfin'])
            dma('sp', pm_d.rearrange("o h -> h o"), car2[:, 1:2], ['mfin'], [], 'pout')
            sample_mixer()
        mixer_done()
    ffn('f2', subt, n2_d, wg2_d, wu2_d, wd2_d)
    for c in range(NCH):
        dma('sp', ybv[:, c, :], x[:, c, :], ['x%d' % c], [], 'yout')
    dma('sp', ys_d, xs[:, :], ['xs'], [], 'yout')
    P.add('sp', lambda e: e.nop(), r=[], w=['x%d' % c for c in range(NCH)] + ['xs', 'sg1', 'hnum', 'tmpA', 'mfin', 'hn0', 'bigs', 'Cst0', 'Cst1', 'sigo'])

    P.emit(nc, es)
    print("sbuf bytes remaining", nc.sbuf_bytes_remaining, "ops", len(P.ops))
    return nc, es


def make_in_maps(inp):
    f = lambda a: np.ascontiguousarray(np.asarray(a, dtype=np.float32))
    xp = f(inp['x_prompt']); xsm = f(inp['x_sample'])[:, 0, :]
    cK = f(inp['cache_swa_k'])[0].reshape(128, 128, 128); cV = f(inp['cache_swa_v'])[0].reshape(128, 128, 128)
    sC = f(inp['state_mlstm_C'])[0]; sn = f(inp['state_mlstm_n'])[0].reshape(128, 512); sm = f(inp['state_mlstm_m'])[0]
    ident = np.eye(128, dtype=np.float32)
    s_ = np.arange(128)[:, None]; t_ = np.arange(128)[None, :]
    maskbig = np.where(s_ > t_, BIG, 0.0).astype(np.float32)
    swam = np.concatenate([np.where(t_ >= s_, 0.0, -BIG), np.where(t_ <= s_, 0.0, -BIG)], axis=1).astype(np.float32)
    swam_first = swam.copy(); swam_first[:, :128] = -BIG
    sel = np.zeros((4, 4, 128), np.float32)
    for h in range(4):
        sel[h, h, :] = 1.0
    sel = sel.reshape(4, 512)
    shared = dict(
        n1=f(inp['ffn1_norm']), wg1=f(inp['ffn1_w_gate'])[0], wu1=f(inp['ffn1_w_up'])[0], wd1=f(inp['ffn1_w_down'])[0],
        nm=f(inp['mix_norm']), win=f(inp['w_in'])[0], bi=f(inp['mlstm_b_i']), bf=f(inp['mlstm_b_f']),
        gon=f(inp['mlstm_out_norm']), gq=f(inp['swa_q_norm']), gk=f(inp['swa_k_norm']), snk=f(inp['swa_sinks']),
        wout=f(inp['w_out'])[0], n2=f(inp['ffn2_norm']), wg2=f(inp['ffn2_w_gate'])[0], wu2=f(inp['ffn2_w_up'])[0],
        wd2=f(inp['ffn2_w_down'])[0], ident=ident, maskbig=maskbig, swam=swam, sel=sel,
        esel=np.repeat(np.eye(16, dtype=np.float32), 8, axis=0))
    maps = []
    for core in range(8):
        b, g = core // 2, core % 2
        m = dict(shared)
        m['xa'] = np.ascontiguousarray(xp[b, 0:TOK]); m['xb'] = np.ascontiguousarray(xp[b, g * TOK:(g + 1) * TOK])
        sl = slice(core * NS, (core + 1) * NS)
        m['xs'] = np.ascontiguousarray(xsm[sl]); m['cK'] = np.ascontiguousarray(cK[sl]); m['cV'] = np.ascontiguousarray(cV[sl])
        m['sC0'] = np.ascontiguousarray(sC[sl]); m['sn0'] = np.ascontiguousarray(sn[sl]); m['sm0'] = np.ascontiguousarray(sm[sl])
        m['swam0'] = swam if g == 1 else swam_first
        am = np.zeros((128, 2), np.float32)
        am[:, 0] = 0.0 if g == 1 else -BIG
        am[:, 1] = 1.0 if g == 1 else 0.0
        m['amask'] = am
        maps.append(m)
    return maps


_NC_CACHE = {}


def kernel(**inputs):
    maps = make_in_maps(inputs)
    if 'nc' not in _NC_CACHE:
        _NC_CACHE['nc'] = build_nc()
    nc, es = _NC_CACHE['nc']
    res = run_bass_kernel_spmd(nc, maps, core_ids=list(range(8)))
    R = res.results
    yp = np.zeros((4, 4096, D), np.float32)
    for core in range(8):
        b, g = core // 2, core % 2
        yp[b, g * TOK:(g + 1) * TOK] = R[core]['yb']
    ysm = np.concatenate([R[c]['ys'] for c in range(8)], 0).reshape(128, 1, D)
    last = [R[2 * b + 1] for b in range(4)]
    pk = np.stack([r['pk'] for r in last]).reshape(1, 4, 128, 2, 64)
    pv = np.stack([r['pv'] for r in last]).reshape(1, 4, 128, 2, 64)
    pC = np.stack([r['pC'] for r in last]).reshape(1, 4, 4, 128, 128)
    pn = np.stack([r['pn'] for r in last]).reshape(1, 4, 4, 128)
    pm = np.stack([r['pm'] for r in last]).reshape(1, 4, 4)
    cat = lambda k: np.concatenate([R[c][k] for c in range(8)], 0)
    sk = cat('sk').reshape(1, 128, 128, 2, 64); sv = cat('sv').reshape(1, 128, 128, 2, 64)
    sC = cat('sC').reshape(1, 128, 4, 128, 128); sn = cat('sn').reshape(1, 128, 4, 128); sm = cat('sm').reshape(1, 128, 4)
    return tuple(np.ascontiguousarray(a, dtype=np.float32) for a in (yp, ysm, pk, pv, pC, pn, pm, sk, sv, sC, sn, sm))
```
